# Optimizing a Trainium2 kernel written in Bass

```python
import jax, jax.numpy as jnp
from jax import lax
import numpy as np

D_MODEL = 1024
BATCH = 8
SEQ = 2048
DEPTH = 1
DEC_BATCH = 128
DEC_SEQ = 8
PAST_LEN = 16384
PAGE_SIZE = 128

CONV_W = 4
LRU_WIDTH = D_MODEL
LRU_BLOCKS = 8
LRU_BLOCK = LRU_WIDTH // LRU_BLOCKS
LRU_C = 8.0
GDN_HEADS = 8
GDN_DK = 128
GDN_DV = 128
GDN_KEY_W = GDN_HEADS * GDN_DK
GDN_VAL_W = GDN_HEADS * GDN_DV
GDN_QKV_W = 2 * GDN_KEY_W + GDN_VAL_W
GDN_CHUNK = 64
NORM_EPS = 1e-6
IN_SPLITS = (LRU_WIDTH, LRU_WIDTH, GDN_QKV_W, GDN_VAL_W, GDN_HEADS, GDN_HEADS, D_MODEL, D_MODEL)
IN_WIDTH = sum(IN_SPLITS)

kernel_name = "hawk_gdn_parallel_gated_decoder_step"


def rms_norm(x, gain):
    x32 = x.astype(jnp.float32)
    y = x32 * lax.rsqrt(jnp.mean(x32 * x32, axis=-1, keepdims=True) + NORM_EPS)
    return (y * gain.astype(jnp.float32)).astype(x.dtype)


def l2_normalize(x):
    return x * lax.rsqrt(jnp.sum(x * x, axis=-1, keepdims=True) + NORM_EPS)


def causal_depthwise_conv(x, buf, w):
    T = x.shape[1]
    xp = jnp.concatenate([buf.astype(x.dtype), x], axis=1)
    y = sum(xp[:, i:i + T] * w[i].astype(x.dtype) for i in range(CONV_W))
    return y, xp[:, xp.shape[1] - (CONV_W - 1):]


def rg_lru(xc, h0, reset, wa, ba, wx, bx, a_logit):
    B, T, _ = xc.shape
    x32 = xc.astype(jnp.float32)
    xb = x32.reshape(B, T, LRU_BLOCKS, LRU_BLOCK)
    r = jax.nn.sigmoid(jnp.einsum('btgi,gij->btgj', xb, wa.astype(jnp.float32)).reshape(B, T, LRU_WIDTH) + ba.astype(jnp.float32))
    ig = jax.nn.sigmoid(jnp.einsum('btgi,gij->btgj', xb, wx.astype(jnp.float32)).reshape(B, T, LRU_WIDTH) + bx.astype(jnp.float32))
    log_a = LRU_C * r * jax.nn.log_sigmoid(a_logit.astype(jnp.float32))
    rs = reset[None, :, None]
    a = jnp.where(rs, 0.0, jnp.exp(log_a))
    mult = jnp.where(rs, 1.0, jnp.sqrt(-jnp.expm1(2.0 * log_a)))
    b = mult * ig * x32
    b = b.at[:, 0].add(a[:, 0] * h0)

    def combine(left, right):
        al, bl = left
        ar, br = right
        return al * ar, ar * bl + br

    _, h = lax.associative_scan(combine, (a, b), axis=1)
    return h, h[:, -1]


def gated_delta_rule_chunked(q, k, v, g, beta, S0):
    B, T, H, DK = q.shape
    DV = v.shape[-1]
    C = min(GDN_CHUNK, T)
    n = -(-T // C)
    pad = n * C - T
    if pad:
        def padf(a):
            return jnp.pad(a, [(0, 0), (0, pad)] + [(0, 0)] * (a.ndim - 2))
        q, k, v, g, beta = padf(q), padf(k), padf(v), padf(g), padf(beta)

    def chunks(a):
        a = a.reshape((B, n, C, H) + a.shape[3:])
        return jnp.moveaxis(a, (1, 3), (0, 2))

    qc, kc, vc, bc = chunks(q), chunks(k), chunks(v), chunks(beta)
    gc = jnp.cumsum(chunks(g), axis=-1)
    idx = jnp.arange(C)
    causal = idx[:, None] >= idx[None, :]
    strict = idx[:, None] > idx[None, :]
    diff = gc[..., :, None] - gc[..., None, :]
    decay = jnp.where(causal, jnp.exp(jnp.where(causal, diff, 0.0)), 0.0)
    kb = kc * bc[..., None]
    M = jnp.where(strict, jnp.einsum('nbhik,nbhjk->nbhij', kb, kc) * decay, 0.0)
    P = -M
    Tinv = jnp.broadcast_to(jnp.eye(C, dtype=M.dtype), M.shape)
    for _ in range((C - 1).bit_length()):
        Tinv = Tinv + jnp.einsum('nbhij,nbhjk->nbhik', Tinv, P)
        P = jnp.einsum('nbhij,nbhjk->nbhik', P, P)
    u = jnp.einsum('nbhij,nbhjv->nbhiv', Tinv, vc * bc[..., None])
    w = jnp.einsum('nbhij,nbhjk->nbhik', Tinv, kb * jnp.exp(gc)[..., None])
    a_qk = jnp.where(causal, jnp.einsum('nbhik,nbhjk->nbhij', qc, kc) * decay, 0.0)
    q_dec = qc * jnp.exp(gc)[..., None]
    g_last = gc[..., -1]
    k_dec = kc * jnp.exp(g_last[..., None] - gc)[..., None]

    def step(S, xs):
        q_i, k_i, u_i, w_i, a_i, gl_i = xs
        v_new = u_i - jnp.einsum('bhck,bhkv->bhcv', w_i, S)
        o_i = jnp.einsum('bhck,bhkv->bhcv', q_i, S) + jnp.einsum('bhij,bhjv->bhiv', a_i, v_new)
        S = S * jnp.exp(gl_i)[..., None, None] + jnp.einsum('bhck,bhcv->bhkv', k_i, v_new)
        return S, o_i

    S_final, o = lax.scan(step, S0, (q_dec, k_dec, u, w, a_qk, g_last))
    o = jnp.moveaxis(o, (0, 2), (1, 3)).reshape(B, n * C, H, DV)[:, :T]
    return o, S_final


def hybrid_layer(x, lru_conv_buf, lru_h0, gdn_conv_buf, gdn_S0, start_pos,
                 norm_pre, norm_post, w_in, lru_conv_w, lru_conv_b, lru_wa, lru_ba, lru_wx, lru_bx,
                 lru_a_logit, gdn_conv_w, gdn_A_log, gdn_dt_bias, gdn_norm_w, w_br_lru, w_br_gdn, w_out):
    B, T, _ = x.shape
    u = rms_norm(x, norm_pre)
    z = jnp.einsum('btd,de->bte', u, w_in)
    offsets = np.cumsum(IN_SPLITS)[:-1].tolist()
    lru_x, lru_gate, gdn_qkv, gdn_gate, gdn_b, gdn_a, m_lru, m_gdn = jnp.split(z, offsets, axis=-1)

    lru_xc, lru_conv_new = causal_depthwise_conv(lru_x, lru_conv_buf, lru_conv_w)
    lru_xc = lru_xc + lru_conv_b.astype(x.dtype)
    reset = (jnp.arange(T) + start_pos) == 0
    h, lru_h_new = rg_lru(lru_xc, lru_h0.astype(jnp.float32), reset, lru_wa, lru_ba, lru_wx, lru_bx, lru_a_logit)
    lru_out = h.astype(x.dtype) * jax.nn.silu(lru_gate)

    qkv, gdn_conv_new = causal_depthwise_conv(gdn_qkv, gdn_conv_buf, gdn_conv_w)
    qkv = jax.nn.silu(qkv).astype(jnp.float32)
    q, k, v = jnp.split(qkv, [GDN_KEY_W, 2 * GDN_KEY_W], axis=-1)
    q = l2_normalize(q.reshape(B, T, GDN_HEADS, GDN_DK)) * (GDN_DK ** -0.5)
    k = l2_normalize(k.reshape(B, T, GDN_HEADS, GDN_DK))
    v = v.reshape(B, T, GDN_HEADS, GDN_DV)
    beta = jax.nn.sigmoid(gdn_b.astype(jnp.float32))
    g = -jnp.exp(gdn_A_log.astype(jnp.float32)) * jax.nn.softplus(gdn_a.astype(jnp.float32) + gdn_dt_bias.astype(jnp.float32))
    o, S_new = gated_delta_rule_chunked(q, k, v, g, beta, gdn_S0.astype(jnp.float32))
    o = o * lax.rsqrt(jnp.mean(o * o, axis=-1, keepdims=True) + NORM_EPS) * gdn_norm_w.astype(jnp.float32)
    o = o * jax.nn.silu(gdn_gate.astype(jnp.float32).reshape(B, T, GDN_HEADS, GDN_DV))
    gdn_out = o.reshape(B, T, GDN_VAL_W).astype(x.dtype)

    p_lru = jnp.einsum('bte,ed->btd', lru_out, w_br_lru)
    p_gdn = jnp.einsum('bte,ed->btd', gdn_out, w_br_gdn)
    merged = jax.nn.sigmoid(m_lru) * p_lru + jax.nn.sigmoid(m_gdn) * p_gdn
    y = jnp.einsum('btd,de->bte', merged, w_out)
    x_new = x + rms_norm(y, norm_post)
    return x_new, lru_conv_new, lru_h_new, gdn_conv_new, S_new


def setup_inputs(seed: int = 0) -> dict:
    key = jax.random.key(seed)
    ks = jax.random.split(key, 24)
    f32 = jnp.float32
    nrm = lambda k, shape, s: jax.random.normal(k, shape, f32) * s
    a_u = jax.random.uniform(ks[14], (DEPTH, LRU_WIDTH), f32, 0.9, 0.999)
    dt = jnp.exp(jax.random.uniform(ks[17], (DEPTH, GDN_HEADS), f32, np.log(0.001), np.log(0.1)))
    return {
        "x_prompt": nrm(ks[0], (BATCH, SEQ, D_MODEL), 1.0),
        "x_sample": nrm(ks[1], (DEC_BATCH, DEC_SEQ, D_MODEL), 1.0),
        "state_lru_conv": nrm(ks[2], (DEPTH, DEC_BATCH, CONV_W - 1, LRU_WIDTH), 1.0),
        "state_lru_h": nrm(ks[3], (DEPTH, DEC_BATCH, LRU_WIDTH), 0.5),
        "state_gdn_conv": nrm(ks[4], (DEPTH, DEC_BATCH, CONV_W - 1, GDN_QKV_W), 1.0),
        "state_gdn_S": nrm(ks[5], (DEPTH, DEC_BATCH, GDN_HEADS, GDN_DK, GDN_DV), 0.1),
        "norm_pre": 1.0 + nrm(ks[6], (DEPTH, D_MODEL), 0.05),
        "norm_post": 1.0 + nrm(ks[7], (DEPTH, D_MODEL), 0.05),
        "w_in": nrm(ks[8], (DEPTH, D_MODEL, IN_WIDTH), D_MODEL ** -0.5),
        "lru_conv_w": nrm(ks[9], (DEPTH, CONV_W, LRU_WIDTH), CONV_W ** -0.5),
        "lru_conv_b": nrm(ks[10], (DEPTH, LRU_WIDTH), 0.02),
        "lru_wa": nrm(ks[11], (DEPTH, LRU_BLOCKS, LRU_BLOCK, LRU_BLOCK), LRU_BLOCK ** -0.5),
        "lru_ba": nrm(ks[12], (DEPTH, LRU_WIDTH), 0.02),
        "lru_wx": nrm(ks[13], (DEPTH, LRU_BLOCKS, LRU_BLOCK, LRU_BLOCK), LRU_BLOCK ** -0.5),
        "lru_bx": nrm(ks[15], (DEPTH, LRU_WIDTH), 0.02),
        "lru_a_logit": jnp.log(a_u) - jnp.log1p(-a_u),
        "gdn_conv_w": nrm(ks[16], (DEPTH, CONV_W, GDN_QKV_W), CONV_W ** -0.5),
        "gdn_A_log": jnp.log(jax.random.uniform(ks[18], (DEPTH, GDN_HEADS), f32, 1.0, 16.0)),
        "gdn_dt_bias": dt + jnp.log(-jnp.expm1(-dt)),
        "gdn_norm_w": 1.0 + nrm(ks[19], (DEPTH, GDN_DV), 0.05),
        "w_br_lru": nrm(ks[20], (DEPTH, LRU_WIDTH, D_MODEL), LRU_WIDTH ** -0.5),
        "w_br_gdn": nrm(ks[21], (DEPTH, GDN_VAL_W, D_MODEL), GDN_VAL_W ** -0.5),
        "w_out": nrm(ks[22], (DEPTH, D_MODEL, D_MODEL), D_MODEL ** -0.5),
    }


def reference(x_prompt, x_sample, state_lru_conv, state_lru_h, state_gdn_conv, state_gdn_S,
              norm_pre, norm_post, w_in, lru_conv_w, lru_conv_b, lru_wa, lru_ba, lru_wx, lru_bx,
              lru_a_logit, gdn_conv_w, gdn_A_log, gdn_dt_bias, gdn_norm_w, w_br_lru, w_br_gdn, w_out):
    yp, ys = x_prompt, x_sample
    B = x_prompt.shape[0]
    p_lc, p_lh, p_gc, p_gs = [], [], [], []
    s_lc, s_lh, s_gc, s_gs = [], [], [], []
    for l in range(DEPTH):
        w_l = (norm_pre[l], norm_post[l], w_in[l], lru_conv_w[l], lru_conv_b[l], lru_wa[l], lru_ba[l],
               lru_wx[l], lru_bx[l], lru_a_logit[l], gdn_conv_w[l], gdn_A_log[l], gdn_dt_bias[l],
               gdn_norm_w[l], w_br_lru[l], w_br_gdn[l], w_out[l])
        yp, lc, lh, gc, gs = hybrid_layer(
            yp,
            jnp.zeros((B, CONV_W - 1, LRU_WIDTH), x_prompt.dtype),
            jnp.zeros((B, LRU_WIDTH), jnp.float32),
            jnp.zeros((B, CONV_W - 1, GDN_QKV_W), x_prompt.dtype),
            jnp.zeros((B, GDN_HEADS, GDN_DK, GDN_DV), jnp.float32),
            0, *w_l)
        p_lc.append(lc.astype(state_lru_conv.dtype))
        p_lh.append(lh.astype(state_lru_h.dtype))
        p_gc.append(gc.astype(state_gdn_conv.dtype))
        p_gs.append(gs.astype(state_gdn_S.dtype))
        ys, lc, lh, gc, gs = hybrid_layer(
            ys, state_lru_conv[l], state_lru_h[l], state_gdn_conv[l], state_gdn_S[l], PAST_LEN, *w_l)
        s_lc.append(lc.astype(state_lru_conv.dtype))
        s_lh.append(lh.astype(state_lru_h.dtype))
        s_gc.append(gc.astype(state_gdn_conv.dtype))
        s_gs.append(gs.astype(state_gdn_S.dtype))
    return (yp, ys,
            jnp.stack(p_lc), jnp.stack(p_lh), jnp.stack(p_gc), jnp.stack(p_gs),
            jnp.stack(s_lc), jnp.stack(s_lh), jnp.stack(s_gc), jnp.stack(s_gs))
```

```python
import math
import numpy as np
import concourse.bass as bass
import concourse.mybir as mybir
from concourse.bass_utils import run_bass_kernel_spmd
from contextlib import ExitStack

F32 = mybir.dt.float32
BF16 = mybir.dt.bfloat16
AF = mybir.ActivationFunctionType
ALU = mybir.AluOpType

NCORES = 8
D = 1024
KC = 8
NT = 17
TT = NT * 128
EPS = 1e-6
NV = 169
V_NPRE, V_LCW, V_LCB, V_LBA, V_LBX, V_LAL, V_GCW, V_GNW = 0, 8, 40, 48, 56, 64, 72, 168


class Buf:
    __slots__ = ("name", "w", "r", "dsem", "dcnt", "excl", "lastdma")

    def __init__(self, name):
        self.name = name
        self.w = None
        self.r = []
        self.dsem = None
        self.dcnt = 0
        self.excl = False
        self.lastdma = None


class Op:
    __slots__ = ("id", "eng", "fn", "deps", "signal", "dur", "tab", "dma", "lat", "epoch", "idx", "unit", "tag")


class Sched:
    ENG = ("pe", "act", "dve", "pool", "sp")
    BLEV = True
    BUCKET = 0.1

    def __init__(self, nc, stack):
        self.nc = nc
        self.stack = stack
        self.sem = {e: stack.enter_context(nc.semaphore("c_" + e)) for e in ("pe", "act", "dve", "pool")}
        self.allops = []
        self.epoch = 0
        self.final = {}
        self.dsems = []
        self.nb = 0

    def buf(self, name="b"):
        self.nb += 1
        return Buf("%s_%d" % (name, self.nb))

    def _record(self, eng, fn, reads, writes, signal, dur, tab, dma=None, lat=0.0):
        ex = [b for b in reads if b.excl]
        if ex:
            reads = [b for b in reads if not b.excl]
            writes = list(writes) + ex
        o = Op()
        o.id = len(self.allops)
        o.eng, o.fn, o.signal, o.dur, o.tab, o.dma, o.lat, o.epoch = eng, fn, signal, dur, tab, dma, lat, self.epoch
        o.tag = getattr(self, "tag", "")
        deps = set()
        for b in reads:
            if b.w is not None:
                deps.add(b.w)
        for b in writes:
            if b.w is not None:
                deps.add(b.w)
            deps.update(b.r)
        o.deps = deps
        for b in reads:
            b.r.append(o.id)
        for b in writes:
            b.w = o.id
            b.r = []
        self.allops.append(o)
        return o

    def op(self, eng, fn, reads=(), writes=(), signal=True, dur=0.3, tab=None):
        self._record(eng, fn, reads, writes, signal, dur, tab)

    def dsem_of(self, buf):
        if buf.dsem is None:
            buf.dsem = self.stack.enter_context(self.nc.semaphore("d_" + buf.name))
            self.dsems.append(buf)
        return buf.dsem

    def dma(self, out, in_, reads=(), writes=(), sembuf=None, eng="sp", is_output=False, nbytes=1 << 20, **kw):
        if sembuf is None:
            sembuf = writes[0] if writes else reads[0]
        self.dsem_of(sembuf)
        o = self._record(eng, lambda e, o=out, i=in_, k=kw: e.dma_start(out=o, in_=i, **k), reads, writes, True,
                         0.15 if eng != "pool" else 1.0, None, dma=(sembuf, is_output), lat=2.0 + nbytes / 2.0e5)
        if sembuf.lastdma is not None:
            o.deps.add(sembuf.lastdma)
        sembuf.lastdma = o.id

    def barrier(self):
        self.epoch += 1

    def _schedule_epoch(self, ops, order, t0):
        units = []
        cur = None
        for o in ops:
            if o.eng == "pe":
                if cur is None:
                    cur = [o]
                else:
                    cur.append(o)
                if o.signal:
                    units.append(cur)
                    cur = None
            else:
                units.append([o])
        assert cur is None, "PE op run without a final signalled op"
        uid = {}
        for ui, u in enumerate(units):
            for o in u:
                uid[o.id] = ui
        nu = len(units)
        first = ops[0].id if ops else 0
        preds = [set() for _ in range(nu)]
        succs = [[] for _ in range(nu)]
        for ui, u in enumerate(units):
            for o in u:
                for d in o.deps:
                    if d >= first and uid[d] != ui:
                        preds[ui].add(uid[d])
        for ui in range(nu):
            for p in preds[ui]:
                succs[p].append(ui)
        indeg = [len(p) for p in preds]
        ready_t = [t0] * nu
        udur = [sum(o.dur for o in u) for u in units]
        ueng = [u[0].eng for u in units]
        blev = [0.0] * nu
        for ui in range(nu - 1, -1, -1):
            m = 0.0
            for sidx in succs[ui]:
                if blev[sidx] > m:
                    m = blev[sidx]
            blev[ui] = udur[ui] + units[ui][-1].lat + 0.3 + m
        rdy = {e: [] for e in self.ENG}
        for ui in range(nu):
            if indeg[ui] == 0:
                rdy[ueng[ui]].append(ui)
        free = {e: t0 for e in self.ENG}
        curtab = {"act": None}
        done = 0
        tend = t0
        while done < nu:
            best = None
            for e in self.ENG:
                lst = rdy[e]
                if not lst:
                    continue
                f = free[e]
                cand = None
                for ui in lst:
                    st = max(f, ready_t[ui])
                    if e == "act":
                        tb = units[ui][0].tab
                        if tb is not None and curtab["act"] is not None and tb != curtab["act"]:
                            st += 1.3
                    key = (int(st / self.BUCKET), -blev[ui], ui) if self.BLEV else (st, ui)
                    if cand is None or key < cand[0]:
                        cand = (key, ui, st)
                if best is None or cand[0] < best[0]:
                    best = (cand[0], cand[1], cand[2], e)
            _, ui, st, e = best
            rdy[e].remove(ui)
            if e == "act" and units[ui][0].tab is not None:
                curtab["act"] = units[ui][0].tab
            fin = st + udur[ui]
            free[e] = fin
            lat = units[ui][-1].lat
            tend = max(tend, fin + lat)
            order[e].extend(units[ui])
            done += 1
            for sidx in succs[ui]:
                extra = 0.12 if ueng[sidx] == e else 0.3
                ready_t[sidx] = max(ready_t[sidx], fin + lat + extra)
                indeg[sidx] -= 1
                if indeg[sidx] == 0:
                    rdy[ueng[sidx]].append(sidx)
        return tend

    def emit(self):
        ops = self.allops
        order = {e: [] for e in self.ENG}
        bounds = []
        t = 0.0
        i = 0
        n = len(ops)
        while i < n:
            j = i
            while j < n and ops[j].epoch == ops[i].epoch:
                j += 1
            t = self._schedule_epoch(ops[i:j], order, t)
            bounds.append({e: len(order[e]) for e in self.ENG})
            i = j
        self.est_us = t
        cnt = {e: 0 for e in self.ENG}
        dmaval = {}
        for e in self.ENG:
            lst = order[e]
            pending = []
            for o in lst:
                if o.dma is not None:
                    sb_, is_out = o.dma
                    sb_.dcnt += 16
                    o.idx = (("d", sb_.name), sb_.dsem, sb_.dcnt)
                    if is_out:
                        self.final[sb_.name] = (sb_.dsem, sb_.dcnt)
                elif o.signal:
                    cnt[e] += 1
                    o.idx = (e, self.sem[e], cnt[e])
                    for p in pending:
                        p.idx = o.idx
                    pending = []
                else:
                    pending.append(o)
            assert not pending
        prog = {e: [] for e in self.ENG}
        waited = {e: {} for e in self.ENG}
        pos = {e: 0 for e in self.ENG}
        for bi, bd in enumerate(bounds):
            for e in self.ENG:
                for o in order[e][pos[e]:bd[e]]:
                    waits = []
                    for d in sorted(o.deps):
                        key, sem, val = ops[d].idx
                        if key == e and e == "pe":
                            continue
                        if waited[e].get(key, 0) >= val:
                            continue
                        waited[e][key] = val
                        waits.append((sem, val))
                    inc = None
                    if o.dma is not None:
                        inc = (o.idx[1], 16)
                    elif o.signal:
                        inc = (self.sem[e], 1)
                    prog[e].append((waits, o.fn, inc))
                pos[e] = bd[e]
            if bi < len(bounds) - 1:
                for e in self.ENG:
                    waits = []
                    for f in ("pe", "act", "dve", "pool"):
                        c = 0
                        for o in order[f][:bd[f]]:
                            if o.dma is None and o.signal:
                                c = max(c, o.idx[2])
                        if f != e and c > 0 and waited[e].get(f, 0) < c:
                            waited[e][f] = c
                            waits.append((self.sem[f], c))
                    for q in ("sp", "pool"):
                        last = {}
                        for o in order[q][:bd[q]]:
                            if o.dma is not None:
                                last[o.idx[0]] = o.idx
                        for key, (k_, sem, val) in last.items():
                            if waited[e].get(key, 0) < val:
                                waited[e][key] = val
                                waits.append((sem, val))
                    if waits:
                        prog[e].append((waits, None, None))
        fw = list(self.final.values())
        if fw:
            prog["sp"].append((fw, None, None))
        self.prog = prog
        with self.nc.Block() as block:
            def mk(name):
                lst = prog[name]

                def body(eng):
                    for waits, fn, inc in lst:
                        for s_, v in waits:
                            eng.wait_ge(s_, v)
                        if fn is not None:
                            ins = fn(eng)
                            if inc is not None:
                                ins.then_inc(*inc)
                return body
            block.tensor(mk("pe"))
            block.scalar(mk("act"))
            block.vector(mk("dve"))
            block.gpsimd(mk("pool"))
            block.sync(mk("sp"))


class PV:
    def __init__(self, t, off):
        self.t, self.off = t, off

    def __getitem__(self, key):
        if isinstance(key, slice):
            return self.t[:, self.off:self.off + 512]
        p, c = key
        return self.t[p, c.start + self.off:c.stop + self.off]


def _fsz(ap):
    n = 1
    for d in ap.shape[1:]:
        n *= int(d)
    return n


class K:
    STOP = 99
    NORM_ENG = "dve"
    S16_DVE = True
    LSLOTS = 4
    DBL256 = False
    PSR = [(0, 1, 2), (4, 5, 6), (3, 7)]
    OT_ALIAS = True
    PSPLIT = True
    GOFF = 0
    SKIPG = False
    GLIM = (8, 5, 9)

    def __init__(self):
        self.nc = bass.Bass("TRN2", target_bir_lowering=False)
        self.st = ExitStack()

    def sb(self, name, shape, dt=F32):
        self.nsb = getattr(self, "nsb", 0) + 1
        return self.cur.enter_context(self.nc.sbuf_tensor("s%d_%s" % (self.nsb, name), shape, dt))

    def phase(self, fn):
        with ExitStack() as ph:
            old, self.cur = self.cur, ph
            fn()
            self.S.barrier()
            self.cur = old

    def din(self, name, shape):
        return self.nc.dram_tensor(name, shape, F32, kind="ExternalInput").ap()

    def dout(self, name, shape):
        return self.nc.dram_tensor(name, shape, F32, kind="ExternalOutput").ap()

    def _banks(self):
        r = self.psrange
        if isinstance(r, dict):
            return r.get(self.S.tag, r["*"])
        return r

    def ps(self):
        banks = self._banks()
        k = self.psc.get(banks, 0)
        i = banks[k % len(banks)]
        self.psc[banks] = k + 1
        return PV(self.pst[i // 2], (i % 2) * 512), self.psb[i]

    def ps2(self):
        banks = self._banks()
        if len(banks) == 8:
            k = self.psc.get("pair", 0)
            self.psc["pair"] = k + 1
            i = 2 * (k % 4)
        else:
            i = banks[0]
            assert i % 2 == 0 and banks[1] == i + 1
        return self.pst[i // 2], [self.psb[i], self.psb[i + 1]]

    def mm(self, out, lhsT, rhs, start, stop, reads, writes, signal=None, sgc=False):
        if signal is None:
            signal = stop
        d = 0.064 + _fsz(rhs) / 2400.0
        if lhsT.dtype == F32:
            d *= 4
        if sgc:
            fn = lambda e, o=out, l=lhsT, r=rhs, a=start, b=stop: e.matmul(o, lhsT=l, rhs=r, start=a, stop=b, skip_group_check=True)
        else:
            fn = lambda e, o=out, l=lhsT, r=rhs, a=start, b=stop: e.matmul(o, lhsT=l, rhs=r, start=a, stop=b)
        self.S.op("pe", fn, reads=reads, writes=writes, signal=signal, dur=d)

    def tr(self, out, in_, ident, reads, writes, signal=True):
        self.S.op("pe", lambda e, o=out, i=in_, d=ident: e.transpose(out=o, in_=i, identity=d),
                  reads=reads, writes=writes, signal=signal, dur=0.12)

    def act(self, out, in_, func, reads, writes, scale=1.0, bias=0.0, accum_out=None):
        def fn(e, o=out, i=in_, f=func, s=scale, b=bias, a=accum_out):
            kw = {}
            if a is not None:
                kw["accum_out"] = a
            return e.activation(out=o, in_=i, func=f, bias=b, scale=s, **kw)
        tab = {AF.Tanh: 0, AF.Ln: 6, AF.Sigmoid: 2, AF.Sqrt: 3}.get(func)
        self.S.op("act", fn, reads=reads, writes=writes, dur=0.25 + _fsz(out) / 1200.0, tab=tab)

    def tt(self, eng, out, in0, in1, op, reads, writes):
        self.S.op(eng, lambda e, o=out, a=in0, b=in1, p=op: e.tensor_tensor(out=o, in0=a, in1=b, op=p),
                  reads=reads, writes=writes, dur=self.edur(eng, out))

    def ts(self, eng, out, in0, s1, s2, op0, op1, reads, writes):
        if s2 is None:
            self.S.op(eng, lambda e, o=out, a=in0, x=s1, p=op0: e.tensor_scalar(out=o, in0=a, scalar1=x, scalar2=None, op0=p),
                      reads=reads, writes=writes, dur=self.edur(eng, out))
        else:
            self.S.op(eng, lambda e, o=out, a=in0, x=s1, y=s2, p=op0, q=op1:
                      e.tensor_scalar(out=o, in0=a, scalar1=x, scalar2=y, op0=p, op1=q), reads=reads, writes=writes,
                      dur=self.edur(eng, out))

    def stt(self, out, in0, scalar, in1, op0, op1, reads, writes):
        self.S.op("dve", lambda e, o=out, a=in0, s=scalar, b=in1, p=op0, q=op1:
                  e.scalar_tensor_tensor(out=o, in0=a, scalar=s, in1=b, op0=p, op1=q), reads=reads, writes=writes,
                  dur=self.edur("dve", out))

    def cp(self, eng, out, in_, reads, writes):
        if eng == "act":
            self.S.op("act", lambda e, o=out, i=in_: e.copy(out=o, in_=i), reads=reads, writes=writes,
                      dur=0.25 + _fsz(out) / 1200.0)
        else:
            self.S.op(eng, lambda e, o=out, i=in_: e.tensor_copy(out=o, in_=i), reads=reads, writes=writes,
                      dur=self.edur(eng, out))

    def edur(self, eng, out):
        n = _fsz(out)
        return (0.12 + n / 960.0) if eng == "dve" else (0.2 + n / 500.0)

    def memset(self, eng, ap, val, writes):
        self.S.op(eng, lambda e, a=ap, v=val: e.memset(a, v), writes=writes, dur=self.edur(eng, ap))

    def asel(self, out, in_, pattern, op, base, cm, reads, writes):
        self.S.op("pool", lambda e, o=out, i=in_, p=pattern, c=op, b=base, m=cm:
                  e.affine_select(out=o, in_=i, pattern=p, compare_op=c, fill=0.0, base=b, channel_multiplier=m),
                  reads=reads, writes=writes)

    def rstd(self, col, scale, reads_writes, eps=EPS):
        self.act(col, col, AF.Ln, reads_writes, reads_writes, scale=scale, bias=eps)
        self.act(col, col, AF.Exp, reads_writes, reads_writes, scale=-0.5)

    def build(self):
        nc, st = self.nc, self.st
        with st:
            self.S = S = Sched(nc, st)
            self._io()
            self.pst = [st.enter_context(nc.psum_tensor("ps%d" % i, [128, 1024], F32)) for i in range(4)]
            self.psb = [S.buf("ps") for _ in range(8)]
            for b in self.psb:
                b.excl = True
            self.psrange = tuple(range(8))
            self.psc = {}
            self.cur = st
            stop = self.STOP
            self._consts()
            if stop >= 1:
                self.phase(lambda: (self._phase_a(), self._gdn_scalars()))
            if stop >= 3 and not self.SKIPG:
                self.phase(self._phase_gdn)
            self.lru_out = self.sb("lru_out", [128, KC, TT], BF16)
            if stop >= 4:
                self.phase(self._phase_lru)
            self.merged = self.sb("merged", [128, KC, TT], BF16)
            self.mgb = S.buf("merged")
            if stop >= 6:
                self.phase(self._phase_merge)
            if stop >= 7:
                self.phase(self._phase_out)
            S.emit()
        return nc

    def _io(self):
        i, o = self.din, self.dout
        self.xp, self.xs = i("xp", [2048, D]), i("xs", [128, D])
        self.slc, self.slh = i("slc", [48, D]), i("slh", [16, D])
        self.sgc, self.sgS = i("sgc", [48, 3072]), i("sgS", [16, 8, 128, 128])
        self.vecs_d, self.npost_d = i("vecs", [128, NV]), i("npost", [1, D])
        self.alog_d, self.dtb_d = i("alog", [1, 8]), i("dtb", [1, 8])
        self.winr, self.wba_d = i("winr", [64, 128, KC, 128]), i("wba", [128, KC, 16])
        self.wa_d, self.wx_d = i("wa", [128, 8, 128]), i("wx", [128, 8, 128])
        self.wbl_d, self.wbg_d = i("wbl", [8, 128, KC, 128]), i("wbg", [8, 128, KC, 128])
        self.wout_d = i("wout", [128, KC, D])
        self.yp, self.ys = o("yp", [2048, D]), o("ys", [128, D])
        self.o_plc, self.o_plh = o("o_plc", [3, D]), o("o_plh", [1, D])
        self.o_pgc, self.o_pgs = o("o_pgc", [3, 3072]), o("o_pgs", [8, 128, 128])
        self.o_slc, self.o_slh = o("o_slc", [48, D]), o("o_slh", [16, D])
        self.o_sgc, self.o_sgs = o("o_sgc", [48, 3072]), o("o_sgs", [16, 8, 128, 128])

    def xsrc(self, i):
        return self.xp[i * 128:(i + 1) * 128, :] if i < 16 else self.xs

    def ydst(self, i):
        return self.yp[i * 128:(i + 1) * 128, :] if i < 16 else self.ys

    def _consts(self):
        S, sb = self.S, self.sb
        cb = self.cb = S.buf("const")
        self.vecs = sb("vecs", [128, NV])
        S.dma(self.vecs[:], self.vecs_d, writes=[cb])
        self.alog = sb("alog", [128, 8])
        self.dtb = sb("dtbs", [128, 8])
        S.dma(self.alog[:], self.alog_d.broadcast_to([128, 8]), writes=[cb])
        S.dma(self.dtb[:], self.dtb_d.broadcast_to([128, 8]), writes=[cb])
        self.idf = sb("idf", [128, 128])
        self.idb = sb("idb", [128, 128], BF16)
        self.memset("pool", self.idf[:], 1.0, [cb])
        self.asel(self.idf[:], self.idf[:], [[-1, 128]], ALU.is_equal, 0, 1, [cb], [cb])
        self.cp("pool", self.idb[:], self.idf[:], [cb], [cb])
        self.onesb = sb("onesb", [128, 128], BF16)
        self.memset("pool", self.onesb[:], 1.0, [cb])
        self.onesf = sb("onesf", [128, 128])
        self.memset("pool", self.onesf[:], 1.0, [cb])
        self.uincl = sb("uincl", [128, 128])
        self.ustr = sb("ustr", [128, 128])
        self.lstr = sb("lstr", [128, 128])
        for t, pat, op, cm in ((self.uincl, [[1, 128]], ALU.is_ge, -1), (self.ustr, [[1, 128]], ALU.is_gt, -1),
                               (self.lstr, [[-1, 128]], ALU.is_gt, 1)):
            self.memset("pool", t[:], 1.0, [cb])
            self.asel(t[:], t[:], pat, op, 0, cm, [cb], [cb])
        self.bd64 = sb("bd64", [128, 128], BF16)
        self.offur = sb("offur", [128, 128], BF16)
        self.memset("pool", self.bd64[:], 0.0, [cb])
        self.memset("pool", self.bd64[0:64, 0:64], 1.0, [cb])
        self.memset("pool", self.bd64[64:128, 64:128], 1.0, [cb])
        self.memset("pool", self.offur[:], 0.0, [cb])
        self.memset("pool", self.offur[0:64, 64:128], 1.0, [cb])
        self.usbd = sb("usbd", [128, 128])
        self.offf = sb("offf", [128, 128])
        self.cp("pool", self.offf[:], self.offur[:], [cb], [cb])
        self.cp("pool", self.usbd[:], self.bd64[:], [cb], [cb])
        self.tt("pool", self.usbd[:], self.usbd[:], self.ustr[:], ALU.mult, [cb], [cb])
        self.selT = sb("selT", [16, 128])
        self.memset("pool", self.selT[:], 1.0, [cb])
        self.asel(self.selT[:], self.selT[:], [[1, 128]], ALU.is_ge, 0, -8, [cb], [cb])
        self.asel(self.selT[:], self.selT[:], [[-1, 128]], ALU.is_ge, 7, 8, [cb], [cb])
        self.selc = sb("selc", [128, 16])
        self.memset("pool", self.selc[:], 1.0, [cb])
        self.asel(self.selc[:], self.selc[:], [[-8, 16]], ALU.is_ge, 0, 1, [cb], [cb])
        self.asel(self.selc[:], self.selc[:], [[8, 16]], ALU.is_ge, 7, -1, [cb], [cb])
        pt, pb = self.ps()
        self.mm(pt[:, 0:128], self.selT[:], self.selT[:], True, True, [cb], [pb])
        self.blk = sb("blk", [128, 128])
        self.cp("dve", self.blk[:], pt[:, 0:128], [pb], [cb])
        self.uincl_b, self.ustr_b, self.lstr_b = sb("uincl_b", [128, 128]), sb("ustr_b", [128, 128]), sb("lstr_b", [128, 128])
        for a, b in ((self.uincl_b, self.uincl), (self.ustr_b, self.ustr), (self.lstr_b, self.lstr)):
            self.tt("pool", a[:], b[:], self.blk[:], ALU.mult, [cb], [cb])
        self.clru = sb("clru", [128, 8])
        self.act(self.clru[:], self.vecs[:, V_LAL:V_LAL + 8], AF.Exp, [cb], [cb], scale=-1.0)
        self.act(self.clru[:], self.clru[:], AF.Ln, [cb], [cb], bias=1.0)
        self.ts("dve", self.clru[:], self.clru[:], -8.0, None, ALU.mult, None, [cb], [cb])
        self.gnwh = sb("gnwh", [128, 1])
        self.ts("dve", self.gnwh[:], self.vecs[:, V_GNW:V_GNW + 1], 0.5, None, ALU.mult, None, [cb], [cb])
        self.hb = sb("hb", [128, 16])
        self.ts("dve", self.hb[:], self.vecs[:, V_LBA:V_LBA + 16], 0.5, None, ALU.mult, None, [cb], [cb])
        self.hc = sb("hc", [128, 8])
        self.ts("dve", self.hc[:], self.clru[:], 0.5, None, ALU.mult, None, [cb], [cb])
        self.negA = sb("negA", [128, 8])
        self.act(self.negA[:], self.alog[:], AF.Exp, [cb], [cb])
        self.ts("dve", self.negA[:], self.negA[:], -1.0, None, ALU.mult, None, [cb], [cb])
        self.lcT = sb("lcT", [128, 8, 48])
        self.h0T = sb("h0T", [128, 8, 16])
        self.gcT = sb("gcT", [128, 24, 48])
        self.xn = sb("xn", [128, KC, TT], BF16)
        self.xnb = [S.buf("xn") for _ in range(NT)]
        self.gdn_out = sb("gdn_out", [128, KC, TT], BF16)
        for nm_, shp in (("betah", [128, NT, 8]), ("beta", [128, NT, 8]), ("nbeta", [128, NT, 8]), ("gtok", [128, NT, 8]), ("egc", [128, NT, 8]),
                         ("kdc", [128, NT, 8]), ("eglb", [128, 16, 8]), ("eglS", [128, 16, 8])):
            setattr(self, nm_, sb(nm_, shp))
        self.stg_lc = sb("stg_lc", [128, 8, 51])
        self.stg_lh = sb("stg_lh", [128, 8, 17])
        self.stg_gc = sb("stg_gc", [128, 24, 51])

    def _load_states(self):
        S, sb = self.S, self.sb
        cb = self.cb
        tmp = sb("st_tmp", [48, 3072 + D])
        tmp2 = sb("st_tmp2", [16, D])
        tb = S.buf("sttmp")
        S.dma(tmp[:, 0:D], self.slc, writes=[tb])
        S.dma(tmp[:, D:D + 3072], self.sgc, writes=[tb])
        S.dma(tmp2[:], self.slh, writes=[tb])
        for g in range(32):
            pt, pb = self.ps()
            self.tr(pt[:, 0:48], tmp[:, g * 128:(g + 1) * 128], self.idf[0:48, 0:48], [tb, cb], [pb])
            dst = self.lcT[:, g, :] if g < 8 else self.gcT[:, g - 8, :]
            self.cp("dve", dst, pt[:, 0:48], [pb], [cb])
        for g in range(8):
            pt, pb = self.ps()
            self.tr(pt[:, 0:16], tmp2[:, g * 128:(g + 1) * 128], self.idf[0:16, 0:16], [tb, cb], [pb])
            self.cp("dve", self.h0T[:, g, :], pt[:, 0:16], [pb], [cb])

    def _phase_a(self):
        S, sb = self.S, self.sb
        cb = self.cb
        self._load_states()
        NB = 4
        xt = [sb("xt%d" % i, [128, D]) for i in range(NB)]
        xtb = [S.buf("xt") for _ in range(NB)]
        x16 = [sb("x16_%d" % i, [128, D], BF16) for i in range(NB)]
        x16b = [S.buf("x16") for _ in range(NB)]
        junk = sb("junk", [128, D], BF16)
        jb = S.buf("junk")
        ss = sb("ssA", [128, NT])
        ssb = [S.buf("ss") for _ in range(NT)]
        for i in range(NT):
            s = i % NB
            S.dma(xt[s][:], self.xsrc(i), writes=[xtb[s]])
            self.act(junk[:], xt[s][:], AF.Square, [xtb[s]], [jb, ssb[i]], accum_out=ss[:, i:i + 1])
            self.rstd(ss[:, i:i + 1], 1.0 / D, [ssb[i]])
            self.ts("dve", x16[s][:], xt[s][:], ss[:, i:i + 1], None, ALU.mult, None, [xtb[s], ssb[i]], [x16b[s]])
            pt, pb = self.ps()
            pv = pt[:].bitcast(BF16)
            for k in range(KC):
                self.tr(pv[:, k * 128:(k + 1) * 128], x16[s][:, k * 128:(k + 1) * 128], self.idb[:], [x16b[s], cb], [pb],
                        signal=(k == KC - 1))
            self.tt("dve", self.xn[:, :, i * 128:(i + 1) * 128], pv.rearrange("p (k t) -> p k t", t=128),
                    self.vecs[:, V_NPRE:V_NPRE + 8].unsqueeze(2).broadcast_to([128, KC, 128]), ALU.mult,
                    [pb, cb], [self.xnb[i]])

    def _gdn_scalars(self):
        S, sb = self.S, self.sb
        cb = self.cb
        gb = self.gsb = S.buf("gsc")
        wba = sb("wba16", [128, KC, 16], BF16)
        S.dma(wba[:], self.wba_d, writes=[gb], eng="pool")
        zba = sb("zba", [128, NT, 16])
        pt, pb = self.ps()
        for i in range(NT):
            for k in range(KC):
                self.mm(pt[:, i * 16:(i + 1) * 16], self.xn[:, k, i * 128:(i + 1) * 128], wba[:, k, :], k == 0, k == KC - 1,
                        [self.xnb[i], gb], [pb])
        self.cp("dve", zba[:].rearrange("p t c -> p (t c)"), pt[:, 0:NT * 16], [pb], [gb])
        self.act(self.beta[:], zba[:, :, 0:8], AF.Sigmoid, [gb], [gb])
        self.ts("dve", self.nbeta[:], self.beta[:], -1.0, None, ALU.mult, None, [gb], [gb])
        self.ts("dve", self.betah[:], self.beta[:], 0.5, None, ALU.mult, None, [gb], [gb])
        tmp = sb("gs_tmp", [128, NT, 8])
        self.tt("dve", tmp[:], zba[:, :, 8:16], self.dtb[:].unsqueeze(1).broadcast_to([128, NT, 8]), ALU.add, [gb, cb], [gb])
        self.act(tmp[:], tmp[:], AF.Exp, [gb], [gb])
        self.act(tmp[:], tmp[:], AF.Ln, [gb], [gb], bias=1.0)
        self.tt("dve", self.gtok[:], tmp[:], self.negA[:].unsqueeze(1).broadcast_to([128, NT, 8]), ALU.mult, [gb, cb], [gb])
        g2 = self.gtok[:].rearrange("p t h -> p (t h)")
        pt, pb = self.ps()
        self.mm(pt[:, 0:128], self.uincl[:], g2[:, 0:128], True, True, [gb, cb], [pb])
        self.mm(pt[:, 128:136], self.uincl_b[:], g2[:, 128:136], True, True, [gb, cb], [pb])
        self.act(self.egc[:].rearrange("p t h -> p (t h)"), pt[:, 0:136], AF.Exp, [pb], [gb])
        pt, pb = self.ps()
        self.mm(pt[:, 0:128], self.lstr[:], g2[:, 0:128], True, True, [gb, cb], [pb])
        self.mm(pt[:, 128:136], self.lstr_b[:], g2[:, 128:136], True, True, [gb, cb], [pb])
        self.act(self.kdc[:].rearrange("p t h -> p (t h)"), pt[:, 0:136], AF.Exp, [pb], [gb])
        pt, pb = self.ps()
        self.mm(pt[:, 0:128], self.onesf[:], g2[:, 0:128], True, True, [gb, cb], [pb])
        self.act(self.eglb[:].rearrange("p t h -> p (t h)"), pt[:, 0:128], AF.Exp, [pb], [gb])
        gsel = sb("gsel", [128, 16, 8])
        self.tt("dve", gsel[:], self.gtok[:, 16, :].unsqueeze(1).broadcast_to([128, 16, 8]),
                self.selc[:].unsqueeze(2).broadcast_to([128, 16, 8]), ALU.mult, [gb, cb], [gb])
        pt, pb = self.ps()
        self.mm(pt[:, 0:128], self.onesf[:], gsel[:].rearrange("p s h -> p (s h)"), True, True, [gb, cb], [pb])
        self.act(self.eglS[:].rearrange("p s h -> p (s h)"), pt[:, 0:128], AF.Exp, [pb], [gb])
        self.negegc = None

    def segs(self):
        return [(0, 512, 4), (512, 1024, 4), (1024, 1536, 4), (1536, 2048, 4), (2048, 2176, 1)]

    def proj(self, w, wb, c0, n):
        pt, pb = self.ps()
        tiles = range(c0 // 128, (c0 + n) // 128)
        rd = [self.xnb[t] for t in tiles] + [wb]
        for k in range(KC):
            self.mm(pt[:, 0:n], w[:, k, :], self.xn[:, k, c0:c0 + n], k == 0, k == KC - 1, rd, [pb])
        return pt, pb

    def conv(self, pt, pb, zp, zpb, zprev, zprevb, si, n, wcol, bcol, stateT, out, outb, stg):
        cb = self.cb
        if si < 4:
            self.cp("act", zp[:, 3:3 + n], pt[:, 0:n], [pb], [zpb])
            if si == 0:
                self.memset("pool", zp[:, 0:3], 0.0, [zpb])
            else:
                self.cp("pool", zp[:, 0:3], zprev[:, 512:515], [zprevb], [zpb])
            if si == 3:
                self.cp("pool", stg[:, 0:3], zp[:, 512:515], [zpb], [])
            src = [zp[:, i:i + n] for i in range(4)]
            dst = out[:, 0:n]
        else:
            z3 = zp[:, 0:176].rearrange("p (s t) -> p s t", t=11)
            self.cp("act", z3[:, :, 3:11], pt[:, 0:128].rearrange("p (s t) -> p s t", t=8), [pb], [zpb])
            self.cp("pool", z3[:, :, 0:3], stateT.rearrange("p (s i) -> p s i", i=3), [cb], [zpb])
            self.cp("pool", stg[:, 3:51].rearrange("p (s i) -> p s i", i=3), z3[:, :, 8:11], [zpb], [])
            src = [z3[:, :, i:i + 8] for i in range(4)]
            dst = out[:, 0:128].rearrange("p (s t) -> p s t", t=8)
        if bcol is None:
            self.ts("dve", dst, src[0], wcol(0), None, ALU.mult, None, [zpb, cb], [outb])
        else:
            self.ts("dve", dst, src[0], wcol(0), bcol, ALU.mult, ALU.add, [zpb, cb], [outb])
        for i in range(1, 4):
            self.stt(dst, src[i], wcol(i), dst, ALU.mult, ALU.add, [zpb, cb, outb], [outb])

    def _phase_lru(self):
        S, sb = self.S, self.sb
        cb = self.cb
        V = self.vecs
        wab = S.buf("wa")
        wa16, wx16 = sb("wa32", [128, 8, 128]), sb("wx32", [128, 8, 128])
        S.dma(wa16[:], self.wa_d, writes=[wab])
        S.dma(wx16[:], self.wx_d, writes=[wab])
        wxs = [sb("lwx%d" % i, [128, KC, 128], BF16) for i in range(2)]
        wgs = [sb("lwg%d" % i, [128, KC, 128], BF16) for i in range(2)]
        wxb = [S.buf("lwx") for _ in range(2)]
        wgb = [S.buf("lwg") for _ in range(2)]
        lzc = [sb("lzc%d" % i, [128, 4]) for i in range(2)]
        lzcb = [S.buf("lzc") for _ in range(2)]
        nm = ["xc", "r", "ig", "a", "bb", "h", "sg"]
        W = [{n: sb("l%s%d" % (n, i), [128, 520] if n == "zp" else [128, 512], BF16 if n == "xc16" else F32) for n in nm}
             for i in range(self.LSLOTS)]
        WB = [{n: S.buf("l" + n) for n in nm} for _ in range(self.LSLOTS)]
        cnt = 0
        for g in range(8):
            s = g % 2
            S.dma(wxs[s][:], self.winr[g], writes=[wxb[s]], eng="pool")
            S.dma(wgs[s][:], self.winr[8 + g], writes=[wgb[s]], eng="pool")
            prev = None
            for si, (c0, c1, nch) in enumerate(self.segs()):
                n = c1 - c0
                w, wb_ = W[cnt % self.LSLOTS], WB[cnt % self.LSLOTS]
                pw, pwb = W[(cnt - 1) % self.LSLOTS], WB[(cnt - 1) % self.LSLOTS]
                cnt += 1
                pt, pb = self.proj(wxs[s], wxb[s], c0, n)
                self.conv3(pt, pb, lzc[g % 2], lzcb[g % 2], si, n,
                           lambda i: V[:, V_LCW + i * 8 + g:V_LCW + i * 8 + g + 1],
                           self.lcT[:, g, :], w["xc"], wb_["xc"], self.stg_lc[:, g, :], bcol=V[:, V_LCB + g:V_LCB + g + 1])
                pa, pab = self.ps()
                self.mm(pa[:, 0:n], wa16[:, g, :], w["xc"][:, 0:n], True, True, [wab, wb_["xc"]], [pab])
                px, pxb = self.ps()
                self.mm(px[:, 0:n], wx16[:, g, :], w["xc"][:, 0:n], True, True, [wab, wb_["xc"]], [pxb])
                self.act(w["r"][:, 0:n], pa[:, 0:n], AF.Tanh, [pab, cb], [wb_["r"]], scale=0.5, bias=self.hb[:, g:g + 1])
                self.act(w["ig"][:, 0:n], px[:, 0:n], AF.Tanh, [pxb, cb], [wb_["ig"]], scale=0.5, bias=self.hb[:, 8 + g:9 + g])
                self.act(w["a"][:, 0:n], w["r"][:, 0:n], AF.Exp, [wb_["r"], cb], [wb_["a"]], scale=self.hc[:, g:g + 1],
                         bias=self.hc[:, g:g + 1])
                self.tt("pool", w["r"][:, 0:n], w["a"][:, 0:n], w["a"][:, 0:n], ALU.mult, [wb_["a"]], [wb_["r"]])
                self.act(w["r"][:, 0:n], w["r"][:, 0:n], AF.Sqrt, [wb_["r"]], [wb_["r"]], scale=-1.0, bias=1.0)
                self.ts("pool", w["ig"][:, 0:n], w["ig"][:, 0:n], 0.5, 0.5, ALU.mult, ALU.add, [wb_["ig"]], [wb_["ig"]])
                self.tt("pool", w["bb"][:, 0:n], w["ig"][:, 0:n], w["xc"][:, 0:n], ALU.mult, [wb_["ig"], wb_["xc"]], [wb_["bb"]])
                if si == 0:
                    self.memset("pool", w["r"][:, 0:1], 1.0, [wb_["r"]])
                    self.memset("pool", w["a"][:, 0:1], 0.0, [wb_["a"]])
                self.tt("pool", w["bb"][:, 0:n], w["bb"][:, 0:n], w["r"][:, 0:n], ALU.mult, [wb_["bb"], wb_["r"]], [wb_["bb"]])
                if si == 4:
                    a3 = w["a"][:, 0:128].rearrange("p (s t) -> p s t", t=8)
                    b3 = w["bb"][:, 0:128].rearrange("p (s t) -> p s t", t=8)
                    t3 = w["r"][:, 0:16].unsqueeze(2)
                    self.tt("pool", t3, a3[:, :, 0:1], self.h0T[:, g, :].unsqueeze(2), ALU.mult, [wb_["a"], cb], [wb_["r"]])
                    self.tt("pool", b3[:, :, 0:1], b3[:, :, 0:1], t3, ALU.add, [wb_["bb"], wb_["r"]], [wb_["bb"]])
                    self.memset("pool", a3[:, :, 0:1], 0.0, [wb_["a"]])
                init = 0.0 if si in (0, 4) else prev
                rds = [wb_["a"], wb_["bb"]] + ([prevb] if si in (1, 2, 3) else [])
                S.op("dve", lambda e, o=w["h"][:, 0:n], a=w["a"][:, 0:n], b=w["bb"][:, 0:n], i0=init:
                     e.tensor_tensor_scan(out=o, data0=a, data1=b, initial=i0, op0=ALU.mult, op1=ALU.add),
                     reads=rds, writes=[wb_["h"]], dur=0.12 + 2 * n / 960.0)
                prev, prevb = w["h"][:, n - 1:n], wb_["h"]
                if si == 3:
                    self.cp("pool", self.stg_lh[:, g, 0:1], w["h"][:, 511:512], [wb_["h"]], [])
                if si == 4:
                    self.cp("pool", self.stg_lh[:, g, 1:17].unsqueeze(2),
                            w["h"][:, 0:128].rearrange("p (s t) -> p s t", t=8)[:, :, 7:8], [wb_["h"]], [])
                pg, pgb = self.proj(wgs[s], wgb[s], c0, n)
                self.act(w["sg"][:, 0:n], pg[:, 0:n], AF.Tanh, [pgb], [wb_["sg"]], scale=0.5)
                self.stt(w["sg"][:, 0:n], w["sg"][:, 0:n], 1.0, pg[:, 0:n], ALU.add, ALU.mult, [wb_["sg"], pgb], [wb_["sg"]])
                self.stt(self.lru_out[:, g, c0:c1], w["sg"][:, 0:n], 0.5, w["h"][:, 0:n], ALU.mult, ALU.mult,
                         [wb_["h"], wb_["sg"]], [])

    def _phase_gdn(self):
        S, sb = self.S, self.sb
        NS = 2
        self.g_wq = [[sb("gw%d_%d" % (j, i), [128, KC, 128], BF16) for j in range(4)] for i in range(NS)]
        self.g_wqb = [[S.buf("gw") for j in range(4)] for i in range(NS)]

        def ws(i, nc_):
            nn = nc_ * 128
            d = {}
            d["zc"] = [sb("gzc%d_%d" % (j, i), [128, 4]) for j in range(3)]
            d["cv"] = sb("gcv%d" % i, [128, nn])
            d["cv2"] = sb("gcv2_%d" % i, [128, nn])
            d["sq"] = sb("gsq%d" % i, [128, nn], BF16)
            d["rt"] = sb("grt%d" % i, [128, nn])
            d["th"] = sb("gth%d" % i, [128, nn])
            d["kq"] = sb("gkq%d" % i, [128, nc_, 2, 128], BF16)
            d["vf"] = sb("gvf%d" % i, [128, nn], BF16)
            d["sgate"] = sb("gsg%d" % i, [128, nn], BF16)
            d["ktv"] = sb("gktv%d" % i, [128, 2, nc_, 128], BF16)
            d["gA"] = sb("ggA%d" % i, [128, nc_, 128])
            d["gB"] = sb("ggB%d" % i, [128, nc_, 128])
            if self.OT_ALIAS or nc_ == 1:
                d["ot"] = d["cv2"][:].rearrange("p (c t) -> p c t", t=128)
            else:
                d["ot"] = sb("got%d" % i, [128, nc_, 128])
            d["on"] = sb("gon%d" % i, [128, nc_, 128], BF16)
            d["ss4"] = sb("gss4_%d" % i, [128, 4])
            d["kg"] = sb("gkg%d" % i, [128, nc_, 128], BF16)
            d["kdec"] = sb("gkd%d" % i, [128, nc_, 128], BF16)
            d["aqk"] = sb("gaq%d" % i, [128, nc_, 128], BF16)
            d["py"] = [sb("gpy%d_%d" % (j, i), [128, nc_, 2, 128], BF16) for j in range(2)]
            d["pt"] = [sb("gpt%d_%d" % (j, i), [128, nc_, 128], BF16) for j in range(2)]
            d["ub"] = sb("gub%d" % i, [128, nc_, 128])
            d["wT"] = sb("gwT%d" % i, [128, nc_, 128], BF16)
            if nc_ == 4:
                d["s32"] = sb("s32_%d" % i, [128, 128])
                d["s16"] = sb("s16_%d" % i, [128, 128], BF16)
            d["qs"] = sb("gqs%d" % i, [128, 128])
            d["vn"] = sb("gvn%d" % i, [128, 128], BF16)
            return d
        self.g_W = [ws(i, 4) for i in range(NS)] + [ws(2, 1)]
        keys = ["zc0", "zc1", "zc2", "cv", "cv2", "sq", "rt", "th", "kq", "vf", "sgate", "ktv", "gA", "gB", "ot", "on", "ss4", "kg",
                "kdec", "aqk", "ub", "wT", "s32", "s16", "qs", "vn"] + \
               ["%s%d_%d" % (a_, j, p) for a_ in ("pyP", "pyY", "pt") for j in range(2) for p in range(2)]
        self.g_WB = [{k: S.buf("g" + k) for k in keys} for _ in range(NS + 1)]
        for di_, d_ in enumerate(self.g_WB):
            if self.OT_ALIAS or di_ == NS:
                d_["cv2"] = d_["ot"]
        self.g_s0 = sb("s0", [128, 16, 128])
        self.g_s0b = S.buf("s0")
        self.g_snew = [sb("snew%d" % i, [128, 4, 128]) for i in range(2)]
        self.g_snewb = [S.buf("snew") for _ in range(2)]
        self.g_kdm = sb("kdm", [128, 16, 128], BF16)
        self.g_wq32 = sb("wq32", [128, 16, 2, 8])
        self.g_wsq = sb("wsq", [128, 2, 128])
        self.g_mb = S.buf("masked")
        gl_h = self.GLIM[0]
        psr = self.PSR
        for h0 in range(0, gl_h, NS):
            hs = list(range(h0, min(h0 + NS, gl_h)))
            gens = [self.gdn_stream(h, h - h0, h - h0, [0, 1, 2, 3]) for h in hs] + [self.gdn_samples(hs, h0)]
            alive = [True] * len(gens)
            while any(alive):
                for gi, g in enumerate(gens):
                    if not alive[gi]:
                        continue
                    self.psrange = psr[gi] if gi < len(hs) else psr[2]
                    try:
                        next(g)
                    except StopIteration:
                        alive[gi] = False
                    self.psrange = tuple(range(8))

    def gdn_samples(self, hs, h0):
        for h in hs:
            yield from self.gdn_stream(h, h - h0, 2, [4])

    def gdn_stream(self, h, wslot, slot, seglist):
        S = self.S
        cb, gb = self.cb, self.gsb
        V = self.vecs
        w, wb_ = self.g_W[slot], self.g_WB[slot]
        wq, wqb = self.g_wq[wslot], self.g_wqb[wslot]
        prompt = (seglist[0] == 0)
        if prompt:
            s32, s16, s32b, s16b = w["s32"], w["s16"], wb_["s32"], wb_["s16"]
        s0, s0b_ = self.g_s0, self.g_s0b
        snew, snewb, kdm, mb = self.g_snew, self.g_snewb, self.g_kdm, self.g_mb
        wq32, wsq = self.g_wq32, self.g_wsq
        gl_h, gl_s, gl_st = self.GLIM
        scale_q = float(128 ** -0.5)
        bP = lambda j, c: wb_["pyP%d_%d" % (j, c // 2)]
        bY = lambda j, c: wb_["pyY%d_%d" % (j, c // 2)]
        bT = lambda j, c: wb_["pt%d_%d" % (j, c // 2)]
        if prompt:
            for j in range(4):
                S.dma(wq[j][:], self.winr[16 + 8 * j + h], writes=[wqb[j]], eng="pool")
            self.memset("pool", s32[:], 0.0, [s32b])
            self.memset("pool", s16[:], 0.0, [s16b])
            yield
        for si, (c0, c1, nch) in enumerate(self.segs()):
            if si not in seglist:
                continue
            n = c1 - c0
            sample = (si == 4)
            if (gl_s == 4 and sample) or (gl_s == 1 and si != 0) or (gl_s == -1 and not sample):
                continue
            um, usm, lm = (self.uincl_b, self.ustr_b, self.lstr_b) if sample else (self.uincl, self.ustr, self.lstr)
            for j, cvn in ((0, "cv"), (1, "cv2"), (2, "cv")):
                S.tag = "1proj"
                blk = 8 * j + h
                cvt, cvb = w[cvn], wb_[cvn]
                pt, pb = self.proj(wq[j], wqb[j], c0, n)
                self.conv3(pt, pb, w["zc"][j], wb_["zc%d" % j], si, n,
                           lambda i, blk=blk: V[:, V_GCW + i * 24 + blk:V_GCW + i * 24 + blk + 1],
                           self.gcT[:, blk, :], cvt, cvb, self.stg_gc[:, blk, :])
                self.act(w["th"][:, 0:n], cvt[:, 0:n], AF.Tanh, [cvb], [wb_["th"]], scale=0.5)
                if j == 2:
                    self.stt(w["vf"][:, 0:n], w["th"][:, 0:n], 1.0, cvt[:, 0:n], ALU.add, ALU.mult,
                             [wb_["th"], cvb], [wb_["vf"]])
                    yield
                    continue
                self.stt(cvt[:, 0:n], w["th"][:, 0:n], 1.0, cvt[:, 0:n], ALU.add, ALU.mult, [wb_["th"], cvb], [cvb])
                self.tt("pool", w["sq"][:, 0:n], cvt[:, 0:n], cvt[:, 0:n], ALU.mult, [cvb], [wb_["sq"]])
                p2, p2b = self.ps()
                self.mm(p2[:, 0:n], self.onesb[:], w["sq"][:, 0:n], True, True, [cb, wb_["sq"]], [p2b])
                self.act(w["rt"][:, 0:n], p2[:, 0:n], AF.Ln, [p2b], [wb_["rt"]], bias=4.0 * EPS)
                self.act(w["rt"][:, 0:n], w["rt"][:, 0:n], AF.Exp, [wb_["rt"]], [wb_["rt"]], scale=-0.5,
                         bias=(math.log(scale_q) if j == 0 else 0.0))
                dst = w["kq"][:, 0:nch, 1 - j, :]
                src = cvt[:, 0:n].rearrange("p (c t) -> p c t", t=128)
                rt3 = w["rt"][:, 0:n].rearrange("p (c t) -> p c t", t=128)
                self.tt(self.NORM_ENG, dst, src, rt3, ALU.mult, [cvb, wb_["rt"]], [wb_["kq"]])
                yield
            S.tag = "1proj"
            pg, pgb = self.proj(wq[3], wqb[3], c0, n)
            self.act(w["th"][:, 0:n], pg[:, 0:n], AF.Tanh, [pgb], [wb_["th"]], scale=0.5)
            self.stt(w["sgate"][:, 0:n], w["th"][:, 0:n], 1.0, pg[:, 0:n], ALU.add, ALU.mult, [wb_["th"], pgb], [wb_["sgate"]])
            yield
            if gl_st < 2:
                continue
            S.tag = "2chunk"
            ti0 = c0 // 128
            bc = lambda t: t[:, ti0:ti0 + nch, h:h + 1].broadcast_to([128, nch, 128])
            cs = slice(0, nch)
            ptr, ptrb = self.ps()
            pv = ptr[:].bitcast(BF16)
            for c in range(nch):
                self.tr(pv[:, c * 128:(c + 1) * 128], w["kq"][:, c, 0, :], self.idb[:], [wb_["kq"], cb], [ptrb], signal=False)
            for c in range(nch):
                self.tr(pv[:, 512 + c * 128:512 + (c + 1) * 128], w["vf"][:, c * 128:(c + 1) * 128], self.idb[:],
                        [wb_["vf"], cb], [ptrb], signal=(c == nch - 1))
            self.cp("act", w["ktv"][:, :, cs, :], pv.rearrange("p (a c t) -> p a c t", a=2, t=128)[:, :, cs, :], [ptrb], [wb_["ktv"]])
            KT = w["ktv"][:, 0, cs, :]
            self.tt("pool", w["kg"][:, cs, :], KT, bc(self.egc), ALU.mult, [wb_["ktv"], gb], [wb_["kg"]])
            self.tt("pool", w["kdec"][:, cs, :], KT, bc(self.kdc), ALU.mult, [wb_["ktv"], gb], [wb_["kdec"]])
            mask3 = lambda m: m[:].unsqueeze(1).broadcast_to([128, nch, 128])
            self.tt("pool", w["gA"][:, cs, :], mask3(um), bc(self.gtok), ALU.mult, [cb, gb], [wb_["gA"]])
            pd, pdb = self.ps()
            for c in range(nch):
                self.mm(pd[:, c * 128:(c + 1) * 128], lm[:], w["gA"][:, c, :], True, True, [cb, wb_["gA"]], [pdb],
                        signal=(c == nch - 1))
            self.act(w["gB"][:, cs, :], pd[:, 0:nch * 128].rearrange("p (c t) -> p c t", t=128), AF.Exp, [pdb], [wb_["gB"]])
            self.tt("pool", w["gA"][:, cs, :], w["gB"][:, cs, :], mask3(um), ALU.mult, [wb_["gB"], cb, wb_["gA"]], [wb_["gA"]])
            if sample:
                self.tt("pool", w["gB"][:, cs, :], w["gB"][:, cs, :], mask3(usm), ALU.mult, [wb_["gB"], cb], [wb_["gB"]])
                self.tt("pool", w["gB"][:, cs, :], w["gB"][:, cs, :], bc(self.nbeta), ALU.mult, [wb_["gB"], gb], [wb_["gB"]])
            else:
                goff = w["rt"][:, 0:n].rearrange("p (c t) -> p c t", t=128)
                self.tt("pool", w["gB"][:, cs, :], w["gB"][:, cs, :], bc(self.nbeta), ALU.mult, [wb_["gB"], gb], [wb_["gB"]])
                self.tt("pool", goff, w["gB"][:, cs, :], mask3(self.offf), ALU.mult, [wb_["gB"], cb], [wb_["rt"]])
                self.tt("pool", w["gB"][:, cs, :], w["gB"][:, cs, :], mask3(self.usbd), ALU.mult, [wb_["gB"], cb], [wb_["gB"]])
            if nch == 1:
                pk2, pk2b1 = self.ps()
                pk2b = [pk2b1]
            else:
                pk2, pk2b = self.ps2()
            for c in range(nch):
                self.mm(pk2[:, c * 256:(c + 1) * 256], w["kq"][:, c, 0, :], w["kq"][:, c, :, :].rearrange("p a b -> p (a b)"),
                        True, True, [wb_["kq"]], pk2b, signal=(c == nch - 1))
            pk4 = pk2[:, 0:nch * 256].rearrange("p (c a t) -> p c a t", a=2, t=128)
            self.tt("dve", w["py"][0][:, cs, 0, :], pk4[:, :, 0, :], w["gB"][:, cs, :], ALU.mult, pk2b + [wb_["gB"]], [bP(0, 0)])
            if not sample:
                self.tt("dve", w["on"][:, cs, :], pk4[:, :, 0, :], goff, ALU.mult, pk2b + [wb_["rt"]], [wb_["on"]])
            self.tt("dve", w["aqk"][:, cs, :], pk4[:, :, 1, :], w["gA"][:, cs, :], ALU.mult, pk2b + [wb_["gA"]], [wb_["aqk"]])
            pp, ppb = self.ps()
            ppv = pp[:].bitcast(BF16)
            if not sample:
                for c in range(nch):
                    self.tr(ppv[:, 512 + c * 128:512 + (c + 1) * 128], w["on"][:, c, :], self.idb[:], [wb_["on"], cb], [ppb], signal=False)
            for c in range(nch):
                self.tr(ppv[:, c * 128:(c + 1) * 128], w["py"][0][:, c, 0, :], self.idb[:], [bP(0, 0), cb], [ppb],
                        signal=(c == nch - 1))
            self.cp("act", w["pt"][0][:, cs, :], ppv[:, 0:nch * 128].rearrange("p (c t) -> p c t", t=128), [ppb], [bT(0, 0)])
            if not sample:
                self.cp("act", w["ktv"][:, 0, cs, :], ppv[:, 512:512 + nch * 128].rearrange("p (c t) -> p c t", t=128), [ppb], [wb_["ktv"]])
            self.tt("pool", w["py"][1][:, cs, 1, :], w["py"][0][:, cs, 0, :], mask3(self.idb), ALU.add,
                    [bP(0, 0), cb], [bY(1, 0)])
            yield
            if gl_st < 3:
                continue
            S.tag = "3dbl"
            nit = 3 if sample else 6
            pX, pXb = self.ps()
            pB, pBb = self.ps()
            for c in range(nch):
                self.mm(pX[:, c * 128:(c + 1) * 128], w["pt"][0][:, c, :], w["py"][0][:, c, 0, :], True, True,
                        [bT(0, 0), bP(0, 0)], [pXb], signal=(c == nch - 1))
            for c in range(nch):
                self.mm(pB[:, c * 128:(c + 1) * 128], w["py"][0][:, c, 0, :], w["pt"][0][:, c, :], True, True,
                        [bT(0, 0), bP(0, 0)], [pBb], signal=(c == nch - 1))
            v3 = lambda p_: p_[:, 0:nch * 128].rearrange("p (c t) -> p c t", t=128)
            self.cp("act", w["py"][1][:, cs, 0, :], v3(pX), [pXb], [bP(1, 0)])
            self.cp("act", w["pt"][1][:, cs, :], v3(pB), [pBb], [bT(1, 0)])
            yield
            for k in range(2, nit + 1):
                rd, wr = (k - 1) % 2, k % 2
                last = (k == nit)
                rdb = [bT(rd, 0), bP(rd, 0), bY(rd, 0)]
                if last or nch == 1 or not self.DBL256:
                    pZ, pZb = self.ps()
                    for c in range(nch):
                        self.mm(pZ[:, c * 128:(c + 1) * 128], w["pt"][rd][:, c, :], w["py"][rd][:, c, 1, :], True, True,
                                rdb, [pZb], signal=(c == nch - 1))
                    self.tt("dve", w["py"][wr][:, cs, 1, :], v3(pZ), w["py"][rd][:, cs, 1, :], ALU.add, [pZb, bY(rd, 0)], [bY(wr, 0)])
                    if not last:
                        pX, pXb = self.ps()
                        for c in range(nch):
                            self.mm(pX[:, c * 128:(c + 1) * 128], w["pt"][rd][:, c, :], w["py"][rd][:, c, 0, :], True, True,
                                    rdb, [pXb], signal=(c == nch - 1))
                        self.cp("act", w["py"][wr][:, cs, 0, :], v3(pX), [pXb], [bP(wr, 0)])
                else:
                    pA2, pA2b = self.ps2()
                    for c in range(nch):
                        self.mm(pA2[:, c * 256:(c + 1) * 256], w["pt"][rd][:, c, :], w["py"][rd][:, c, :, :].rearrange("p a b -> p (a b)"),
                                True, True, rdb, pA2b, signal=(c == nch - 1))
                    p4 = pA2[:, 0:nch * 256].rearrange("p (c a t) -> p c a t", a=2, t=128)
                    self.tt("dve", w["py"][wr][:, cs, 1, :], p4[:, :, 1, :], w["py"][rd][:, cs, 1, :], ALU.add, pA2b + [bY(rd, 0)], [bY(wr, 0)])
                    self.cp("act", w["py"][wr][:, cs, 0, :], p4[:, :, 0, :], pA2b, [bP(wr, 0)])
                if not last:
                    pB, pBb = self.ps()
                    for c in range(nch):
                        self.mm(pB[:, c * 128:(c + 1) * 128], w["py"][rd][:, c, 0, :], w["pt"][rd][:, c, :], True, True,
                                rdb, [pBb], signal=(c == nch - 1))
                    self.cp("act", w["pt"][wr][:, cs, :], v3(pB), [pBb], [bT(wr, 0)])
                yield
            if gl_st < 4:
                continue
            yb = nit % 2
            S.tag = "4ubw"
            if sample:
                pu, pub = self.ps()
                pw2, pw2b = self.ps()
                for c in range(nch):
                    self.mm(pu[:, c * 128:(c + 1) * 128], w["py"][yb][:, c, 1, :], w["ktv"][:, 1, c, :], True, True,
                            [bY(yb, 0), wb_["ktv"]], [pub], signal=(c == nch - 1))
                for c in range(nch):
                    self.mm(pw2[:, c * 128:(c + 1) * 128], w["kg"][:, c, :], w["py"][yb][:, c, 1, :], True, True,
                            [bY(yb, 0), wb_["kg"]], [pw2b], signal=(c == nch - 1))
            else:
                Yd = lambda c: w["py"][yb][:, c, 1, :]
                sq3 = w["sq"][:, 0:n].rearrange("p (c t) -> p c t", t=128)
                vf3 = w["vf"][:, 0:n].rearrange("p (c t) -> p c t", t=128)
                pz_, pzb_ = self.ps()
                for c in range(nch):
                    self.mm(pz_[:, c * 128:(c + 1) * 128], w["ktv"][:, 0, c, :], Yd(c), True, True, [wb_["ktv"], bY(yb, 0)], [pzb_],
                            signal=(c == nch - 1))
                self.cp("act", w["on"][:, cs, :], v3(pz_), [pzb_], [wb_["on"]])
                pu, pub = self.ps()
                for c in range(nch):
                    self.mm(pu[:, c * 128:(c + 1) * 128], Yd(c), w["ktv"][:, 1, c, :], c == 0, False, [bY(yb, 0), wb_["ktv"]], [pub],
                            signal=(c == nch - 1), sgc=True)
                self.cp("act", sq3, v3(pu), [pub], [wb_["sq"]])
                pw0, pw0b = self.ps()
                for c in range(nch):
                    self.mm(pw0[:, c * 128:(c + 1) * 128], Yd(c), w["kg"][:, c, :], True, True, [bY(yb, 0), wb_["kg"]], [pw0b],
                            signal=(c == nch - 1))
                self.cp("act", vf3, v3(pw0), [pw0b], [wb_["vf"]])
                for c in range(nch):
                    self.mm(pu[:, c * 128:(c + 1) * 128], w["on"][:, c, :], sq3[:, c, :], False, True, [wb_["on"], wb_["sq"]], [pub],
                            signal=(c == nch - 1), sgc=True)
                pw2, pw2b = self.ps()
                for c in range(nch):
                    self.mm(pw2[:, c * 128:(c + 1) * 128], w["kg"][:, c, :], Yd(c), c == 0, False, [bY(yb, 0), wb_["kg"]], [pw2b],
                            signal=False, sgc=True)
                for c in range(nch):
                    self.mm(pw2[:, c * 128:(c + 1) * 128], vf3[:, c, :], w["on"][:, c, :], False, True, [wb_["vf"], wb_["on"]], [pw2b],
                            signal=(c == nch - 1), sgc=True)
            self.tt("dve", w["ub"][:, cs, :], v3(pu), bc(self.betah), ALU.mult, [pub, gb], [wb_["ub"]])
            if sample:
                self.cp("act", wq32[:, :, 0, :], pw2[:, 0:128].rearrange("p (s t) -> p s t", t=8), [pw2b], [mb])
            else:
                self.cp("act", w["wT"][:, cs, :], v3(pw2), [pw2b], [wb_["wT"]])
                yield
            if gl_st < 5:
                continue
            if sample:
                S.tag = "5samp"
                S.dma(s0[:], self.sgS[:, h, :, :].rearrange("s k v -> k s v"), writes=[s0b_])
                self.tt("pool", kdm[:], w["kdec"][:, 0, :].unsqueeze(1).broadcast_to([128, 16, 128]),
                        self.selc[:].unsqueeze(2).broadcast_to([128, 16, 128]), ALU.mult, [wb_["kdec"], cb], [mb])
                self.cp("pool", wq32[:, :, 1, :], w["kq"][:, 0, 1, :].rearrange("p (s t) -> p s t", t=8), [wb_["kq"]], [mb])
            if gl_st < 6:
                continue
            po, pob = self.ps()
            for c in range(nch):
                S.tag = "6chain" if not sample else "6chainS"
                ti = c0 // 128 + c
                vn, qss = w["vn"], w["qs"]
                col = lambda t, ti=ti: t[:, ti, h:h + 1]
                if not sample:
                    pw_, pwb_ = self.ps()
                    pq_, pqb_ = self.ps()
                    self.mm(pw_[:, 0:128], w["wT"][:, c, :], s16[:], True, True, [wb_["wT"], s16b], [pwb_])
                    self.mm(pq_[:, 0:128], w["kq"][:, c, 1, :], s16[:], True, True, [wb_["kq"], s16b], [pqb_])
                    pwv, pqv = pw_[:, 0:128], pq_[:, 0:128]
                    pqb2 = pqb_
                else:
                    pws, pwsb = self.ps()
                    for s_ in range(16):
                        self.mm(pws[:, s_ * 16:(s_ + 1) * 16], s0[:, s_, :], wq32[:, s_, :, :].rearrange("p a t -> p (a t)"),
                                True, True, [s0b_, mb], [pwsb], signal=(s_ == 15))
                    self.cp("act", wsq[:].rearrange("p a (s t) -> p a s t", t=8),
                            pws[:, 0:256].rearrange("p (s a t) -> p a s t", a=2, t=8), [pwsb], [mb])
                    pw_, pwb_ = self.ps()
                    self.tr(pw_[:, 0:128], wsq[:, 0, :], self.idf[:], [mb, cb], [pwb_], signal=False)
                    self.tr(pw_[:, 128:256], wsq[:, 1, :], self.idf[:], [mb, cb], [pwb_])
                    pwv, pqv = pw_[:, 0:128], pw_[:, 128:256]
                    pqb2 = pwb_
                self.stt(vn[:], pwv, col(self.nbeta), w["ub"][:, c, :], ALU.mult, ALU.add,
                         [pwb_, gb, wb_["ub"]], [wb_["vn"]])
                self.ts("dve", qss[:], pqv, col(self.egc), None, ALU.mult, None, [pqb2, gb], [wb_["qs"]])
                self.mm(po[:, c * 128:(c + 1) * 128], w["aqk"][:, c, :], vn[:], True, True, [wb_["aqk"], wb_["vn"]], [pob])
                self.tt("dve", w["ot"][:, c, :], po[:, c * 128:(c + 1) * 128], qss[:], ALU.add, [pob, wb_["qs"]], [wb_["ot"]])
                if not sample:
                    pkv, pkvb = self.ps()
                    self.mm(pkv[:, 0:128], w["kdec"][:, c, :], vn[:], True, True, [wb_["kdec"], wb_["vn"]], [pkvb])
                    gl = self.eglb[:, ti, h:h + 1]
                    if self.S16_DVE:
                        self.stt(s16[:], s32[:], gl, pkv[:, 0:128], ALU.mult, ALU.add, [s32b, gb, pkvb], [s16b])
                        self.stt(s32[:], s32[:], gl, pkv[:, 0:128], ALU.mult, ALU.add, [s32b, gb, pkvb], [s32b])
                    else:
                        self.stt(s32[:], s32[:], gl, pkv[:, 0:128], ALU.mult, ALU.add, [s32b, gb, pkvb], [s32b])
                        self.cp("pool", s16[:], s32[:], [s32b], [s16b])
                else:
                    for q4 in range(4):
                        pkv, pkvb = self.ps()
                        for u in range(4):
                            s_ = q4 * 4 + u
                            self.mm(pkv[:, u * 128:(u + 1) * 128], kdm[:, s_, :], vn[:], True, True, [mb, wb_["vn"]], [pkvb],
                                    signal=(u == 3))
                        for u in range(4):
                            s_ = q4 * 4 + u
                            self.stt(snew[q4 % 2][:, u, :], s0[:, s_, :], self.eglS[:, s_, h:h + 1], pkv[:, u * 128:(u + 1) * 128],
                                     ALU.mult, ALU.add, [s0b_, gb, pkvb], [snewb[q4 % 2]])
                        S.dma(self.o_sgs[q4 * 4:(q4 + 1) * 4, h, :, :].rearrange("s k v -> k s v"), snew[q4 % 2][:],
                              reads=[snewb[q4 % 2]], is_output=True)
                yield
            S.tag = "7opath"
            for c in range(nch):
                self.act(w["on"][:, c, :], w["ot"][:, c, :], AF.Square, [wb_["ot"]], [wb_["on"], wb_["ss4"]],
                         accum_out=w["ss4"][:, c:c + 1])
            self.rstd(w["ss4"][:, cs], 1.0 / 128, [wb_["ss4"]])
            self.tt("dve", w["on"][:, cs, :], w["ot"][:, cs, :], w["ss4"][:, cs].unsqueeze(2).broadcast_to([128, nch, 128]),
                    ALU.mult, [wb_["ot"], wb_["ss4"]], [wb_["on"]])
            pz, pzb = self.ps()
            pzv = pz[:].bitcast(BF16)
            for c in range(nch):
                self.tr(pzv[:, c * 128:(c + 1) * 128], w["on"][:, c, :], self.idb[:], [wb_["on"], cb], [pzb], signal=(c == nch - 1))
            self.stt(self.gdn_out[:, h, c0:c0 + n], pzv[:, 0:n], self.gnwh[:, 0:1], w["sgate"][:, 0:n], ALU.mult, ALU.mult,
                     [pzb, cb, wb_["sgate"]], [])
            yield
        if prompt:
            S.dma(self.o_pgs[h, :, :], s32[:], reads=[s32b], is_output=True)
            yield

    def conv3(self, pt, pb, zc, zcb, si, n, wcol, stateT, out, outb, stg, bcol=0.0):
        cb = self.cb
        if si < 4:
            self.act(out[:, 0:n], pt[:, 0:n], AF.Identity, [pb, cb], [outb], scale=wcol(3), bias=bcol)
            for i in range(3):
                sh = 3 - i
                self.stt(out[:, sh:n], pt[:, 0:n - sh], wcol(i), out[:, sh:n], ALU.mult, ALU.add, [pb, cb, outb], [outb])
                if si > 0:
                    self.stt(out[:, 0:sh], zc[:, i:i + sh], wcol(i), out[:, 0:sh], ALU.mult, ALU.add, [zcb, cb, outb], [outb])
            if si == 3:
                self.cp("dve", stg[:, 0:3], pt[:, n - 3:n], [pb], [])
            else:
                self.cp("dve", zc[:, 0:3], pt[:, n - 3:n], [pb, outb], [zcb])
        else:
            p3 = pt[:, 0:128].rearrange("p (s t) -> p s t", t=8)
            o3 = out[:, 0:128].rearrange("p (s t) -> p s t", t=8)
            st3 = stateT.rearrange("p (s i) -> p s i", i=3)
            self.act(o3, p3, AF.Identity, [pb, cb], [outb], scale=wcol(3), bias=bcol)
            for i in range(3):
                sh = 3 - i
                self.stt(o3[:, :, sh:8], p3[:, :, 0:8 - sh], wcol(i), o3[:, :, sh:8], ALU.mult, ALU.add, [pb, cb, outb], [outb])
                self.stt(o3[:, :, 0:sh], st3[:, :, i:i + sh], wcol(i), o3[:, :, 0:sh], ALU.mult, ALU.add, [cb, outb], [outb])
            self.cp("dve", stg[:, 3:51].rearrange("p (s i) -> p s i", i=3), p3[:, :, 5:8], [pb], [])

    def conv2(self, pt, pb, zp, zpb, zc, zcb, si, n, wcol, bcol, stateT, out, outb, stg):
        cb = self.cb
        if si < 4:
            self.cp("act", zp[:, 3:3 + n], pt[:, 0:n], [pb], [zpb])
            if si == 0:
                self.memset("pool", zp[:, 0:3], 0.0, [zpb])
            else:
                self.cp("pool", zp[:, 0:3], zc[:, 0:3], [zcb], [zpb])
            if si == 3:
                self.cp("pool", stg[:, 0:3], zp[:, 512:515], [zpb], [])
            else:
                self.cp("pool", zc[:, 0:3], zp[:, 512:515], [zpb], [zcb])
            src = [zp[:, i:i + n] for i in range(4)]
            dst = out[:, 0:n]
        else:
            z3 = zp[:, 0:176].rearrange("p (s t) -> p s t", t=11)
            self.cp("act", z3[:, :, 3:11], pt[:, 0:128].rearrange("p (s t) -> p s t", t=8), [pb], [zpb])
            self.cp("pool", z3[:, :, 0:3], stateT.rearrange("p (s i) -> p s i", i=3), [cb], [zpb])
            self.cp("pool", stg[:, 3:51].rearrange("p (s i) -> p s i", i=3), z3[:, :, 8:11], [zpb], [])
            src = [z3[:, :, i:i + 8] for i in range(4)]
            dst = out[:, 0:128].rearrange("p (s t) -> p s t", t=8)
        if bcol is None:
            self.ts("dve", dst, src[0], wcol(0), None, ALU.mult, None, [zpb, cb], [outb])
        else:
            self.ts("dve", dst, src[0], wcol(0), bcol, ALU.mult, ALU.add, [zpb, cb], [outb])
        for i in range(1, 4):
            self.stt(dst, src[i], wcol(i), dst, ALU.mult, ALU.add, [zpb, cb, outb], [outb])

    def _state_outputs(self):
        S, sb = self.S, self.sb
        cb = self.cb
        scr2 = self.gdn_out[:].rearrange("p k t -> p (k t)").bitcast(F32)
        olc, olh, ogc = scr2[0:51, 0:D], scr2[0:17, D:2 * D], scr2[0:51, 2 * D:2 * D + 3072]
        ob = S.buf("ostate")
        for g in range(8):
            pt, pb = self.ps()
            self.tr(pt[0:51, 0:128], self.stg_lc[:, g, :], self.idf[:], [cb], [pb])
            self.cp("dve", olc[:, g * 128:(g + 1) * 128], pt[0:51, 0:128], [pb], [ob])
            pt, pb = self.ps()
            self.tr(pt[0:17, 0:128], self.stg_lh[:, g, :], self.idf[:], [cb], [pb])
            self.cp("dve", olh[:, g * 128:(g + 1) * 128], pt[0:17, 0:128], [pb], [ob])
        for g in range(24):
            pt, pb = self.ps()
            self.tr(pt[0:51, 0:128], self.stg_gc[:, g, :], self.idf[:], [cb], [pb])
            self.cp("dve", ogc[:, g * 128:(g + 1) * 128], pt[0:51, 0:128], [pb], [ob])
        S.dma(self.o_plc, olc[0:3, :], reads=[ob], is_output=True)
        S.dma(self.o_slc, olc[3:51, :], reads=[ob], is_output=True)
        S.dma(self.o_plh, olh[0:1, :], reads=[ob], is_output=True)
        S.dma(self.o_slh, olh[1:17, :], reads=[ob], is_output=True)
        S.dma(self.o_pgc, ogc[0:3, :], reads=[ob], is_output=True)
        S.dma(self.o_sgc, ogc[3:51, :], reads=[ob], is_output=True)

    def _phase_merge(self):
        S, sb = self.S, self.sb
        cb = self.cb
        merged, mgb = self.merged, self.mgb
        wm = [[sb("mw%d_%d" % (j, i), [128, KC, 128], BF16) for j in range(4)] for i in range(2)]
        wmb = [[S.buf("mw") for j in range(4)] for i in range(2)]
        s3 = [sb("ms3_%d" % i, [128, 512]) for i in range(2)]
        s4 = [sb("ms4_%d" % i, [128, 512]) for i in range(2)]
        t1 = [sb("mt1_%d" % i, [128, 512]) for i in range(2)]
        t2 = [sb("mt2_%d" % i, [128, 512]) for i in range(2)]
        mbs = [{k: S.buf("m" + k) for k in ("s3", "s4", "t1", "t2")} for _ in range(2)]
        cnt = 0
        for c in range(8):
            s = c % 2
            srcs = (self.wbl_d[c], self.wbg_d[c], self.winr[48 + c], self.winr[56 + c])
            for j in range(4):
                S.dma(wm[s][j][:], srcs[j], writes=[wmb[s][j]], eng="pool")
            for si, (c0, c1, nch) in enumerate(self.segs()):
                n = c1 - c0
                b = mbs[cnt % 2]
                i2 = cnt % 2
                cnt += 1
                acts = (self.lru_out, self.gdn_out, self.xn, self.xn)
                pp = []
                for j in range(4):
                    pt, pb = self.ps()
                    for k in range(KC):
                        self.mm(pt[:, 0:n], wm[s][j][:, k, :], acts[j][:, k, c0:c1], k == 0, k == KC - 1, [wmb[s][j]], [pb])
                    pp.append((pt, pb))
                self.act(s3[i2][:, 0:n], pp[2][0][:, 0:n], AF.Tanh, [pp[2][1]], [b["s3"]], scale=0.5)
                self.act(s4[i2][:, 0:n], pp[3][0][:, 0:n], AF.Tanh, [pp[3][1]], [b["s4"]], scale=0.5)
                self.stt(t1[i2][:, 0:n], s3[i2][:, 0:n], 1.0, pp[0][0][:, 0:n], ALU.add, ALU.mult, [pp[0][1], b["s3"]], [b["t1"]])
                self.stt(t2[i2][:, 0:n], s4[i2][:, 0:n], 1.0, pp[1][0][:, 0:n], ALU.add, ALU.mult, [pp[1][1], b["s4"]], [b["t2"]])
                self.tt("pool", merged[:, c, c0:c1], t1[i2][:, 0:n], t2[i2][:, 0:n], ALU.add, [b["t1"], b["t2"]], [mgb])

    def _phase_out(self):
        S, sb = self.S, self.sb
        cb = self.cb
        self._state_outputs()
        merged, mgb = self.merged, self.mgb
        wout16 = sb("wout16", [128, KC, D], BF16)
        wob = S.buf("wout")
        S.dma(wout16[:], self.wout_d, writes=[wob], eng="pool")
        npb = sb("npb", [128, D])
        npbb = S.buf("npb")
        S.dma(npb[:], self.npost_d.broadcast_to([128, D]), writes=[npbb])
        NB = 4
        scr = self.xn[:].rearrange("p k t -> p (k t)").bitcast(F32)
        yt = [scr[:, i * D:(i + 1) * D] for i in range(NB)]
        xr = [scr[:, (NB + i) * D:(NB + i + 1) * D] for i in range(NB)]
        ytb = [S.buf("yt") for _ in range(NB)]
        xrb = [S.buf("xr") for _ in range(NB)]
        ss = sb("ssD", [128, NT])
        ssb = [S.buf("ssd") for _ in range(NT)]
        jb = S.buf("junkd")
        junk = sb("junkd", [128, D], BF16)
        for i in range(NT):
            s = i % NB
            S.dma(xr[s][:], self.xsrc(i), writes=[xrb[s]])
            for hf in range(2):
                pt, pb = self.ps()
                for k in range(KC):
                    self.mm(pt[:, 0:512], merged[:, k, i * 128:(i + 1) * 128], wout16[:, k, hf * 512:(hf + 1) * 512],
                            k == 0, k == KC - 1, [mgb, wob], [pb])
                self.cp("act", yt[s][:, hf * 512:(hf + 1) * 512], pt[:, 0:512], [pb], [ytb[s]])
            self.act(junk[:], yt[s][:], AF.Square, [ytb[s]], [jb, ssb[i]], accum_out=ss[:, i:i + 1])
            self.rstd(ss[:, i:i + 1], 1.0 / D, [ssb[i]], eps=4.0 * EPS)
            self.stt(yt[s][:], yt[s][:], ss[:, i:i + 1], npb[:], ALU.mult, ALU.mult, [ytb[s], ssb[i], npbb], [ytb[s]])
            self.tt("pool", yt[s][:], yt[s][:], xr[s][:], ALU.add, [ytb[s], xrb[s]], [ytb[s]])
            S.dma(self.ydst(i), yt[s][:], reads=[ytb[s]], is_output=True)


_NC_CACHE = {}


def _program():
    if "nc" not in _NC_CACHE:
        _NC_CACHE["nc"] = K().build()
    return _NC_CACHE["nc"]


def _f(a):
    return np.ascontiguousarray(a, dtype=np.float32)


def kernel(x_prompt, x_sample, state_lru_conv, state_lru_h, state_gdn_conv, state_gdn_S,
           norm_pre, norm_post, w_in, lru_conv_w, lru_conv_b, lru_wa, lru_ba, lru_wx, lru_bx,
           lru_a_logit, gdn_conv_w, gdn_A_log, gdn_dt_bias, gdn_norm_w, w_br_lru, w_br_gdn, w_out):
    w_in0 = np.asarray(w_in)[0]
    wcat = np.concatenate([w_in0[:, 0:6144], w_in0[:, 6160:8208]], axis=1)
    winr = _f(wcat.reshape(KC, 128, 64, 128).transpose(2, 1, 0, 3))
    wba = _f(w_in0[:, 6144:6160].reshape(KC, 128, 16).transpose(1, 0, 2))
    wa = _f(np.asarray(lru_wa)[0].transpose(1, 0, 2))
    wx = _f(np.asarray(lru_wx)[0].transpose(1, 0, 2))
    wbl = _f(np.asarray(w_br_lru)[0].reshape(KC, 128, 8, 128).transpose(2, 1, 0, 3))
    wbg = _f(np.asarray(w_br_gdn)[0].reshape(KC, 128, 8, 128).transpose(2, 1, 0, 3))
    wout = _f(np.asarray(w_out)[0].reshape(KC, 128, D).transpose(1, 0, 2))
    cols = lambda v, n: np.asarray(v).reshape(n, 128).T
    vecs = _f(np.concatenate([
        cols(norm_pre[0], 8),
        np.asarray(lru_conv_w)[0].reshape(4, 8, 128).transpose(2, 0, 1).reshape(128, 32),
        cols(lru_conv_b[0], 8), cols(lru_ba[0], 8), cols(lru_bx[0], 8), cols(lru_a_logit[0], 8),
        np.asarray(gdn_conv_w)[0].reshape(4, 24, 128).transpose(2, 0, 1).reshape(128, 96),
        np.asarray(gdn_norm_w)[0].reshape(128, 1)], axis=1))
    npost = _f(np.asarray(norm_post).reshape(1, D))
    alog = _f(np.asarray(gdn_A_log).reshape(1, 8))
    dtb = _f(np.asarray(gdn_dt_bias).reshape(1, 8))
    in_maps = []
    for c in range(NCORES):
        sl = slice(16 * c, 16 * c + 16)
        in_maps.append(dict(
            xp=_f(x_prompt[c]), xs=_f(np.asarray(x_sample)[sl].reshape(128, D)),
            slc=_f(np.asarray(state_lru_conv)[0, sl].reshape(48, D)), slh=_f(np.asarray(state_lru_h)[0, sl]),
            sgc=_f(np.asarray(state_gdn_conv)[0, sl].reshape(48, 3072)), sgS=_f(np.asarray(state_gdn_S)[0, sl]),
            vecs=vecs, npost=npost, alog=alog, dtb=dtb, winr=winr, wba=wba, wa=wa, wx=wx, wbl=wbl, wbg=wbg, wout=wout))
    nc = _program()
    res = run_bass_kernel_spmd(nc, in_maps, core_ids=list(range(NCORES)))
    R = res.results
    cat = lambda name: np.stack([np.asarray(R[c][name]) for c in range(NCORES)], axis=0)
    yp = cat("yp")
    ys = cat("ys").reshape(128, 8, D)
    p_lc = cat("o_plc")[None]
    p_lh = cat("o_plh").reshape(1, 8, D)
    p_gc = cat("o_pgc")[None]
    p_gs = cat("o_pgs")[None]
    s_lc = cat("o_slc").reshape(1, 128, 3, D)
    s_lh = cat("o_slh").reshape(1, 128, D)
    s_gc = cat("o_sgc").reshape(1, 128, 3, 3072)
    s_gs = cat("o_sgs").reshape(1, 128, 8, 128, 128)
    return tuple(np.ascontiguousarray(a, dtype=np.float32) for a in (yp, ys, p_lc, p_lh, p_gc, p_gs, s_lc, s_lh, s_gc, s_gs))
```

```python
import math
import numpy as np
import concourse.bass as bass
import concourse.mybir as mybir
from concourse.bass_utils import run_bass_kernel_spmd
from contextlib import ExitStack

F32 = mybir.dt.float32
BF16 = mybir.dt.bfloat16
AF = mybir.ActivationFunctionType
ALU = mybir.AluOpType

NCORES = 8
D = 1024
KC = 8
NT = 17
TT = NT * 128
EPS = 1e-6
NV = 169
V_NPRE, V_LCW, V_LCB, V_LBA, V_LBX, V_LAL, V_GCW, V_GNW = 0, 8, 40, 48, 56, 64, 72, 168


class Buf:
    __slots__ = ("name", "w", "r", "dsem", "dcnt", "excl", "lastdma")

    def __init__(self, name):
        self.name = name
        self.w = None
        self.r = []
        self.dsem = None
        self.dcnt = 0
        self.excl = False
        self.lastdma = None


class Op:
    __slots__ = ("id", "eng", "fn", "deps", "signal", "dur", "tab", "dma", "lat", "epoch", "idx", "unit", "tag")


class Sched:
    ENG = ("pe", "act", "dve", "pool", "sp")
    BLEV = True
    BUCKET = 0.1

    def __init__(self, nc, stack):
        self.nc = nc
        self.stack = stack
        self.sem = {e: stack.enter_context(nc.semaphore("c_" + e)) for e in ("pe", "act", "dve", "pool")}
        self.allops = []
        self.epoch = 0
        self.final = {}
        self.dsems = []
        self.nb = 0

    def buf(self, name="b"):
        self.nb += 1
        return Buf("%s_%d" % (name, self.nb))

    def _record(self, eng, fn, reads, writes, signal, dur, tab, dma=None, lat=0.0):
        ex = [b for b in reads if b.excl]
        if ex:
            reads = [b for b in reads if not b.excl]
            writes = list(writes) + ex
        o = Op()
        o.id = len(self.allops)
        o.eng, o.fn, o.signal, o.dur, o.tab, o.dma, o.lat, o.epoch = eng, fn, signal, dur, tab, dma, lat, self.epoch
        o.tag = getattr(self, "tag", "")
        deps = set()
        for b in reads:
            if b.w is not None:
                deps.add(b.w)
        for b in writes:
            if b.w is not None:
                deps.add(b.w)
            deps.update(b.r)
        o.deps = deps
        for b in reads:
            b.r.append(o.id)
        for b in writes:
            b.w = o.id
            b.r = []
        self.allops.append(o)
        return o

    def op(self, eng, fn, reads=(), writes=(), signal=True, dur=0.3, tab=None):
        self._record(eng, fn, reads, writes, signal, dur, tab)

    def dsem_of(self, buf):
        if buf.dsem is None:
            buf.dsem = self.stack.enter_context(self.nc.semaphore("d_" + buf.name))
            self.dsems.append(buf)
        return buf.dsem

    def dma(self, out, in_, reads=(), writes=(), sembuf=None, eng="sp", is_output=False, nbytes=1 << 20, **kw):
        if sembuf is None:
            sembuf = writes[0] if writes else reads[0]
        self.dsem_of(sembuf)
        o = self._record(eng, lambda e, o=out, i=in_, k=kw: e.dma_start(out=o, in_=i, **k), reads, writes, True,
                         0.15 if eng != "pool" else 1.0, None, dma=(sembuf, is_output), lat=2.0 + nbytes / 2.0e5)
        if sembuf.lastdma is not None:
            o.deps.add(sembuf.lastdma)
        sembuf.lastdma = o.id

    def barrier(self):
        self.epoch += 1

    def _schedule_epoch(self, ops, order, t0):
        units = []
        cur = None
        for o in ops:
            if o.eng == "pe":
                if cur is None:
                    cur = [o]
                else:
                    cur.append(o)
                if o.signal:
                    units.append(cur)
                    cur = None
            else:
                units.append([o])
        assert cur is None, "PE op run without a final signalled op"
        uid = {}
        for ui, u in enumerate(units):
            for o in u:
                uid[o.id] = ui
        nu = len(units)
        first = ops[0].id if ops else 0
        preds = [set() for _ in range(nu)]
        succs = [[] for _ in range(nu)]
        for ui, u in enumerate(units):
            for o in u:
                for d in o.deps:
                    if d >= first and uid[d] != ui:
                        preds[ui].add(uid[d])
        for ui in range(nu):
            for p in preds[ui]:
                succs[p].append(ui)
        indeg = [len(p) for p in preds]
        ready_t = [t0] * nu
        udur = [sum(o.dur for o in u) for u in units]
        ueng = [u[0].eng for u in units]
        blev = [0.0] * nu
        for ui in range(nu - 1, -1, -1):
            m = 0.0
            for sidx in succs[ui]:
                if blev[sidx] > m:
                    m = blev[sidx]
            blev[ui] = udur[ui] + units[ui][-1].lat + 0.3 + m
        rdy = {e: [] for e in self.ENG}
        for ui in range(nu):
            if indeg[ui] == 0:
                rdy[ueng[ui]].append(ui)
        free = {e: t0 for e in self.ENG}
        curtab = {"act": None}
        done = 0
        tend = t0
        while done < nu:
            best = None
            for e in self.ENG:
                lst = rdy[e]
                if not lst:
                    continue
                f = free[e]
                cand = None
                for ui in lst:
                    st = max(f, ready_t[ui])
                    if e == "act":
                        tb = units[ui][0].tab
                        if tb is not None and curtab["act"] is not None and tb != curtab["act"]:
                            st += 1.3
                    key = (int(st / self.BUCKET), -blev[ui], ui) if self.BLEV else (st, ui)
                    if cand is None or key < cand[0]:
                        cand = (key, ui, st)
                if best is None or cand[0] < best[0]:
                    best = (cand[0], cand[1], cand[2], e)
            _, ui, st, e = best
            rdy[e].remove(ui)
            if e == "act" and units[ui][0].tab is not None:
                curtab["act"] = units[ui][0].tab
            fin = st + udur[ui]
            free[e] = fin
            lat = units[ui][-1].lat
            tend = max(tend, fin + lat)
            order[e].extend(units[ui])
            done += 1
            for sidx in succs[ui]:
                extra = 0.12 if ueng[sidx] == e else 0.3
                ready_t[sidx] = max(ready_t[sidx], fin + lat + extra)
                indeg[sidx] -= 1
                if indeg[sidx] == 0:
                    rdy[ueng[sidx]].append(sidx)
        return tend

    def emit(self):
        ops = self.allops
        order = {e: [] for e in self.ENG}
        bounds = []
        t = 0.0
        i = 0
        n = len(ops)
        while i < n:
            j = i
            while j < n and ops[j].epoch == ops[i].epoch:
                j += 1
            t = self._schedule_epoch(ops[i:j], order, t)
            bounds.append({e: len(order[e]) for e in self.ENG})
            i = j
        self.est_us = t
        cnt = {e: 0 for e in self.ENG}
        dmaval = {}
        for e in self.ENG:
            lst = order[e]
            pending = []
            for o in lst:
                if o.dma is not None:
                    sb_, is_out = o.dma
                    sb_.dcnt += 16
                    o.idx = (("d", sb_.name), sb_.dsem, sb_.dcnt)
                    if is_out:
                        self.final[sb_.name] = (sb_.dsem, sb_.dcnt)
                elif o.signal:
                    cnt[e] += 1
                    o.idx = (e, self.sem[e], cnt[e])
                    for p in pending:
                        p.idx = o.idx
                    pending = []
                else:
                    pending.append(o)
            assert not pending
        prog = {e: [] for e in self.ENG}
        waited = {e: {} for e in self.ENG}
        pos = {e: 0 for e in self.ENG}
        for bi, bd in enumerate(bounds):
            for e in self.ENG:
                for o in order[e][pos[e]:bd[e]]:
                    waits = []
                    for d in sorted(o.deps):
                        key, sem, val = ops[d].idx
                        if key == e and e == "pe":
                            continue
                        if waited[e].get(key, 0) >= val:
                            continue
                        waited[e][key] = val
                        waits.append((sem, val))
                    inc = None
                    if o.dma is not None:
                        inc = (o.idx[1], 16)
                    elif o.signal:
                        inc = (self.sem[e], 1)
                    prog[e].append((waits, o.fn, inc))
                pos[e] = bd[e]
            if bi < len(bounds) - 1:
                for e in self.ENG:
                    waits = []
                    for f in ("pe", "act", "dve", "pool"):
                        c = 0
                        for o in order[f][:bd[f]]:
                            if o.dma is None and o.signal:
                                c = max(c, o.idx[2])
                        if f != e and c > 0 and waited[e].get(f, 0) < c:
                            waited[e][f] = c
                            waits.append((self.sem[f], c))
                    for q in ("sp", "pool"):
                        last = {}
                        for o in order[q][:bd[q]]:
                            if o.dma is not None:
                                last[o.idx[0]] = o.idx
                        for key, (k_, sem, val) in last.items():
                            if waited[e].get(key, 0) < val:
                                waited[e][key] = val
                                waits.append((sem, val))
                    if waits:
                        prog[e].append((waits, None, None))
        fw = list(self.final.values())
        if fw:
            prog["sp"].append((fw, None, None))
        self.prog = prog
        with self.nc.Block() as block:
            def mk(name):
                lst = prog[name]

                def body(eng):
                    for waits, fn, inc in lst:
                        for s_, v in waits:
                            eng.wait_ge(s_, v)
                        if fn is not None:
                            ins = fn(eng)
                            if inc is not None:
                                ins.then_inc(*inc)
                return body
            block.tensor(mk("pe"))
            block.scalar(mk("act"))
            block.vector(mk("dve"))
            block.gpsimd(mk("pool"))
            block.sync(mk("sp"))


class PV:
    def __init__(self, t, off):
        self.t, self.off = t, off

    def __getitem__(self, key):
        if isinstance(key, slice):
            return self.t[:, self.off:self.off + 512]
        p, c = key
        return self.t[p, c.start + self.off:c.stop + self.off]


def _fsz(ap):
    n = 1
    for d in ap.shape[1:]:
        n *= int(d)
    return n


class K:
    STOP = 99
    NORM_ENG = "dve"
    S16_DVE = True
    LSLOTS = 4
    DBL256 = False
    PSR = [(0, 1, 2), (4, 5, 6), (3, 7)]
    OT_ALIAS = True
    PSPLIT = True
    GOFF = 0
    SKIPG = False
    GLIM = (8, 5, 9)

    def __init__(self):
        self.nc = bass.Bass("TRN2", target_bir_lowering=False)
        self.st = ExitStack()

    def sb(self, name, shape, dt=F32):
        self.nsb = getattr(self, "nsb", 0) + 1
        return self.cur.enter_context(self.nc.sbuf_tensor("s%d_%s" % (self.nsb, name), shape, dt))

    def phase(self, fn):
        with ExitStack() as ph:
            old, self.cur = self.cur, ph
            fn()
            self.S.barrier()
            self.cur = old

    def din(self, name, shape):
        return self.nc.dram_tensor(name, shape, F32, kind="ExternalInput").ap()

    def dout(self, name, shape):
        return self.nc.dram_tensor(name, shape, F32, kind="ExternalOutput").ap()

    def _banks(self):
        r = self.psrange
        if isinstance(r, dict):
            return r.get(self.S.tag, r["*"])
        return r

    def ps(self):
        banks = self._banks()
        k = self.psc.get(banks, 0)
        i = banks[k % len(banks)]
        self.psc[banks] = k + 1
        return PV(self.pst[i // 2], (i % 2) * 512), self.psb[i]

    def ps2(self):
        banks = self._banks()
        if len(banks) == 8:
            k = self.psc.get("pair", 0)
            self.psc["pair"] = k + 1
            i = 2 * (k % 4)
        else:
            i = banks[0]
            assert i % 2 == 0 and banks[1] == i + 1
        return self.pst[i // 2], [self.psb[i], self.psb[i + 1]]

    def mm(self, out, lhsT, rhs, start, stop, reads, writes, signal=None, sgc=False):
        if signal is None:
            signal = stop
        d = 0.064 + _fsz(rhs) / 2400.0
        if lhsT.dtype == F32:
            d *= 4
        if sgc:
            fn = lambda e, o=out, l=lhsT, r=rhs, a=start, b=stop: e.matmul(o, lhsT=l, rhs=r, start=a, stop=b, skip_group_check=True)
        else:
            fn = lambda e, o=out, l=lhsT, r=rhs, a=start, b=stop: e.matmul(o, lhsT=l, rhs=r, start=a, stop=b)
        self.S.op("pe", fn, reads=reads, writes=writes, signal=signal, dur=d)

    def tr(self, out, in_, ident, reads, writes, signal=True):
        self.S.op("pe", lambda e, o=out, i=in_, d=ident: e.transpose(out=o, in_=i, identity=d),
                  reads=reads, writes=writes, signal=signal, dur=0.12)

    def act(self, out, in_, func, reads, writes, scale=1.0, bias=0.0, accum_out=None):
        def fn(e, o=out, i=in_, f=func, s=scale, b=bias, a=accum_out):
            kw = {}
            if a is not None:
                kw["accum_out"] = a
            return e.activation(out=o, in_=i, func=f, bias=b, scale=s, **kw)
        tab = {AF.Tanh: 0, AF.Ln: 6, AF.Sigmoid: 2, AF.Sqrt: 3}.get(func)
        self.S.op("act", fn, reads=reads, writes=writes, dur=0.25 + _fsz(out) / 1200.0, tab=tab)

    def tt(self, eng, out, in0, in1, op, reads, writes):
        self.S.op(eng, lambda e, o=out, a=in0, b=in1, p=op: e.tensor_tensor(out=o, in0=a, in1=b, op=p),
                  reads=reads, writes=writes, dur=self.edur(eng, out))

    def ts(self, eng, out, in0, s1, s2, op0, op1, reads, writes):
        if s2 is None:
            self.S.op(eng, lambda e, o=out, a=in0, x=s1, p=op0: e.tensor_scalar(out=o, in0=a, scalar1=x, scalar2=None, op0=p),
                      reads=reads, writes=writes, dur=self.edur(eng, out))
        else:
            self.S.op(eng, lambda e, o=out, a=in0, x=s1, y=s2, p=op0, q=op1:
                      e.tensor_scalar(out=o, in0=a, scalar1=x, scalar2=y, op0=p, op1=q), reads=reads, writes=writes,
                      dur=self.edur(eng, out))

    def stt(self, out, in0, scalar, in1, op0, op1, reads, writes):
        self.S.op("dve", lambda e, o=out, a=in0, s=scalar, b=in1, p=op0, q=op1:
                  e.scalar_tensor_tensor(out=o, in0=a, scalar=s, in1=b, op0=p, op1=q), reads=reads, writes=writes,
                  dur=self.edur("dve", out))

    def cp(self, eng, out, in_, reads, writes):
        if eng == "act":
            self.S.op("act", lambda e, o=out, i=in_: e.copy(out=o, in_=i), reads=reads, writes=writes,
                      dur=0.25 + _fsz(out) / 1200.0)
        else:
            self.S.op(eng, lambda e, o=out, i=in_: e.tensor_copy(out=o, in_=i), reads=reads, writes=writes,
                      dur=self.edur(eng, out))

    def edur(self, eng, out):
        n = _fsz(out)
        return (0.12 + n / 960.0) if eng == "dve" else (0.2 + n / 500.0)

    def memset(self, eng, ap, val, writes):
        self.S.op(eng, lambda e, a=ap, v=val: e.memset(a, v), writes=writes, dur=self.edur(eng, ap))

    def asel(self, out, in_, pattern, op, base, cm, reads, writes):
        self.S.op("pool", lambda e, o=out, i=in_, p=pattern, c=op, b=base, m=cm:
                  e.affine_select(out=o, in_=i, pattern=p, compare_op=c, fill=0.0, base=b, channel_multiplier=m),
                  reads=reads, writes=writes)

    def rstd(self, col, scale, reads_writes, eps=EPS):
        self.act(col, col, AF.Ln, reads_writes, reads_writes, scale=scale, bias=eps)
        self.act(col, col, AF.Exp, reads_writes, reads_writes, scale=-0.5)

    def build(self):
        nc, st = self.nc, self.st
        with st:
            self.S = S = Sched(nc, st)
            self._io()
            self.pst = [st.enter_context(nc.psum_tensor("ps%d" % i, [128, 1024], F32)) for i in range(4)]
            self.psb = [S.buf("ps") for _ in range(8)]
            for b in self.psb:
                b.excl = True
            self.psrange = tuple(range(8))
            self.psc = {}
            self.cur = st
            stop = self.STOP
            self._consts()
            if stop >= 1:
                self.phase(lambda: (self._phase_a(), self._gdn_scalars()))
            if stop >= 3 and not self.SKIPG:
                self.phase(self._phase_gdn)
            self.lru_out = self.sb("lru_out", [128, KC, TT], BF16)
            if stop >= 4:
                self.phase(self._phase_lru)
            self.merged = self.sb("merged", [128, KC, TT], BF16)
            self.mgb = S.buf("merged")
            if stop >= 6:
                self.phase(self._phase_merge)
            if stop >= 7:
                self.phase(self._phase_out)
            S.emit()
        return nc

    def _io(self):
        i, o = self.din, self.dout
        self.xp, self.xs = i("xp", [2048, D]), i("xs", [128, D])
        self.slc, self.slh = i("slc", [48, D]), i("slh", [16, D])
        self.sgc, self.sgS = i("sgc", [48, 3072]), i("sgS", [16, 8, 128, 128])
        self.vecs_d, self.npost_d = i("vecs", [128, NV]), i("npost", [1, D])
        self.alog_d, self.dtb_d = i("alog", [1, 8]), i("dtb", [1, 8])
        self.winr, self.wba_d = i("winr", [64, 128, KC, 128]), i("wba", [128, KC, 16])
        self.wa_d, self.wx_d = i("wa", [128, 8, 128]), i("wx", [128, 8, 128])
        self.wbl_d, self.wbg_d = i("wbl", [8, 128, KC, 128]), i("wbg", [8, 128, KC, 128])
        self.wout_d = i("wout", [128, KC, D])
        self.yp, self.ys = o("yp", [2048, D]), o("ys", [128, D])
        self.o_plc, self.o_plh = o("o_plc", [3, D]), o("o_plh", [1, D])
        self.o_pgc, self.o_pgs = o("o_pgc", [3, 3072]), o("o_pgs", [8, 128, 128])
        self.o_slc, self.o_slh = o("o_slc", [48, D]), o("o_slh", [16, D])
        self.o_sgc, self.o_sgs = o("o_sgc", [48, 3072]), o("o_sgs", [16, 8, 128, 128])

    def xsrc(self, i):
        return self.xp[i * 128:(i + 1) * 128, :] if i < 16 else self.xs

    def ydst(self, i):
        return self.yp[i * 128:(i + 1) * 128, :] if i < 16 else self.ys

    def _consts(self):
        S, sb = self.S, self.sb
        cb = self.cb = S.buf("const")
        self.vecs = sb("vecs", [128, NV])
        S.dma(self.vecs[:], self.vecs_d, writes=[cb])
        self.alog = sb("alog", [128, 8])
        self.dtb = sb("dtbs", [128, 8])
        S.dma(self.alog[:], self.alog_d.broadcast_to([128, 8]), writes=[cb])
        S.dma(self.dtb[:], self.dtb_d.broadcast_to([128, 8]), writes=[cb])
        self.idf = sb("idf", [128, 128])
        self.idb = sb("idb", [128, 128], BF16)
        self.memset("pool", self.idf[:], 1.0, [cb])
        self.asel(self.idf[:], self.idf[:], [[-1, 128]], ALU.is_equal, 0, 1, [cb], [cb])
        self.cp("pool", self.idb[:], self.idf[:], [cb], [cb])
        self.onesb = sb("onesb", [128, 128], BF16)
        self.memset("pool", self.onesb[:], 1.0, [cb])
        self.onesf = sb("onesf", [128, 128])
        self.memset("pool", self.onesf[:], 1.0, [cb])
        self.uincl = sb("uincl", [128, 128])
        self.ustr = sb("ustr", [128, 128])
        self.lstr = sb("lstr", [128, 128])
        for t, pat, op, cm in ((self.uincl, [[1, 128]], ALU.is_ge, -1), (self.ustr, [[1, 128]], ALU.is_gt, -1),
                               (self.lstr, [[-1, 128]], ALU.is_gt, 1)):
            self.memset("pool", t[:], 1.0, [cb])
            self.asel(t[:], t[:], pat, op, 0, cm, [cb], [cb])
        self.bd64 = sb("bd64", [128, 128], BF16)
        self.offur = sb("offur", [128, 128], BF16)
        self.memset("pool", self.bd64[:], 0.0, [cb])
        self.memset("pool", self.bd64[0:64, 0:64], 1.0, [cb])
        self.memset("pool", self.bd64[64:128, 64:128], 1.0, [cb])
        self.memset("pool", self.offur[:], 0.0, [cb])
        self.memset("pool", self.offur[0:64, 64:128], 1.0, [cb])
        self.usbd = sb("usbd", [128, 128])
        self.offf = sb("offf", [128, 128])
        self.cp("pool", self.offf[:], self.offur[:], [cb], [cb])
        self.cp("pool", self.usbd[:], self.bd64[:], [cb], [cb])
        self.tt("pool", self.usbd[:], self.usbd[:], self.ustr[:], ALU.mult, [cb], [cb])
        self.selT = sb("selT", [16, 128])
        self.memset("pool", self.selT[:], 1.0, [cb])
        self.asel(self.selT[:], self.selT[:], [[1, 128]], ALU.is_ge, 0, -8, [cb], [cb])
        self.asel(self.selT[:], self.selT[:], [[-1, 128]], ALU.is_ge, 7, 8, [cb], [cb])
        self.selc = sb("selc", [128, 16])
        self.memset("pool", self.selc[:], 1.0, [cb])
        self.asel(self.selc[:], self.selc[:], [[-8, 16]], ALU.is_ge, 0, 1, [cb], [cb])
        self.asel(self.selc[:], self.selc[:], [[8, 16]], ALU.is_ge, 7, -1, [cb], [cb])
        pt, pb = self.ps()
        self.mm(pt[:, 0:128], self.selT[:], self.selT[:], True, True, [cb], [pb])
        self.blk = sb("blk", [128, 128])
        self.cp("dve", self.blk[:], pt[:, 0:128], [pb], [cb])
        self.uincl_b, self.ustr_b, self.lstr_b = sb("uincl_b", [128, 128]), sb("ustr_b", [128, 128]), sb("lstr_b", [128, 128])
        for a, b in ((self.uincl_b, self.uincl), (self.ustr_b, self.ustr), (self.lstr_b, self.lstr)):
            self.tt("pool", a[:], b[:], self.blk[:], ALU.mult, [cb], [cb])
        self.clru = sb("clru", [128, 8])
        self.act(self.clru[:], self.vecs[:, V_LAL:V_LAL + 8], AF.Exp, [cb], [cb], scale=-1.0)
        self.act(self.clru[:], self.clru[:], AF.Ln, [cb], [cb], bias=1.0)
        self.ts("dve", self.clru[:], self.clru[:], -8.0, None, ALU.mult, None, [cb], [cb])
        self.gnwh = sb("gnwh", [128, 1])
        self.ts("dve", self.gnwh[:], self.vecs[:, V_GNW:V_GNW + 1], 0.5, None, ALU.mult, None, [cb], [cb])
        self.hb = sb("hb", [128, 16])
        self.ts("dve", self.hb[:], self.vecs[:, V_LBA:V_LBA + 16], 0.5, None, ALU.mult, None, [cb], [cb])
        self.hc = sb("hc", [128, 8])
        self.ts("dve", self.hc[:], self.clru[:], 0.5, None, ALU.mult, None, [cb], [cb])
        self.negA = sb("negA", [128, 8])
        self.act(self.negA[:], self.alog[:], AF.Exp, [cb], [cb])
        self.ts("dve", self.negA[:], self.negA[:], -1.0, None, ALU.mult, None, [cb], [cb])
        self.lcT = sb("lcT", [128, 8, 48])
        self.h0T = sb("h0T", [128, 8, 16])
        self.gcT = sb("gcT", [128, 24, 48])
        self.xn = sb("xn", [128, KC, TT], BF16)
        self.xnb = [S.buf("xn") for _ in range(NT)]
        self.gdn_out = sb("gdn_out", [128, KC, TT], BF16)
        for nm_, shp in (("betah", [128, NT, 8]), ("beta", [128, NT, 8]), ("nbeta", [128, NT, 8]), ("gtok", [128, NT, 8]), ("egc", [128, NT, 8]),
                         ("kdc", [128, NT, 8]), ("eglb", [128, 16, 8]), ("eglS", [128, 16, 8])):
            setattr(self, nm_, sb(nm_, shp))
        self.stg_lc = sb("stg_lc", [128, 8, 51])
        self.stg_lh = sb("stg_lh", [128, 8, 17])
        self.stg_gc = sb("stg_gc", [128, 24, 51])

    def _load_states(self):
        S, sb = self.S, self.sb
        cb = self.cb
        tmp = sb("st_tmp", [48, 3072 + D])
        tmp2 = sb("st_tmp2", [16, D])
        tb = S.buf("sttmp")
        S.dma(tmp[:, 0:D], self.slc, writes=[tb])
        S.dma(tmp[:, D:D + 3072], self.sgc, writes=[tb])
        S.dma(tmp2[:], self.slh, writes=[tb])
        for g in range(32):
            pt, pb = self.ps()
            self.tr(pt[:, 0:48], tmp[:, g * 128:(g + 1) * 128], self.idf[0:48, 0:48], [tb, cb], [pb])
            dst = self.lcT[:, g, :] if g < 8 else self.gcT[:, g - 8, :]
            self.cp("dve", dst, pt[:, 0:48], [pb], [cb])
        for g in range(8):
            pt, pb = self.ps()
            self.tr(pt[:, 0:16], tmp2[:, g * 128:(g + 1) * 128], self.idf[0:16, 0:16], [tb, cb], [pb])
            self.cp("dve", self.h0T[:, g, :], pt[:, 0:16], [pb], [cb])

    def _phase_a(self):
        S, sb = self.S, self.sb
        cb = self.cb
        self._load_states()
        NB = 4
        xt = [sb("xt%d" % i, [128, D]) for i in range(NB)]
        xtb = [S.buf("xt") for _ in range(NB)]
        x16 = [sb("x16_%d" % i, [128, D], BF16) for i in range(NB)]
        x16b = [S.buf("x16") for _ in range(NB)]
        junk = sb("junk", [128, D], BF16)
        jb = S.buf("junk")
        ss = sb("ssA", [128, NT])
        ssb = [S.buf("ss") for _ in range(NT)]
        for i in range(NT):
            s = i % NB
            S.dma(xt[s][:], self.xsrc(i), writes=[xtb[s]])
            self.act(junk[:], xt[s][:], AF.Square, [xtb[s]], [jb, ssb[i]], accum_out=ss[:, i:i + 1])
            self.rstd(ss[:, i:i + 1], 1.0 / D, [ssb[i]])
            self.ts("dve", x16[s][:], xt[s][:], ss[:, i:i + 1], None, ALU.mult, None, [xtb[s], ssb[i]], [x16b[s]])
            pt, pb = self.ps()
            pv = pt[:].bitcast(BF16)
            for k in range(KC):
                self.tr(pv[:, k * 128:(k + 1) * 128], x16[s][:, k * 128:(k + 1) * 128], self.idb[:], [x16b[s], cb], [pb],
                        signal=(k == KC - 1))
            self.tt("dve", self.xn[:, :, i * 128:(i + 1) * 128], pv.rearrange("p (k t) -> p k t", t=128),
                    self.vecs[:, V_NPRE:V_NPRE + 8].unsqueeze(2).broadcast_to([128, KC, 128]), ALU.mult,
                    [pb, cb], [self.xnb[i]])

    def _gdn_scalars(self):
        S, sb = self.S, self.sb
        cb = self.cb
        gb = self.gsb = S.buf("gsc")
        wba = sb("wba16", [128, KC, 16], BF16)
        S.dma(wba[:], self.wba_d, writes=[gb], eng="pool")
        zba = sb("zba", [128, NT, 16])
        pt, pb = self.ps()
        for i in range(NT):
            for k in range(KC):
                self.mm(pt[:, i * 16:(i + 1) * 16], self.xn[:, k, i * 128:(i + 1) * 128], wba[:, k, :], k == 0, k == KC - 1,
                        [self.xnb[i], gb], [pb])
        self.cp("dve", zba[:].rearrange("p t c -> p (t c)"), pt[:, 0:NT * 16], [pb], [gb])
        self.act(self.beta[:], zba[:, :, 0:8], AF.Sigmoid, [gb], [gb])
        self.ts("dve", self.nbeta[:], self.beta[:], -1.0, None, ALU.mult, None, [gb], [gb])
        self.ts("dve", self.betah[:], self.beta[:], 0.5, None, ALU.mult, None, [gb], [gb])
        tmp = sb("gs_tmp", [128, NT, 8])
        self.tt("dve", tmp[:], zba[:, :, 8:16], self.dtb[:].unsqueeze(1).broadcast_to([128, NT, 8]), ALU.add, [gb, cb], [gb])
        self.act(tmp[:], tmp[:], AF.Exp, [gb], [gb])
        self.act(tmp[:], tmp[:], AF.Ln, [gb], [gb], bias=1.0)
        self.tt("dve", self.gtok[:], tmp[:], self.negA[:].unsqueeze(1).broadcast_to([128, NT, 8]), ALU.mult, [gb, cb], [gb])
        g2 = self.gtok[:].rearrange("p t h -> p (t h)")
        pt, pb = self.ps()
        self.mm(pt[:, 0:128], self.uincl[:], g2[:, 0:128], True, True, [gb, cb], [pb])
        self.mm(pt[:, 128:136], self.uincl_b[:], g2[:, 128:136], True, True, [gb, cb], [pb])
        self.act(self.egc[:].rearrange("p t h -> p (t h)"), pt[:, 0:136], AF.Exp, [pb], [gb])
        pt, pb = self.ps()
        self.mm(pt[:, 0:128], self.lstr[:], g2[:, 0:128], True, True, [gb, cb], [pb])
        self.mm(pt[:, 128:136], self.lstr_b[:], g2[:, 128:136], True, True, [gb, cb], [pb])
        self.act(self.kdc[:].rearrange("p t h -> p (t h)"), pt[:, 0:136], AF.Exp, [pb], [gb])
        pt, pb = self.ps()
        self.mm(pt[:, 0:128], self.onesf[:], g2[:, 0:128], True, True, [gb, cb], [pb])
        self.act(self.eglb[:].rearrange("p t h -> p (t h)"), pt[:, 0:128], AF.Exp, [pb], [gb])
        gsel = sb("gsel", [128, 16, 8])
        self.tt("dve", gsel[:], self.gtok[:, 16, :].unsqueeze(1).broadcast_to([128, 16, 8]),
                self.selc[:].unsqueeze(2).broadcast_to([128, 16, 8]), ALU.mult, [gb, cb], [gb])
        pt, pb = self.ps()
        self.mm(pt[:, 0:128], self.onesf[:], gsel[:].rearrange("p s h -> p (s h)"), True, True, [gb, cb], [pb])
        self.act(self.eglS[:].rearrange("p s h -> p (s h)"), pt[:, 0:128], AF.Exp, [pb], [gb])
        self.negegc = None

    def segs(self):
        return [(0, 512, 4), (512, 1024, 4), (1024, 1536, 4), (1536, 2048, 4), (2048, 2176, 1)]

    def proj(self, w, wb, c0, n):
        pt, pb = self.ps()
        tiles = range(c0 // 128, (c0 + n) // 128)
        rd = [self.xnb[t] for t in tiles] + [wb]
        for k in range(KC):
            self.mm(pt[:, 0:n], w[:, k, :], self.xn[:, k, c0:c0 + n], k == 0, k == KC - 1, rd, [pb])
        return pt, pb

    def conv(self, pt, pb, zp, zpb, zprev, zprevb, si, n, wcol, bcol, stateT, out, outb, stg):
        cb = self.cb
        if si < 4:
            self.cp("act", zp[:, 3:3 + n], pt[:, 0:n], [pb], [zpb])
            if si == 0:
                self.memset("pool", zp[:, 0:3], 0.0, [zpb])
            else:
                self.cp("pool", zp[:, 0:3], zprev[:, 512:515], [zprevb], [zpb])
            if si == 3:
                self.cp("pool", stg[:, 0:3], zp[:, 512:515], [zpb], [])
            src = [zp[:, i:i + n] for i in range(4)]
            dst = out[:, 0:n]
        else:
            z3 = zp[:, 0:176].rearrange("p (s t) -> p s t", t=11)
            self.cp("act", z3[:, :, 3:11], pt[:, 0:128].rearrange("p (s t) -> p s t", t=8), [pb], [zpb])
            self.cp("pool", z3[:, :, 0:3], stateT.rearrange("p (s i) -> p s i", i=3), [cb], [zpb])
            self.cp("pool", stg[:, 3:51].rearrange("p (s i) -> p s i", i=3), z3[:, :, 8:11], [zpb], [])
            src = [z3[:, :, i:i + 8] for i in range(4)]
            dst = out[:, 0:128].rearrange("p (s t) -> p s t", t=8)
        if bcol is None:
            self.ts("dve", dst, src[0], wcol(0), None, ALU.mult, None, [zpb, cb], [outb])
        else:
            self.ts("dve", dst, src[0], wcol(0), bcol, ALU.mult, ALU.add, [zpb, cb], [outb])
        for i in range(1, 4):
            self.stt(dst, src[i], wcol(i), dst, ALU.mult, ALU.add, [zpb, cb, outb], [outb])

    def _phase_lru(self):
        S, sb = self.S, self.sb
        cb = self.cb
        V = self.vecs
        wab = S.buf("wa")
        wa16, wx16 = sb("wa32", [128, 8, 128]), sb("wx32", [128, 8, 128])
        S.dma(wa16[:], self.wa_d, writes=[wab])
        S.dma(wx16[:], self.wx_d, writes=[wab])
        wxs = [sb("lwx%d" % i, [128, KC, 128], BF16) for i in range(2)]
        wgs = [sb("lwg%d" % i, [128, KC, 128], BF16) for i in range(2)]
        wxb = [S.buf("lwx") for _ in range(2)]
        wgb = [S.buf("lwg") for _ in range(2)]
        lzc = [sb("lzc%d" % i, [128, 4]) for i in range(2)]
        lzcb = [S.buf("lzc") for _ in range(2)]
        nm = ["xc", "r", "ig", "a", "bb", "h", "sg"]
        W = [{n: sb("l%s%d" % (n, i), [128, 520] if n == "zp" else [128, 512], BF16 if n == "xc16" else F32) for n in nm}
             for i in range(self.LSLOTS)]
        WB = [{n: S.buf("l" + n) for n in nm} for _ in range(self.LSLOTS)]
        cnt = 0
        for g in range(8):
            s = g % 2
            S.dma(wxs[s][:], self.winr[g], writes=[wxb[s]], eng="pool")
            S.dma(wgs[s][:], self.winr[8 + g], writes=[wgb[s]], eng="pool")
            prev = None
            for si, (c0, c1, nch) in enumerate(self.segs()):
                n = c1 - c0
                w, wb_ = W[cnt % self.LSLOTS], WB[cnt % self.LSLOTS]
                pw, pwb = W[(cnt - 1) % self.LSLOTS], WB[(cnt - 1) % self.LSLOTS]
                cnt += 1
                pt, pb = self.proj(wxs[s], wxb[s], c0, n)
                self.conv3(pt, pb, lzc[g % 2], lzcb[g % 2], si, n,
                           lambda i: V[:, V_LCW + i * 8 + g:V_LCW + i * 8 + g + 1],
                           self.lcT[:, g, :], w["xc"], wb_["xc"], self.stg_lc[:, g, :], bcol=V[:, V_LCB + g:V_LCB + g + 1])
                pa, pab = self.ps()
                self.mm(pa[:, 0:n], wa16[:, g, :], w["xc"][:, 0:n], True, True, [wab, wb_["xc"]], [pab])
                px, pxb = self.ps()
                self.mm(px[:, 0:n], wx16[:, g, :], w["xc"][:, 0:n], True, True, [wab, wb_["xc"]], [pxb])
                self.act(w["r"][:, 0:n], pa[:, 0:n], AF.Tanh, [pab, cb], [wb_["r"]], scale=0.5, bias=self.hb[:, g:g + 1])
                self.act(w["ig"][:, 0:n], px[:, 0:n], AF.Tanh, [pxb, cb], [wb_["ig"]], scale=0.5, bias=self.hb[:, 8 + g:9 + g])
                self.act(w["a"][:, 0:n], w["r"][:, 0:n], AF.Exp, [wb_["r"], cb], [wb_["a"]], scale=self.hc[:, g:g + 1],
                         bias=self.hc[:, g:g + 1])
                self.tt("pool", w["r"][:, 0:n], w["a"][:, 0:n], w["a"][:, 0:n], ALU.mult, [wb_["a"]], [wb_["r"]])
                self.act(w["r"][:, 0:n], w["r"][:, 0:n], AF.Sqrt, [wb_["r"]], [wb_["r"]], scale=-1.0, bias=1.0)
                self.ts("pool", w["ig"][:, 0:n], w["ig"][:, 0:n], 0.5, 0.5, ALU.mult, ALU.add, [wb_["ig"]], [wb_["ig"]])
                self.tt("pool", w["bb"][:, 0:n], w["ig"][:, 0:n], w["xc"][:, 0:n], ALU.mult, [wb_["ig"], wb_["xc"]], [wb_["bb"]])
                if si == 0:
                    self.memset("pool", w["r"][:, 0:1], 1.0, [wb_["r"]])
                    self.memset("pool", w["a"][:, 0:1], 0.0, [wb_["a"]])
                self.tt("pool", w["bb"][:, 0:n], w["bb"][:, 0:n], w["r"][:, 0:n], ALU.mult, [wb_["bb"], wb_["r"]], [wb_["bb"]])
                if si == 4:
                    a3 = w["a"][:, 0:128].rearrange("p (s t) -> p s t", t=8)
                    b3 = w["bb"][:, 0:128].rearrange("p (s t) -> p s t", t=8)
                    t3 = w["r"][:, 0:16].unsqueeze(2)
                    self.tt("pool", t3, a3[:, :, 0:1], self.h0T[:, g, :].unsqueeze(2), ALU.mult, [wb_["a"], cb], [wb_["r"]])
                    self.tt("pool", b3[:, :, 0:1], b3[:, :, 0:1], t3, ALU.add, [wb_["bb"], wb_["r"]], [wb_["bb"]])
                    self.memset("pool", a3[:, :, 0:1], 0.0, [wb_["a"]])
                init = 0.0 if si in (0, 4) else prev
                rds = [wb_["a"], wb_["bb"]] + ([prevb] if si in (1, 2, 3) else [])
                S.op("dve", lambda e, o=w["h"][:, 0:n], a=w["a"][:, 0:n], b=w["bb"][:, 0:n], i0=init:
                     e.tensor_tensor_scan(out=o, data0=a, data1=b, initial=i0, op0=ALU.mult, op1=ALU.add),
                     reads=rds, writes=[wb_["h"]], dur=0.12 + 2 * n / 960.0)
                prev, prevb = w["h"][:, n - 1:n], wb_["h"]
                if si == 3:
                    self.cp("pool", self.stg_lh[:, g, 0:1], w["h"][:, 511:512], [wb_["h"]], [])
                if si == 4:
                    self.cp("pool", self.stg_lh[:, g, 1:17].unsqueeze(2),
                            w["h"][:, 0:128].rearrange("p (s t) -> p s t", t=8)[:, :, 7:8], [wb_["h"]], [])
                pg, pgb = self.proj(wgs[s], wgb[s], c0, n)
                self.act(w["sg"][:, 0:n], pg[:, 0:n], AF.Tanh, [pgb], [wb_["sg"]], scale=0.5)
                self.stt(w["sg"][:, 0:n], w["sg"][:, 0:n], 1.0, pg[:, 0:n], ALU.add, ALU.mult, [wb_["sg"], pgb], [wb_["sg"]])
                self.stt(self.lru_out[:, g, c0:c1], w["sg"][:, 0:n], 0.5, w["h"][:, 0:n], ALU.mult, ALU.mult,
                         [wb_["h"], wb_["sg"]], [])

    def _phase_gdn(self):
        S, sb = self.S, self.sb
        NS = 2
        self.g_wq = [[sb("gw%d_%d" % (j, i), [128, KC, 128], BF16) for j in range(4)] for i in range(NS)]
        self.g_wqb = [[S.buf("gw") for j in range(4)] for i in range(NS)]

        def ws(i, nc_):
            nn = nc_ * 128
            d = {}
            d["zc"] = [sb("gzc%d_%d" % (j, i), [128, 4]) for j in range(3)]
            d["cv"] = sb("gcv%d" % i, [128, nn])
            d["cv2"] = sb("gcv2_%d" % i, [128, nn])
            d["sq"] = sb("gsq%d" % i, [128, nn], BF16)
            d["rt"] = sb("grt%d" % i, [128, nn])
            d["th"] = sb("gth%d" % i, [128, nn])
            d["kq"] = sb("gkq%d" % i, [128, nc_, 2, 128], BF16)
            d["vf"] = sb("gvf%d" % i, [128, nn], BF16)
            d["sgate"] = sb("gsg%d" % i, [128, nn], BF16)
            d["ktv"] = sb("gktv%d" % i, [128, 2, nc_, 128], BF16)
            d["gA"] = sb("ggA%d" % i, [128, nc_, 128])
            d["gB"] = sb("ggB%d" % i, [128, nc_, 128])
            if self.OT_ALIAS or nc_ == 1:
                d["ot"] = d["cv2"][:].rearrange("p (c t) -> p c t", t=128)
            else:
                d["ot"] = sb("got%d" % i, [128, nc_, 128])
            d["on"] = sb("gon%d" % i, [128, nc_, 128], BF16)
            d["ss4"] = sb("gss4_%d" % i, [128, 4])
            d["kg"] = sb("gkg%d" % i, [128, nc_, 128], BF16)
            d["kdec"] = sb("gkd%d" % i, [128, nc_, 128], BF16)
            d["aqk"] = sb("gaq%d" % i, [128, nc_, 128], BF16)
            d["py"] = [sb("gpy%d_%d" % (j, i), [128, nc_, 2, 128], BF16) for j in range(2)]
            d["pt"] = [sb("gpt%d_%d" % (j, i), [128, nc_, 128], BF16) for j in range(2)]
            d["ub"] = sb("gub%d" % i, [128, nc_, 128])
            d["wT"] = sb("gwT%d" % i, [128, nc_, 128], BF16)
            if nc_ == 4:
                d["s32"] = sb("s32_%d" % i, [128, 128])
                d["s16"] = sb("s16_%d" % i, [128, 128], BF16)
            d["qs"] = sb("gqs%d" % i, [128, 128])
            d["vn"] = sb("gvn%d" % i, [128, 128], BF16)
            return d
        self.g_W = [ws(i, 4) for i in range(NS)] + [ws(2, 1)]
        keys = ["zc0", "zc1", "zc2", "cv", "cv2", "sq", "rt", "th", "kq", "vf", "sgate", "ktv", "gA", "gB", "ot", "on", "ss4", "kg",
                "kdec", "aqk", "ub", "wT", "s32", "s16", "qs", "vn"] + \
               ["%s%d_%d" % (a_, j, p) for a_ in ("pyP", "pyY", "pt") for j in range(2) for p in range(2)]
        self.g_WB = [{k: S.buf("g" + k) for k in keys} for _ in range(NS + 1)]
        for di_, d_ in enumerate(self.g_WB):
            if self.OT_ALIAS or di_ == NS:
                d_["cv2"] = d_["ot"]
        self.g_s0 = sb("s0", [128, 16, 128])
        self.g_s0b = S.buf("s0")
        self.g_snew = [sb("snew%d" % i, [128, 4, 128]) for i in range(2)]
        self.g_snewb = [S.buf("snew") for _ in range(2)]
        self.g_kdm = sb("kdm", [128, 16, 128], BF16)
        self.g_wq32 = sb("wq32", [128, 16, 2, 8])
        self.g_wsq = sb("wsq", [128, 2, 128])
        self.g_mb = S.buf("masked")
        gl_h = self.GLIM[0]
        psr = self.PSR
        for h0 in range(0, gl_h, NS):
            hs = list(range(h0, min(h0 + NS, gl_h)))
            gens = [self.gdn_stream(h, h - h0, h - h0, [0, 1, 2, 3]) for h in hs] + [self.gdn_samples(hs, h0)]
            alive = [True] * len(gens)
            while any(alive):
                for gi, g in enumerate(gens):
                    if not alive[gi]:
                        continue
                    self.psrange = psr[gi] if gi < len(hs) else psr[2]
                    try:
                        next(g)
                    except StopIteration:
                        alive[gi] = False
                    self.psrange = tuple(range(8))

    def gdn_samples(self, hs, h0):
        for h in hs:
            yield from self.gdn_stream(h, h - h0, 2, [4])

    def gdn_stream(self, h, wslot, slot, seglist):
        S = self.S
        cb, gb = self.cb, self.gsb
        V = self.vecs
        w, wb_ = self.g_W[slot], self.g_WB[slot]
        wq, wqb = self.g_wq[wslot], self.g_wqb[wslot]
        prompt = (seglist[0] == 0)
        if prompt:
            s32, s16, s32b, s16b = w["s32"], w["s16"], wb_["s32"], wb_["s16"]
        s0, s0b_ = self.g_s0, self.g_s0b
        snew, snewb, kdm, mb = self.g_snew, self.g_snewb, self.g_kdm, self.g_mb
        wq32, wsq = self.g_wq32, self.g_wsq
        gl_h, gl_s, gl_st = self.GLIM
        scale_q = float(128 ** -0.5)
        bP = lambda j, c: wb_["pyP%d_%d" % (j, c // 2)]
        bY = lambda j, c: wb_["pyY%d_%d" % (j, c // 2)]
        bT = lambda j, c: wb_["pt%d_%d" % (j, c // 2)]
        if prompt:
            for j in range(4):
                S.dma(wq[j][:], self.winr[16 + 8 * j + h], writes=[wqb[j]], eng="pool")
            self.memset("pool", s32[:], 0.0, [s32b])
            self.memset("pool", s16[:], 0.0, [s16b])
            yield
        for si, (c0, c1, nch) in enumerate(self.segs()):
            if si not in seglist:
                continue
            n = c1 - c0
            sample = (si == 4)
            if (gl_s == 4 and sample) or (gl_s == 1 and si != 0) or (gl_s == -1 and not sample):
                continue
            um, usm, lm = (self.uincl_b, self.ustr_b, self.lstr_b) if sample else (self.uincl, self.ustr, self.lstr)
            for j, cvn in ((0, "cv"), (1, "cv2"), (2, "cv")):
                S.tag = "1proj"
                blk = 8 * j + h
                cvt, cvb = w[cvn], wb_[cvn]
                pt, pb = self.proj(wq[j], wqb[j], c0, n)
                self.conv3(pt, pb, w["zc"][j], wb_["zc%d" % j], si, n,
                           lambda i, blk=blk: V[:, V_GCW + i * 24 + blk:V_GCW + i * 24 + blk + 1],
                           self.gcT[:, blk, :], cvt, cvb, self.stg_gc[:, blk, :])
                self.act(w["th"][:, 0:n], cvt[:, 0:n], AF.Tanh, [cvb], [wb_["th"]], scale=0.5)
                if j == 2:
                    self.stt(w["vf"][:, 0:n], w["th"][:, 0:n], 1.0, cvt[:, 0:n], ALU.add, ALU.mult,
                             [wb_["th"], cvb], [wb_["vf"]])
                    yield
                    continue
                self.stt(cvt[:, 0:n], w["th"][:, 0:n], 1.0, cvt[:, 0:n], ALU.add, ALU.mult, [wb_["th"], cvb], [cvb])
                self.tt("pool", w["sq"][:, 0:n], cvt[:, 0:n], cvt[:, 0:n], ALU.mult, [cvb], [wb_["sq"]])
                p2, p2b = self.ps()
                self.mm(p2[:, 0:n], self.onesb[:], w["sq"][:, 0:n], True, True, [cb, wb_["sq"]], [p2b])
                self.act(w["rt"][:, 0:n], p2[:, 0:n], AF.Ln, [p2b], [wb_["rt"]], bias=4.0 * EPS)
                self.act(w["rt"][:, 0:n], w["rt"][:, 0:n], AF.Exp, [wb_["rt"]], [wb_["rt"]], scale=-0.5,
                         bias=(math.log(scale_q) if j == 0 else 0.0))
                dst = w["kq"][:, 0:nch, 1 - j, :]
                src = cvt[:, 0:n].rearrange("p (c t) -> p c t", t=128)
                rt3 = w["rt"][:, 0:n].rearrange("p (c t) -> p c t", t=128)
                self.tt(self.NORM_ENG, dst, src, rt3, ALU.mult, [cvb, wb_["rt"]], [wb_["kq"]])
                yield
            S.tag = "1proj"
            pg, pgb = self.proj(wq[3], wqb[3], c0, n)
            self.act(w["th"][:, 0:n], pg[:, 0:n], AF.Tanh, [pgb], [wb_["th"]], scale=0.5)
            self.stt(w["sgate"][:, 0:n], w["th"][:, 0:n], 1.0, pg[:, 0:n], ALU.add, ALU.mult, [wb_["th"], pgb], [wb_["sgate"]])
            yield
            if gl_st < 2:
                continue
            S.tag = "2chunk"
            ti0 = c0 // 128
            bc = lambda t: t[:, ti0:ti0 + nch, h:h + 1].broadcast_to([128, nch, 128])
            cs = slice(0, nch)
            ptr, ptrb = self.ps()
            pv = ptr[:].bitcast(BF16)
            for c in range(nch):
                self.tr(pv[:, c * 128:(c + 1) * 128], w["kq"][:, c, 0, :], self.idb[:], [wb_["kq"], cb], [ptrb], signal=False)
            for c in range(nch):
                self.tr(pv[:, 512 + c * 128:512 + (c + 1) * 128], w["vf"][:, c * 128:(c + 1) * 128], self.idb[:],
                        [wb_["vf"], cb], [ptrb], signal=(c == nch - 1))
            self.cp("act", w["ktv"][:, :, cs, :], pv.rearrange("p (a c t) -> p a c t", a=2, t=128)[:, :, cs, :], [ptrb], [wb_["ktv"]])
            KT = w["ktv"][:, 0, cs, :]
            self.tt("pool", w["kg"][:, cs, :], KT, bc(self.egc), ALU.mult, [wb_["ktv"], gb], [wb_["kg"]])
            self.tt("pool", w["kdec"][:, cs, :], KT, bc(self.kdc), ALU.mult, [wb_["ktv"], gb], [wb_["kdec"]])
            mask3 = lambda m: m[:].unsqueeze(1).broadcast_to([128, nch, 128])
            self.tt("pool", w["gA"][:, cs, :], mask3(um), bc(self.gtok), ALU.mult, [cb, gb], [wb_["gA"]])
            pd, pdb = self.ps()
            for c in range(nch):
                self.mm(pd[:, c * 128:(c + 1) * 128], lm[:], w["gA"][:, c, :], True, True, [cb, wb_["gA"]], [pdb],
                        signal=(c == nch - 1))
            self.act(w["gB"][:, cs, :], pd[:, 0:nch * 128].rearrange("p (c t) -> p c t", t=128), AF.Exp, [pdb], [wb_["gB"]])
            self.tt("pool", w["gA"][:, cs, :], w["gB"][:, cs, :], mask3(um), ALU.mult, [wb_["gB"], cb, wb_["gA"]], [wb_["gA"]])
            if sample:
                self.tt("pool", w["gB"][:, cs, :], w["gB"][:, cs, :], mask3(usm), ALU.mult, [wb_["gB"], cb], [wb_["gB"]])
                self.tt("pool", w["gB"][:, cs, :], w["gB"][:, cs, :], bc(self.nbeta), ALU.mult, [wb_["gB"], gb], [wb_["gB"]])
            else:
                goff = w["rt"][:, 0:n].rearrange("p (c t) -> p c t", t=128)
                self.tt("pool", w["gB"][:, cs, :], w["gB"][:, cs, :], bc(self.nbeta), ALU.mult, [wb_["gB"], gb], [wb_["gB"]])
                self.tt("pool", goff, w["gB"][:, cs, :], mask3(self.offf), ALU.mult, [wb_["gB"], cb], [wb_["rt"]])
                self.tt("pool", w["gB"][:, cs, :], w["gB"][:, cs, :], mask3(self.usbd), ALU.mult, [wb_["gB"], cb], [wb_["gB"]])
            if nch == 1:
                pk2, pk2b1 = self.ps()
                pk2b = [pk2b1]
            else:
                pk2, pk2b = self.ps2()
            for c in range(nch):
                self.mm(pk2[:, c * 256:(c + 1) * 256], w["kq"][:, c, 0, :], w["kq"][:, c, :, :].rearrange("p a b -> p (a b)"),
                        True, True, [wb_["kq"]], pk2b, signal=(c == nch - 1))
            pk4 = pk2[:, 0:nch * 256].rearrange("p (c a t) -> p c a t", a=2, t=128)
            self.tt("dve", w["py"][0][:, cs, 0, :], pk4[:, :, 0, :], w["gB"][:, cs, :], ALU.mult, pk2b + [wb_["gB"]], [bP(0, 0)])
            if not sample:
                self.tt("dve", w["on"][:, cs, :], pk4[:, :, 0, :], goff, ALU.mult, pk2b + [wb_["rt"]], [wb_["on"]])
            self.tt("dve", w["aqk"][:, cs, :], pk4[:, :, 1, :], w["gA"][:, cs, :], ALU.mult, pk2b + [wb_["gA"]], [wb_["aqk"]])
            pp, ppb = self.ps()
            ppv = pp[:].bitcast(BF16)
            if not sample:
                for c in range(nch):
                    self.tr(ppv[:, 512 + c * 128:512 + (c + 1) * 128], w["on"][:, c, :], self.idb[:], [wb_["on"], cb], [ppb], signal=False)
            for c in range(nch):
                self.tr(ppv[:, c * 128:(c + 1) * 128], w["py"][0][:, c, 0, :], self.idb[:], [bP(0, 0), cb], [ppb],
                        signal=(c == nch - 1))
            self.cp("act", w["pt"][0][:, cs, :], ppv[:, 0:nch * 128].rearrange("p (c t) -> p c t", t=128), [ppb], [bT(0, 0)])
            if not sample:
                self.cp("act", w["ktv"][:, 0, cs, :], ppv[:, 512:512 + nch * 128].rearrange("p (c t) -> p c t", t=128), [ppb], [wb_["ktv"]])
            self.tt("dve", w["py"][1][:, cs, 1, :], w["py"][0][:, cs, 0, :], mask3(self.idb), ALU.add,
                    [bP(0, 0), cb], [bY(1, 0)])
            yield
            if gl_st < 3:
                continue
            S.tag = "3dbl"
            nit = 3 if sample else 6
            pX, pXb = self.ps()
            pB, pBb = self.ps()
            for c in range(nch):
                self.mm(pX[:, c * 128:(c + 1) * 128], w["pt"][0][:, c, :], w["py"][0][:, c, 0, :], True, True,
                        [bT(0, 0), bP(0, 0)], [pXb], signal=(c == nch - 1))
            for c in range(nch):
                self.mm(pB[:, c * 128:(c + 1) * 128], w["py"][0][:, c, 0, :], w["pt"][0][:, c, :], True, True,
                        [bT(0, 0), bP(0, 0)], [pBb], signal=(c == nch - 1))
            v3 = lambda p_: p_[:, 0:nch * 128].rearrange("p (c t) -> p c t", t=128)
            self.cp("act", w["py"][1][:, cs, 0, :], v3(pX), [pXb], [bP(1, 0)])
            self.cp("act", w["pt"][1][:, cs, :], v3(pB), [pBb], [bT(1, 0)])
            yield
            for k in range(2, nit + 1):
                rd, wr = (k - 1) % 2, k % 2
                last = (k == nit)
                rdb = [bT(rd, 0), bP(rd, 0), bY(rd, 0)]
                if last or nch == 1 or not self.DBL256:
                    pZ, pZb = self.ps()
                    for c in range(nch):
                        self.mm(pZ[:, c * 128:(c + 1) * 128], w["pt"][rd][:, c, :], w["py"][rd][:, c, 1, :], True, True,
                                rdb, [pZb], signal=(c == nch - 1))
                    self.tt("dve", w["py"][wr][:, cs, 1, :], v3(pZ), w["py"][rd][:, cs, 1, :], ALU.add, [pZb, bY(rd, 0)], [bY(wr, 0)])
                    if not last:
                        pX, pXb = self.ps()
                        for c in range(nch):
                            self.mm(pX[:, c * 128:(c + 1) * 128], w["pt"][rd][:, c, :], w["py"][rd][:, c, 0, :], True, True,
                                    rdb, [pXb], signal=(c == nch - 1))
                        self.cp("act", w["py"][wr][:, cs, 0, :], v3(pX), [pXb], [bP(wr, 0)])
                else:
                    pA2, pA2b = self.ps2()
                    for c in range(nch):
                        self.mm(pA2[:, c * 256:(c + 1) * 256], w["pt"][rd][:, c, :], w["py"][rd][:, c, :, :].rearrange("p a b -> p (a b)"),
                                True, True, rdb, pA2b, signal=(c == nch - 1))
                    p4 = pA2[:, 0:nch * 256].rearrange("p (c a t) -> p c a t", a=2, t=128)
                    self.tt("dve", w["py"][wr][:, cs, 1, :], p4[:, :, 1, :], w["py"][rd][:, cs, 1, :], ALU.add, pA2b + [bY(rd, 0)], [bY(wr, 0)])
                    self.cp("act", w["py"][wr][:, cs, 0, :], p4[:, :, 0, :], pA2b, [bP(wr, 0)])
                if not last:
                    pB, pBb = self.ps()
                    for c in range(nch):
                        self.mm(pB[:, c * 128:(c + 1) * 128], w["py"][rd][:, c, 0, :], w["pt"][rd][:, c, :], True, True,
                                rdb, [pBb], signal=(c == nch - 1))
                    self.cp("act", w["pt"][wr][:, cs, :], v3(pB), [pBb], [bT(wr, 0)])
                yield
            if gl_st < 4:
                continue
            yb = nit % 2
            S.tag = "4ubw"
            if sample:
                pu, pub = self.ps()
                pw2, pw2b = self.ps()
                for c in range(nch):
                    self.mm(pu[:, c * 128:(c + 1) * 128], w["py"][yb][:, c, 1, :], w["ktv"][:, 1, c, :], True, True,
                            [bY(yb, 0), wb_["ktv"]], [pub], signal=(c == nch - 1))
                for c in range(nch):
                    self.mm(pw2[:, c * 128:(c + 1) * 128], w["kg"][:, c, :], w["py"][yb][:, c, 1, :], True, True,
                            [bY(yb, 0), wb_["kg"]], [pw2b], signal=(c == nch - 1))
            else:
                Yd = lambda c: w["py"][yb][:, c, 1, :]
                sq3 = w["sq"][:, 0:n].rearrange("p (c t) -> p c t", t=128)
                vf3 = w["vf"][:, 0:n].rearrange("p (c t) -> p c t", t=128)
                pz_, pzb_ = self.ps()
                for c in range(nch):
                    self.mm(pz_[:, c * 128:(c + 1) * 128], w["ktv"][:, 0, c, :], Yd(c), True, True, [wb_["ktv"], bY(yb, 0)], [pzb_],
                            signal=(c == nch - 1))
                self.cp("act", w["on"][:, cs, :], v3(pz_), [pzb_], [wb_["on"]])
                pu, pub = self.ps()
                for c in range(nch):
                    self.mm(pu[:, c * 128:(c + 1) * 128], Yd(c), w["ktv"][:, 1, c, :], c == 0, False, [bY(yb, 0), wb_["ktv"]], [pub],
                            signal=(c == nch - 1), sgc=True)
                self.cp("act", sq3, v3(pu), [pub], [wb_["sq"]])
                pw0, pw0b = self.ps()
                for c in range(nch):
                    self.mm(pw0[:, c * 128:(c + 1) * 128], Yd(c), w["kg"][:, c, :], True, True, [bY(yb, 0), wb_["kg"]], [pw0b],
                            signal=(c == nch - 1))
                self.cp("act", vf3, v3(pw0), [pw0b], [wb_["vf"]])
                for c in range(nch):
                    self.mm(pu[:, c * 128:(c + 1) * 128], w["on"][:, c, :], sq3[:, c, :], False, True, [wb_["on"], wb_["sq"]], [pub],
                            signal=(c == nch - 1), sgc=True)
                pw2, pw2b = self.ps()
                for c in range(nch):
                    self.mm(pw2[:, c * 128:(c + 1) * 128], w["kg"][:, c, :], Yd(c), c == 0, False, [bY(yb, 0), wb_["kg"]], [pw2b],
                            signal=False, sgc=True)
                for c in range(nch):
                    self.mm(pw2[:, c * 128:(c + 1) * 128], vf3[:, c, :], w["on"][:, c, :], False, True, [wb_["vf"], wb_["on"]], [pw2b],
                            signal=(c == nch - 1), sgc=True)
            self.tt("dve", w["ub"][:, cs, :], v3(pu), bc(self.betah), ALU.mult, [pub, gb], [wb_["ub"]])
            if sample:
                self.cp("act", wq32[:, :, 0, :], pw2[:, 0:128].rearrange("p (s t) -> p s t", t=8), [pw2b], [mb])
            else:
                self.cp("act", w["wT"][:, cs, :], v3(pw2), [pw2b], [wb_["wT"]])
                yield
            if gl_st < 5:
                continue
            if sample:
                S.tag = "5samp"
                S.dma(s0[:], self.sgS[:, h, :, :].rearrange("s k v -> k s v"), writes=[s0b_])
                self.tt("pool", kdm[:], w["kdec"][:, 0, :].unsqueeze(1).broadcast_to([128, 16, 128]),
                        self.selc[:].unsqueeze(2).broadcast_to([128, 16, 128]), ALU.mult, [wb_["kdec"], cb], [mb])
                self.cp("pool", wq32[:, :, 1, :], w["kq"][:, 0, 1, :].rearrange("p (s t) -> p s t", t=8), [wb_["kq"]], [mb])
            if gl_st < 6:
                continue
            po, pob = self.ps()
            for c in range(nch):
                S.tag = "6chain" if not sample else "6chainS"
                ti = c0 // 128 + c
                vn, qss = w["vn"], w["qs"]
                col = lambda t, ti=ti: t[:, ti, h:h + 1]
                if not sample:
                    pw_, pwb_ = self.ps()
                    pq_, pqb_ = self.ps()
                    self.mm(pw_[:, 0:128], w["wT"][:, c, :], s16[:], True, True, [wb_["wT"], s16b], [pwb_])
                    self.mm(pq_[:, 0:128], w["kq"][:, c, 1, :], s16[:], True, True, [wb_["kq"], s16b], [pqb_])
                    pwv, pqv = pw_[:, 0:128], pq_[:, 0:128]
                    pqb2 = pqb_
                else:
                    pws, pwsb = self.ps()
                    for s_ in range(16):
                        self.mm(pws[:, s_ * 16:(s_ + 1) * 16], s0[:, s_, :], wq32[:, s_, :, :].rearrange("p a t -> p (a t)"),
                                True, True, [s0b_, mb], [pwsb], signal=(s_ == 15))
                    self.cp("act", wsq[:].rearrange("p a (s t) -> p a s t", t=8),
                            pws[:, 0:256].rearrange("p (s a t) -> p a s t", a=2, t=8), [pwsb], [mb])
                    pw_, pwb_ = self.ps()
                    self.tr(pw_[:, 0:128], wsq[:, 0, :], self.idf[:], [mb, cb], [pwb_], signal=False)
                    self.tr(pw_[:, 128:256], wsq[:, 1, :], self.idf[:], [mb, cb], [pwb_])
                    pwv, pqv = pw_[:, 0:128], pw_[:, 128:256]
                    pqb2 = pwb_
                self.stt(vn[:], pwv, col(self.nbeta), w["ub"][:, c, :], ALU.mult, ALU.add,
                         [pwb_, gb, wb_["ub"]], [wb_["vn"]])
                self.act(qss[:], pqv, AF.Copy, [pqb2, gb], [wb_["qs"]], scale=col(self.egc))
                self.mm(po[:, c * 128:(c + 1) * 128], w["aqk"][:, c, :], vn[:], True, True, [wb_["aqk"], wb_["vn"]], [pob])
                self.tt("dve", w["ot"][:, c, :], po[:, c * 128:(c + 1) * 128], qss[:], ALU.add, [pob, wb_["qs"]], [wb_["ot"]])
                if not sample:
                    pkv, pkvb = self.ps()
                    self.mm(pkv[:, 0:128], w["kdec"][:, c, :], vn[:], True, True, [wb_["kdec"], wb_["vn"]], [pkvb])
                    gl = self.eglb[:, ti, h:h + 1]
                    if self.S16_DVE:
                        self.stt(s16[:], s32[:], gl, pkv[:, 0:128], ALU.mult, ALU.add, [s32b, gb, pkvb], [s16b])
                        self.stt(s32[:], s32[:], gl, pkv[:, 0:128], ALU.mult, ALU.add, [s32b, gb, pkvb], [s32b])
                    else:
                        self.stt(s32[:], s32[:], gl, pkv[:, 0:128], ALU.mult, ALU.add, [s32b, gb, pkvb], [s32b])
                        self.cp("pool", s16[:], s32[:], [s32b], [s16b])
                else:
                    for q4 in range(4):
                        pkv, pkvb = self.ps()
                        for u in range(4):
                            s_ = q4 * 4 + u
                            self.mm(pkv[:, u * 128:(u + 1) * 128], kdm[:, s_, :], vn[:], True, True, [mb, wb_["vn"]], [pkvb],
                                    signal=(u == 3))
                        for u in range(4):
                            s_ = q4 * 4 + u
                            self.stt(snew[q4 % 2][:, u, :], s0[:, s_, :], self.eglS[:, s_, h:h + 1], pkv[:, u * 128:(u + 1) * 128],
                                     ALU.mult, ALU.add, [s0b_, gb, pkvb], [snewb[q4 % 2]])
                        S.dma(self.o_sgs[q4 * 4:(q4 + 1) * 4, h, :, :].rearrange("s k v -> k s v"), snew[q4 % 2][:],
                              reads=[snewb[q4 % 2]], is_output=True)
                yield
            S.tag = "7opath"
            for c in range(nch):
                self.act(w["on"][:, c, :], w["ot"][:, c, :], AF.Square, [wb_["ot"]], [wb_["on"], wb_["ss4"]],
                         accum_out=w["ss4"][:, c:c + 1])
            self.rstd(w["ss4"][:, cs], 1.0 / 128, [wb_["ss4"]])
            self.tt("dve", w["on"][:, cs, :], w["ot"][:, cs, :], w["ss4"][:, cs].unsqueeze(2).broadcast_to([128, nch, 128]),
                    ALU.mult, [wb_["ot"], wb_["ss4"]], [wb_["on"]])
            pz, pzb = self.ps()
            pzv = pz[:].bitcast(BF16)
            for c in range(nch):
                self.tr(pzv[:, c * 128:(c + 1) * 128], w["on"][:, c, :], self.idb[:], [wb_["on"], cb], [pzb], signal=(c == nch - 1))
            self.stt(self.gdn_out[:, h, c0:c0 + n], pzv[:, 0:n], self.gnwh[:, 0:1], w["sgate"][:, 0:n], ALU.mult, ALU.mult,
                     [pzb, cb, wb_["sgate"]], [])
            yield
        if prompt:
            S.dma(self.o_pgs[h, :, :], s32[:], reads=[s32b], is_output=True)
            yield

    def conv3(self, pt, pb, zc, zcb, si, n, wcol, stateT, out, outb, stg, bcol=0.0):
        cb = self.cb
        if si < 4:
            self.act(out[:, 0:n], pt[:, 0:n], AF.Identity, [pb, cb], [outb], scale=wcol(3), bias=bcol)
            for i in range(3):
                sh = 3 - i
                self.stt(out[:, sh:n], pt[:, 0:n - sh], wcol(i), out[:, sh:n], ALU.mult, ALU.add, [pb, cb, outb], [outb])
                if si > 0:
                    self.stt(out[:, 0:sh], zc[:, i:i + sh], wcol(i), out[:, 0:sh], ALU.mult, ALU.add, [zcb, cb, outb], [outb])
            if si == 3:
                self.cp("dve", stg[:, 0:3], pt[:, n - 3:n], [pb], [])
            else:
                self.cp("dve", zc[:, 0:3], pt[:, n - 3:n], [pb, outb], [zcb])
        else:
            p3 = pt[:, 0:128].rearrange("p (s t) -> p s t", t=8)
            o3 = out[:, 0:128].rearrange("p (s t) -> p s t", t=8)
            st3 = stateT.rearrange("p (s i) -> p s i", i=3)
            self.act(o3, p3, AF.Identity, [pb, cb], [outb], scale=wcol(3), bias=bcol)
            for i in range(3):
                sh = 3 - i
                self.stt(o3[:, :, sh:8], p3[:, :, 0:8 - sh], wcol(i), o3[:, :, sh:8], ALU.mult, ALU.add, [pb, cb, outb], [outb])
                self.stt(o3[:, :, 0:sh], st3[:, :, i:i + sh], wcol(i), o3[:, :, 0:sh], ALU.mult, ALU.add, [cb, outb], [outb])
            self.cp("dve", stg[:, 3:51].rearrange("p (s i) -> p s i", i=3), p3[:, :, 5:8], [pb], [])

    def conv2(self, pt, pb, zp, zpb, zc, zcb, si, n, wcol, bcol, stateT, out, outb, stg):
        cb = self.cb
        if si < 4:
            self.cp("act", zp[:, 3:3 + n], pt[:, 0:n], [pb], [zpb])
            if si == 0:
                self.memset("pool", zp[:, 0:3], 0.0, [zpb])
            else:
                self.cp("pool", zp[:, 0:3], zc[:, 0:3], [zcb], [zpb])
            if si == 3:
                self.cp("pool", stg[:, 0:3], zp[:, 512:515], [zpb], [])
            else:
                self.cp("pool", zc[:, 0:3], zp[:, 512:515], [zpb], [zcb])
            src = [zp[:, i:i + n] for i in range(4)]
            dst = out[:, 0:n]
        else:
            z3 = zp[:, 0:176].rearrange("p (s t) -> p s t", t=11)
            self.cp("act", z3[:, :, 3:11], pt[:, 0:128].rearrange("p (s t) -> p s t", t=8), [pb], [zpb])
            self.cp("pool", z3[:, :, 0:3], stateT.rearrange("p (s i) -> p s i", i=3), [cb], [zpb])
            self.cp("pool", stg[:, 3:51].rearrange("p (s i) -> p s i", i=3), z3[:, :, 8:11], [zpb], [])
            src = [z3[:, :, i:i + 8] for i in range(4)]
            dst = out[:, 0:128].rearrange("p (s t) -> p s t", t=8)
        if bcol is None:
            self.ts("dve", dst, src[0], wcol(0), None, ALU.mult, None, [zpb, cb], [outb])
        else:
            self.ts("dve", dst, src[0], wcol(0), bcol, ALU.mult, ALU.add, [zpb, cb], [outb])
        for i in range(1, 4):
            self.stt(dst, src[i], wcol(i), dst, ALU.mult, ALU.add, [zpb, cb, outb], [outb])

    def _state_outputs(self):
        S, sb = self.S, self.sb
        cb = self.cb
        scr2 = self.gdn_out[:].rearrange("p k t -> p (k t)").bitcast(F32)
        olc, olh, ogc = scr2[0:51, 0:D], scr2[0:17, D:2 * D], scr2[0:51, 2 * D:2 * D + 3072]
        ob = S.buf("ostate")
        for g in range(8):
            pt, pb = self.ps()
            self.tr(pt[0:51, 0:128], self.stg_lc[:, g, :], self.idf[:], [cb], [pb])
            self.cp("dve", olc[:, g * 128:(g + 1) * 128], pt[0:51, 0:128], [pb], [ob])
            pt, pb = self.ps()
            self.tr(pt[0:17, 0:128], self.stg_lh[:, g, :], self.idf[:], [cb], [pb])
            self.cp("dve", olh[:, g * 128:(g + 1) * 128], pt[0:17, 0:128], [pb], [ob])
        for g in range(24):
            pt, pb = self.ps()
            self.tr(pt[0:51, 0:128], self.stg_gc[:, g, :], self.idf[:], [cb], [pb])
            self.cp("dve", ogc[:, g * 128:(g + 1) * 128], pt[0:51, 0:128], [pb], [ob])
        S.dma(self.o_plc, olc[0:3, :], reads=[ob], is_output=True)
        S.dma(self.o_slc, olc[3:51, :], reads=[ob], is_output=True)
        S.dma(self.o_plh, olh[0:1, :], reads=[ob], is_output=True)
        S.dma(self.o_slh, olh[1:17, :], reads=[ob], is_output=True)
        S.dma(self.o_pgc, ogc[0:3, :], reads=[ob], is_output=True)
        S.dma(self.o_sgc, ogc[3:51, :], reads=[ob], is_output=True)

    def _phase_merge(self):
        S, sb = self.S, self.sb
        cb = self.cb
        merged, mgb = self.merged, self.mgb
        wm = [[sb("mw%d_%d" % (j, i), [128, KC, 128], BF16) for j in range(4)] for i in range(2)]
        wmb = [[S.buf("mw") for j in range(4)] for i in range(2)]
        s3 = [sb("ms3_%d" % i, [128, 512]) for i in range(2)]
        s4 = [sb("ms4_%d" % i, [128, 512]) for i in range(2)]
        t1 = [sb("mt1_%d" % i, [128, 512]) for i in range(2)]
        t2 = [sb("mt2_%d" % i, [128, 512]) for i in range(2)]
        mbs = [{k: S.buf("m" + k) for k in ("s3", "s4", "t1", "t2")} for _ in range(2)]
        cnt = 0
        for c in range(8):
            s = c % 2
            srcs = (self.wbl_d[c], self.wbg_d[c], self.winr[48 + c], self.winr[56 + c])
            for j in range(4):
                S.dma(wm[s][j][:], srcs[j], writes=[wmb[s][j]], eng="pool")
            for si, (c0, c1, nch) in enumerate(self.segs()):
                n = c1 - c0
                b = mbs[cnt % 2]
                i2 = cnt % 2
                cnt += 1
                acts = (self.lru_out, self.gdn_out, self.xn, self.xn)
                pp = []
                for j in range(4):
                    pt, pb = self.ps()
                    for k in range(KC):
                        self.mm(pt[:, 0:n], wm[s][j][:, k, :], acts[j][:, k, c0:c1], k == 0, k == KC - 1, [wmb[s][j]], [pb])
                    pp.append((pt, pb))
                self.act(s3[i2][:, 0:n], pp[2][0][:, 0:n], AF.Tanh, [pp[2][1]], [b["s3"]], scale=0.5)
                self.act(s4[i2][:, 0:n], pp[3][0][:, 0:n], AF.Tanh, [pp[3][1]], [b["s4"]], scale=0.5)
                self.stt(t1[i2][:, 0:n], s3[i2][:, 0:n], 1.0, pp[0][0][:, 0:n], ALU.add, ALU.mult, [pp[0][1], b["s3"]], [b["t1"]])
                self.stt(t2[i2][:, 0:n], s4[i2][:, 0:n], 1.0, pp[1][0][:, 0:n], ALU.add, ALU.mult, [pp[1][1], b["s4"]], [b["t2"]])
                self.tt("pool", merged[:, c, c0:c1], t1[i2][:, 0:n], t2[i2][:, 0:n], ALU.add, [b["t1"], b["t2"]], [mgb])

    def _phase_out(self):
        S, sb = self.S, self.sb
        cb = self.cb
        self._state_outputs()
        merged, mgb = self.merged, self.mgb
        wout16 = sb("wout16", [128, KC, D], BF16)
        wob = S.buf("wout")
        S.dma(wout16[:], self.wout_d, writes=[wob], eng="pool")
        npb = sb("npb", [128, D])
        npbb = S.buf("npb")
        S.dma(npb[:], self.npost_d.broadcast_to([128, D]), writes=[npbb])
        NB = 4
        scr = self.xn[:].rearrange("p k t -> p (k t)").bitcast(F32)
        yt = [scr[:, i * D:(i + 1) * D] for i in range(NB)]
        xr = [scr[:, (NB + i) * D:(NB + i + 1) * D] for i in range(NB)]
        ytb = [S.buf("yt") for _ in range(NB)]
        xrb = [S.buf("xr") for _ in range(NB)]
        ss = sb("ssD", [128, NT])
        ssb = [S.buf("ssd") for _ in range(NT)]
        jb = S.buf("junkd")
        junk = sb("junkd", [128, D], BF16)
        for i in range(NT):
            s = i % NB
            S.dma(xr[s][:], self.xsrc(i), writes=[xrb[s]])
            for hf in range(2):
                pt, pb = self.ps()
                for k in range(KC):
                    self.mm(pt[:, 0:512], merged[:, k, i * 128:(i + 1) * 128], wout16[:, k, hf * 512:(hf + 1) * 512],
                            k == 0, k == KC - 1, [mgb, wob], [pb])
                self.cp("act", yt[s][:, hf * 512:(hf + 1) * 512], pt[:, 0:512], [pb], [ytb[s]])
            self.act(junk[:], yt[s][:], AF.Square, [ytb[s]], [jb, ssb[i]], accum_out=ss[:, i:i + 1])
            self.rstd(ss[:, i:i + 1], 1.0 / D, [ssb[i]], eps=4.0 * EPS)
            self.stt(yt[s][:], yt[s][:], ss[:, i:i + 1], npb[:], ALU.mult, ALU.mult, [ytb[s], ssb[i], npbb], [ytb[s]])
            self.tt("pool", yt[s][:], yt[s][:], xr[s][:], ALU.add, [ytb[s], xrb[s]], [ytb[s]])
            S.dma(self.ydst(i), yt[s][:], reads=[ytb[s]], is_output=True)


_NC_CACHE = {}


def _program():
    if "nc" not in _NC_CACHE:
        _NC_CACHE["nc"] = K().build()
    return _NC_CACHE["nc"]


def _f(a):
    return np.ascontiguousarray(a, dtype=np.float32)


def kernel(x_prompt, x_sample, state_lru_conv, state_lru_h, state_gdn_conv, state_gdn_S,
           norm_pre, norm_post, w_in, lru_conv_w, lru_conv_b, lru_wa, lru_ba, lru_wx, lru_bx,
           lru_a_logit, gdn_conv_w, gdn_A_log, gdn_dt_bias, gdn_norm_w, w_br_lru, w_br_gdn, w_out):
    w_in0 = np.asarray(w_in)[0]
    wcat = np.concatenate([w_in0[:, 0:6144], w_in0[:, 6160:8208]], axis=1)
    winr = _f(wcat.reshape(KC, 128, 64, 128).transpose(2, 1, 0, 3))
    wba = _f(w_in0[:, 6144:6160].reshape(KC, 128, 16).transpose(1, 0, 2))
    wa = _f(np.asarray(lru_wa)[0].transpose(1, 0, 2))
    wx = _f(np.asarray(lru_wx)[0].transpose(1, 0, 2))
    wbl = _f(np.asarray(w_br_lru)[0].reshape(KC, 128, 8, 128).transpose(2, 1, 0, 3))
    wbg = _f(np.asarray(w_br_gdn)[0].reshape(KC, 128, 8, 128).transpose(2, 1, 0, 3))
    wout = _f(np.asarray(w_out)[0].reshape(KC, 128, D).transpose(1, 0, 2))
    cols = lambda v, n: np.asarray(v).reshape(n, 128).T
    vecs = _f(np.concatenate([
        cols(norm_pre[0], 8),
        np.asarray(lru_conv_w)[0].reshape(4, 8, 128).transpose(2, 0, 1).reshape(128, 32),
        cols(lru_conv_b[0], 8), cols(lru_ba[0], 8), cols(lru_bx[0], 8), cols(lru_a_logit[0], 8),
        np.asarray(gdn_conv_w)[0].reshape(4, 24, 128).transpose(2, 0, 1).reshape(128, 96),
        np.asarray(gdn_norm_w)[0].reshape(128, 1)], axis=1))
    npost = _f(np.asarray(norm_post).reshape(1, D))
    alog = _f(np.asarray(gdn_A_log).reshape(1, 8))
    dtb = _f(np.asarray(gdn_dt_bias).reshape(1, 8))
    in_maps = []
    for c in range(NCORES):
        sl = slice(16 * c, 16 * c + 16)
        in_maps.append(dict(
            xp=_f(x_prompt[c]), xs=_f(np.asarray(x_sample)[sl].reshape(128, D)),
            slc=_f(np.asarray(state_lru_conv)[0, sl].reshape(48, D)), slh=_f(np.asarray(state_lru_h)[0, sl]),
            sgc=_f(np.asarray(state_gdn_conv)[0, sl].reshape(48, 3072)), sgS=_f(np.asarray(state_gdn_S)[0, sl]),
            vecs=vecs, npost=npost, alog=alog, dtb=dtb, winr=winr, wba=wba, wa=wa, wx=wx, wbl=wbl, wbg=wbg, wout=wout))
    nc = _program()
    res = run_bass_kernel_spmd(nc, in_maps, core_ids=list(range(NCORES)))
    R = res.results
    cat = lambda name: np.stack([np.asarray(R[c][name]) for c in range(NCORES)], axis=0)
    yp = cat("yp")
    ys = cat("ys").reshape(128, 8, D)
    p_lc = cat("o_plc")[None]
    p_lh = cat("o_plh").reshape(1, 8, D)
    p_gc = cat("o_pgc")[None]
    p_gs = cat("o_pgs")[None]
    s_lc = cat("o_slc").reshape(1, 128, 3, D)
    s_lh = cat("o_slh").reshape(1, 128, D)
    s_gc = cat("o_sgc").reshape(1, 128, 3, 3072)
    s_gs = cat("o_sgs").reshape(1, 128, 8, 128, 128)
    return tuple(np.ascontiguousarray(a, dtype=np.float32) for a in (yp, ys, p_lc, p_lh, p_gc, p_gs, s_lc, s_lh, s_gc, s_gs))
```

```python
import math
import numpy as np
import concourse.bass as bass
import concourse.mybir as mybir
from concourse.bass_utils import run_bass_kernel_spmd
from contextlib import ExitStack

F32 = mybir.dt.float32
BF16 = mybir.dt.bfloat16
AF = mybir.ActivationFunctionType
ALU = mybir.AluOpType

NCORES = 8
D = 1024
KC = 8
NT = 17
TT = NT * 128
EPS = 1e-6
NV = 169
V_NPRE, V_LCW, V_LCB, V_LBA, V_LBX, V_LAL, V_GCW, V_GNW = 0, 8, 40, 48, 56, 64, 72, 168


class Buf:
    __slots__ = ("name", "w", "r", "dsem", "dcnt", "excl", "lastdma")

    def __init__(self, name):
        self.name = name
        self.w = None
        self.r = []
        self.dsem = None
        self.dcnt = 0
        self.excl = False
        self.lastdma = None


class Op:
    __slots__ = ("id", "eng", "fn", "deps", "signal", "dur", "tab", "dma", "lat", "epoch", "idx", "unit", "tag")


class Sched:
    ENG = ("pe", "act", "dve", "pool", "sp")
    BLEV = True
    BUCKET = 0.1

    def __init__(self, nc, stack):
        self.nc = nc
        self.stack = stack
        self.sem = {e: stack.enter_context(nc.semaphore("c_" + e)) for e in ("pe", "act", "dve", "pool")}
        self.allops = []
        self.epoch = 0
        self.final = {}
        self.dsems = []
        self.nb = 0

    def buf(self, name="b"):
        self.nb += 1
        return Buf("%s_%d" % (name, self.nb))

    def _record(self, eng, fn, reads, writes, signal, dur, tab, dma=None, lat=0.0):
        ex = [b for b in reads if b.excl]
        if ex:
            reads = [b for b in reads if not b.excl]
            writes = list(writes) + ex
        o = Op()
        o.id = len(self.allops)
        o.eng, o.fn, o.signal, o.dur, o.tab, o.dma, o.lat, o.epoch = eng, fn, signal, dur, tab, dma, lat, self.epoch
        o.tag = getattr(self, "tag", "")
        deps = set()
        for b in reads:
            if b.w is not None:
                deps.add(b.w)
        for b in writes:
            if b.w is not None:
                deps.add(b.w)
            deps.update(b.r)
        o.deps = deps
        for b in reads:
            b.r.append(o.id)
        for b in writes:
            b.w = o.id
            b.r = []
        self.allops.append(o)
        return o

    def op(self, eng, fn, reads=(), writes=(), signal=True, dur=0.3, tab=None):
        self._record(eng, fn, reads, writes, signal, dur, tab)

    def dsem_of(self, buf):
        if buf.dsem is None:
            buf.dsem = self.stack.enter_context(self.nc.semaphore("d_" + buf.name))
            self.dsems.append(buf)
        return buf.dsem

    def dma(self, out, in_, reads=(), writes=(), sembuf=None, eng="sp", is_output=False, nbytes=1 << 20, **kw):
        if sembuf is None:
            sembuf = writes[0] if writes else reads[0]
        self.dsem_of(sembuf)
        o = self._record(eng, lambda e, o=out, i=in_, k=kw: e.dma_start(out=o, in_=i, **k), reads, writes, True,
                         0.15 if eng != "pool" else 1.0, None, dma=(sembuf, is_output), lat=2.0 + nbytes / 2.0e5)
        if sembuf.lastdma is not None:
            o.deps.add(sembuf.lastdma)
        sembuf.lastdma = o.id

    def barrier(self):
        self.epoch += 1

    def _schedule_epoch(self, ops, order, t0):
        units = []
        cur = None
        for o in ops:
            if o.eng == "pe":
                if cur is None:
                    cur = [o]
                else:
                    cur.append(o)
                if o.signal:
                    units.append(cur)
                    cur = None
            else:
                units.append([o])
        assert cur is None, "PE op run without a final signalled op"
        uid = {}
        for ui, u in enumerate(units):
            for o in u:
                uid[o.id] = ui
        nu = len(units)
        first = ops[0].id if ops else 0
        preds = [set() for _ in range(nu)]
        succs = [[] for _ in range(nu)]
        for ui, u in enumerate(units):
            for o in u:
                for d in o.deps:
                    if d >= first and uid[d] != ui:
                        preds[ui].add(uid[d])
        for ui in range(nu):
            for p in preds[ui]:
                succs[p].append(ui)
        indeg = [len(p) for p in preds]
        ready_t = [t0] * nu
        udur = [sum(o.dur for o in u) for u in units]
        ueng = [u[0].eng for u in units]
        blev = [0.0] * nu
        for ui in range(nu - 1, -1, -1):
            m = 0.0
            for sidx in succs[ui]:
                if blev[sidx] > m:
                    m = blev[sidx]
            blev[ui] = udur[ui] + units[ui][-1].lat + 0.3 + m
        rdy = {e: [] for e in self.ENG}
        for ui in range(nu):
            if indeg[ui] == 0:
                rdy[ueng[ui]].append(ui)
        free = {e: t0 for e in self.ENG}
        curtab = {"act": None}
        done = 0
        tend = t0
        while done < nu:
            best = None
            for e in self.ENG:
                lst = rdy[e]
                if not lst:
                    continue
                f = free[e]
                cand = None
                for ui in lst:
                    st = max(f, ready_t[ui])
                    if e == "act":
                        tb = units[ui][0].tab
                        if tb is not None and curtab["act"] is not None and tb != curtab["act"]:
                            st += 1.3
                    key = (int(st / self.BUCKET), -blev[ui], ui) if self.BLEV else (st, ui)
                    if cand is None or key < cand[0]:
                        cand = (key, ui, st)
                if best is None or cand[0] < best[0]:
                    best = (cand[0], cand[1], cand[2], e)
            _, ui, st, e = best
            rdy[e].remove(ui)
            if e == "act" and units[ui][0].tab is not None:
                curtab["act"] = units[ui][0].tab
            fin = st + udur[ui]
            free[e] = fin
            lat = units[ui][-1].lat
            tend = max(tend, fin + lat)
            order[e].extend(units[ui])
            done += 1
            for sidx in succs[ui]:
                extra = 0.12 if ueng[sidx] == e else 0.3
                ready_t[sidx] = max(ready_t[sidx], fin + lat + extra)
                indeg[sidx] -= 1
                if indeg[sidx] == 0:
                    rdy[ueng[sidx]].append(sidx)
        return tend

    def emit(self):
        ops = self.allops
        order = {e: [] for e in self.ENG}
        bounds = []
        t = 0.0
        i = 0
        n = len(ops)
        while i < n:
            j = i
            while j < n and ops[j].epoch == ops[i].epoch:
                j += 1
            t = self._schedule_epoch(ops[i:j], order, t)
            bounds.append({e: len(order[e]) for e in self.ENG})
            i = j
        self.est_us = t
        cnt = {e: 0 for e in self.ENG}
        dmaval = {}
        for e in self.ENG:
            lst = order[e]
            pending = []
            for o in lst:
                if o.dma is not None:
                    sb_, is_out = o.dma
                    sb_.dcnt += 16
                    o.idx = (("d", sb_.name), sb_.dsem, sb_.dcnt)
                    if is_out:
                        self.final[sb_.name] = (sb_.dsem, sb_.dcnt)
                elif o.signal:
                    cnt[e] += 1
                    o.idx = (e, self.sem[e], cnt[e])
                    for p in pending:
                        p.idx = o.idx
                    pending = []
                else:
                    pending.append(o)
            assert not pending
        prog = {e: [] for e in self.ENG}
        waited = {e: {} for e in self.ENG}
        pos = {e: 0 for e in self.ENG}
        for bi, bd in enumerate(bounds):
            for e in self.ENG:
                for o in order[e][pos[e]:bd[e]]:
                    waits = []
                    for d in sorted(o.deps):
                        key, sem, val = ops[d].idx
                        if key == e and e == "pe":
                            continue
                        if waited[e].get(key, 0) >= val:
                            continue
                        waited[e][key] = val
                        waits.append((sem, val))
                    inc = None
                    if o.dma is not None:
                        inc = (o.idx[1], 16)
                    elif o.signal:
                        inc = (self.sem[e], 1)
                    prog[e].append((waits, o.fn, inc))
                pos[e] = bd[e]
            if bi < len(bounds) - 1:
                for e in self.ENG:
                    waits = []
                    for f in ("pe", "act", "dve", "pool"):
                        c = 0
                        for o in order[f][:bd[f]]:
                            if o.dma is None and o.signal:
                                c = max(c, o.idx[2])
                        if f != e and c > 0 and waited[e].get(f, 0) < c:
                            waited[e][f] = c
                            waits.append((self.sem[f], c))
                    for q in ("sp", "pool"):
                        last = {}
                        for o in order[q][:bd[q]]:
                            if o.dma is not None:
                                last[o.idx[0]] = o.idx
                        for key, (k_, sem, val) in last.items():
                            if waited[e].get(key, 0) < val:
                                waited[e][key] = val
                                waits.append((sem, val))
                    if waits:
                        prog[e].append((waits, None, None))
        fw = list(self.final.values())
        if fw:
            prog["sp"].append((fw, None, None))
        self.prog = prog
        with self.nc.Block() as block:
            def mk(name):
                lst = prog[name]

                def body(eng):
                    for waits, fn, inc in lst:
                        for s_, v in waits:
                            eng.wait_ge(s_, v)
                        if fn is not None:
                            ins = fn(eng)
                            if inc is not None:
                                ins.then_inc(*inc)
                return body
            block.tensor(mk("pe"))
            block.scalar(mk("act"))
            block.vector(mk("dve"))
            block.gpsimd(mk("pool"))
            block.sync(mk("sp"))


class PV:
    def __init__(self, t, off):
        self.t, self.off = t, off

    def __getitem__(self, key):
        if isinstance(key, slice):
            return self.t[:, self.off:self.off + 512]
        p, c = key
        return self.t[p, c.start + self.off:c.stop + self.off]


def _fsz(ap):
    n = 1
    for d in ap.shape[1:]:
        n *= int(d)
    return n


class K:
    STOP = 99
    NORM_ENG = "dve"
    S16_DVE = True
    LSLOTS = 4
    DBL256 = False
    PSR = [(0, 1, 2), (4, 5, 6), (3, 7)]
    OT_ALIAS = True
    PSPLIT = True
    GOFF = 0
    SKIPG = False
    GLIM = (8, 5, 9)

    def __init__(self):
        self.nc = bass.Bass("TRN2", target_bir_lowering=False)
        self.st = ExitStack()

    def sb(self, name, shape, dt=F32):
        self.nsb = getattr(self, "nsb", 0) + 1
        return self.cur.enter_context(self.nc.sbuf_tensor("s%d_%s" % (self.nsb, name), shape, dt))

    def phase(self, fn):
        with ExitStack() as ph:
            old, self.cur = self.cur, ph
            fn()
            self.S.barrier()
            self.cur = old

    def din(self, name, shape):
        return self.nc.dram_tensor(name, shape, F32, kind="ExternalInput").ap()

    def dout(self, name, shape):
        return self.nc.dram_tensor(name, shape, F32, kind="ExternalOutput").ap()

    def _banks(self):
        r = self.psrange
        if isinstance(r, dict):
            return r.get(self.S.tag, r["*"])
        return r

    def ps(self):
        banks = self._banks()
        k = self.psc.get(banks, 0)
        i = banks[k % len(banks)]
        self.psc[banks] = k + 1
        return PV(self.pst[i // 2], (i % 2) * 512), self.psb[i]

    def ps2(self):
        banks = self._banks()
        if len(banks) == 8:
            k = self.psc.get("pair", 0)
            self.psc["pair"] = k + 1
            i = 2 * (k % 4)
        else:
            i = banks[0]
            assert i % 2 == 0 and banks[1] == i + 1
        return self.pst[i // 2], [self.psb[i], self.psb[i + 1]]

    def mm(self, out, lhsT, rhs, start, stop, reads, writes, signal=None, sgc=False):
        if signal is None:
            signal = stop
        d = 0.064 + _fsz(rhs) / 2400.0
        if lhsT.dtype == F32:
            d *= 4
        if sgc:
            fn = lambda e, o=out, l=lhsT, r=rhs, a=start, b=stop: e.matmul(o, lhsT=l, rhs=r, start=a, stop=b, skip_group_check=True)
        else:
            fn = lambda e, o=out, l=lhsT, r=rhs, a=start, b=stop: e.matmul(o, lhsT=l, rhs=r, start=a, stop=b)
        self.S.op("pe", fn, reads=reads, writes=writes, signal=signal, dur=d)

    def tr(self, out, in_, ident, reads, writes, signal=True):
        self.S.op("pe", lambda e, o=out, i=in_, d=ident: e.transpose(out=o, in_=i, identity=d),
                  reads=reads, writes=writes, signal=signal, dur=0.12)

    def act(self, out, in_, func, reads, writes, scale=1.0, bias=0.0, accum_out=None):
        def fn(e, o=out, i=in_, f=func, s=scale, b=bias, a=accum_out):
            kw = {}
            if a is not None:
                kw["accum_out"] = a
            return e.activation(out=o, in_=i, func=f, bias=b, scale=s, **kw)
        tab = {AF.Tanh: 0, AF.Ln: 6, AF.Sigmoid: 2, AF.Sqrt: 3}.get(func)
        self.S.op("act", fn, reads=reads, writes=writes, dur=0.25 + _fsz(out) / 1200.0, tab=tab)

    def tt(self, eng, out, in0, in1, op, reads, writes):
        self.S.op(eng, lambda e, o=out, a=in0, b=in1, p=op: e.tensor_tensor(out=o, in0=a, in1=b, op=p),
                  reads=reads, writes=writes, dur=self.edur(eng, out))

    def ts(self, eng, out, in0, s1, s2, op0, op1, reads, writes):
        if s2 is None:
            self.S.op(eng, lambda e, o=out, a=in0, x=s1, p=op0: e.tensor_scalar(out=o, in0=a, scalar1=x, scalar2=None, op0=p),
                      reads=reads, writes=writes, dur=self.edur(eng, out))
        else:
            self.S.op(eng, lambda e, o=out, a=in0, x=s1, y=s2, p=op0, q=op1:
                      e.tensor_scalar(out=o, in0=a, scalar1=x, scalar2=y, op0=p, op1=q), reads=reads, writes=writes,
                      dur=self.edur(eng, out))

    def stt(self, out, in0, scalar, in1, op0, op1, reads, writes):
        self.S.op("dve", lambda e, o=out, a=in0, s=scalar, b=in1, p=op0, q=op1:
                  e.scalar_tensor_tensor(out=o, in0=a, scalar=s, in1=b, op0=p, op1=q), reads=reads, writes=writes,
                  dur=self.edur("dve", out))

    def cp(self, eng, out, in_, reads, writes):
        if eng == "act":
            self.S.op("act", lambda e, o=out, i=in_: e.copy(out=o, in_=i), reads=reads, writes=writes,
                      dur=0.25 + _fsz(out) / 1200.0)
        else:
            self.S.op(eng, lambda e, o=out, i=in_: e.tensor_copy(out=o, in_=i), reads=reads, writes=writes,
                      dur=self.edur(eng, out))

    def edur(self, eng, out):
        n = _fsz(out)
        return (0.12 + n / 960.0) if eng == "dve" else (0.2 + n / 500.0)

    def memset(self, eng, ap, val, writes):
        self.S.op(eng, lambda e, a=ap, v=val: e.memset(a, v), writes=writes, dur=self.edur(eng, ap))

    def asel(self, out, in_, pattern, op, base, cm, reads, writes):
        self.S.op("pool", lambda e, o=out, i=in_, p=pattern, c=op, b=base, m=cm:
                  e.affine_select(out=o, in_=i, pattern=p, compare_op=c, fill=0.0, base=b, channel_multiplier=m),
                  reads=reads, writes=writes)

    def rstd(self, col, scale, reads_writes, eps=EPS):
        self.act(col, col, AF.Ln, reads_writes, reads_writes, scale=scale, bias=eps)
        self.act(col, col, AF.Exp, reads_writes, reads_writes, scale=-0.5)

    def build(self):
        nc, st = self.nc, self.st
        with st:
            self.S = S = Sched(nc, st)
            self._io()
            self.pst = [st.enter_context(nc.psum_tensor("ps%d" % i, [128, 1024], F32)) for i in range(4)]
            self.psb = [S.buf("ps") for _ in range(8)]
            for b in self.psb:
                b.excl = True
            self.psrange = tuple(range(8))
            self.psc = {}
            self.cur = st
            stop = self.STOP
            self._consts()
            if stop >= 1:
                self.phase(lambda: (self._phase_a(), self._gdn_scalars()))
            if stop >= 3 and not self.SKIPG:
                self.phase(self._phase_gdn)
            self.lru_out = self.sb("lru_out", [128, KC, TT], BF16)
            if stop >= 4:
                self.phase(self._phase_lru)
            self.merged = self.sb("merged", [128, KC, TT], BF16)
            self.mgb = S.buf("merged")
            if stop >= 6:
                self.phase(self._phase_merge)
            if stop >= 7:
                self.phase(self._phase_out)
            S.emit()
        return nc

    def _io(self):
        i, o = self.din, self.dout
        self.xp, self.xs = i("xp", [2048, D]), i("xs", [128, D])
        self.slc, self.slh = i("slc", [48, D]), i("slh", [16, D])
        self.sgc, self.sgS = i("sgc", [48, 3072]), i("sgS", [16, 8, 128, 128])
        self.vecs_d, self.npost_d = i("vecs", [128, NV]), i("npost", [1, D])
        self.alog_d, self.dtb_d = i("alog", [1, 8]), i("dtb", [1, 8])
        self.winr, self.wba_d = i("winr", [64, 128, KC, 128]), i("wba", [128, KC, 16])
        self.wa_d, self.wx_d = i("wa", [128, 8, 128]), i("wx", [128, 8, 128])
        self.wbl_d, self.wbg_d = i("wbl", [8, 128, KC, 128]), i("wbg", [8, 128, KC, 128])
        self.wout_d = i("wout", [128, KC, D])
        self.yp, self.ys = o("yp", [2048, D]), o("ys", [128, D])
        self.o_plc, self.o_plh = o("o_plc", [3, D]), o("o_plh", [1, D])
        self.o_pgc, self.o_pgs = o("o_pgc", [3, 3072]), o("o_pgs", [8, 128, 128])
        self.o_slc, self.o_slh = o("o_slc", [48, D]), o("o_slh", [16, D])
        self.o_sgc, self.o_sgs = o("o_sgc", [48, 3072]), o("o_sgs", [16, 8, 128, 128])

    def xsrc(self, i):
        return self.xp[i * 128:(i + 1) * 128, :] if i < 16 else self.xs

    def ydst(self, i):
        return self.yp[i * 128:(i + 1) * 128, :] if i < 16 else self.ys

    def _consts(self):
        S, sb = self.S, self.sb
        cb = self.cb = S.buf("const")
        self.vecs = sb("vecs", [128, NV])
        S.dma(self.vecs[:], self.vecs_d, writes=[cb])
        self.alog = sb("alog", [128, 8])
        self.dtb = sb("dtbs", [128, 8])
        S.dma(self.alog[:], self.alog_d.broadcast_to([128, 8]), writes=[cb])
        S.dma(self.dtb[:], self.dtb_d.broadcast_to([128, 8]), writes=[cb])
        self.idf = sb("idf", [128, 128])
        self.idb = sb("idb", [128, 128], BF16)
        self.memset("pool", self.idf[:], 1.0, [cb])
        self.asel(self.idf[:], self.idf[:], [[-1, 128]], ALU.is_equal, 0, 1, [cb], [cb])
        self.cp("pool", self.idb[:], self.idf[:], [cb], [cb])
        self.onesb = sb("onesb", [128, 128], BF16)
        self.memset("pool", self.onesb[:], 1.0, [cb])
        self.onesf = sb("onesf", [128, 128])
        self.memset("pool", self.onesf[:], 1.0, [cb])
        self.uincl = sb("uincl", [128, 128])
        self.ustr = sb("ustr", [128, 128])
        self.lstr = sb("lstr", [128, 128])
        for t, pat, op, cm in ((self.uincl, [[1, 128]], ALU.is_ge, -1), (self.ustr, [[1, 128]], ALU.is_gt, -1),
                               (self.lstr, [[-1, 128]], ALU.is_gt, 1)):
            self.memset("pool", t[:], 1.0, [cb])
            self.asel(t[:], t[:], pat, op, 0, cm, [cb], [cb])
        self.bd64 = sb("bd64", [128, 128], BF16)
        self.offur = sb("offur", [128, 128], BF16)
        self.memset("pool", self.bd64[:], 0.0, [cb])
        self.memset("pool", self.bd64[0:64, 0:64], 1.0, [cb])
        self.memset("pool", self.bd64[64:128, 64:128], 1.0, [cb])
        self.memset("pool", self.offur[:], 0.0, [cb])
        self.memset("pool", self.offur[0:64, 64:128], 1.0, [cb])
        self.usbd = sb("usbd", [128, 128])
        self.offf = sb("offf", [128, 128])
        self.cp("pool", self.offf[:], self.offur[:], [cb], [cb])
        self.cp("pool", self.usbd[:], self.bd64[:], [cb], [cb])
        self.tt("pool", self.usbd[:], self.usbd[:], self.ustr[:], ALU.mult, [cb], [cb])
        self.selT = sb("selT", [16, 128])
        self.memset("pool", self.selT[:], 1.0, [cb])
        self.asel(self.selT[:], self.selT[:], [[1, 128]], ALU.is_ge, 0, -8, [cb], [cb])
        self.asel(self.selT[:], self.selT[:], [[-1, 128]], ALU.is_ge, 7, 8, [cb], [cb])
        self.selc = sb("selc", [128, 16])
        self.memset("pool", self.selc[:], 1.0, [cb])
        self.asel(self.selc[:], self.selc[:], [[-8, 16]], ALU.is_ge, 0, 1, [cb], [cb])
        self.asel(self.selc[:], self.selc[:], [[8, 16]], ALU.is_ge, 7, -1, [cb], [cb])
        pt, pb = self.ps()
        self.mm(pt[:, 0:128], self.selT[:], self.selT[:], True, True, [cb], [pb])
        self.blk = sb("blk", [128, 128])
        self.cp("dve", self.blk[:], pt[:, 0:128], [pb], [cb])
        self.uincl_b, self.ustr_b, self.lstr_b = sb("uincl_b", [128, 128]), sb("ustr_b", [128, 128]), sb("lstr_b", [128, 128])
        for a, b in ((self.uincl_b, self.uincl), (self.ustr_b, self.ustr), (self.lstr_b, self.lstr)):
            self.tt("pool", a[:], b[:], self.blk[:], ALU.mult, [cb], [cb])
        self.clru = sb("clru", [128, 8])
        self.act(self.clru[:], self.vecs[:, V_LAL:V_LAL + 8], AF.Exp, [cb], [cb], scale=-1.0)
        self.act(self.clru[:], self.clru[:], AF.Ln, [cb], [cb], bias=1.0)
        self.ts("dve", self.clru[:], self.clru[:], -8.0, None, ALU.mult, None, [cb], [cb])
        self.gnwh = sb("gnwh", [128, 1])
        self.ts("dve", self.gnwh[:], self.vecs[:, V_GNW:V_GNW + 1], 0.5, None, ALU.mult, None, [cb], [cb])
        self.hb = sb("hb", [128, 16])
        self.ts("dve", self.hb[:], self.vecs[:, V_LBA:V_LBA + 16], 0.5, None, ALU.mult, None, [cb], [cb])
        self.hc = sb("hc", [128, 8])
        self.ts("dve", self.hc[:], self.clru[:], 0.5, None, ALU.mult, None, [cb], [cb])
        self.negA = sb("negA", [128, 8])
        self.act(self.negA[:], self.alog[:], AF.Exp, [cb], [cb])
        self.ts("dve", self.negA[:], self.negA[:], -1.0, None, ALU.mult, None, [cb], [cb])
        self.lcT = sb("lcT", [128, 8, 48])
        self.h0T = sb("h0T", [128, 8, 16])
        self.gcT = sb("gcT", [128, 24, 48])
        self.xn = sb("xn", [128, KC, TT], BF16)
        self.xnb = [S.buf("xn") for _ in range(NT)]
        self.gdn_out = sb("gdn_out", [128, KC, TT], BF16)
        for nm_, shp in (("betah", [128, NT, 8]), ("beta", [128, NT, 8]), ("nbeta", [128, NT, 8]), ("gtok", [128, NT, 8]), ("egc", [128, NT, 8]),
                         ("kdc", [128, NT, 8]), ("eglb", [128, 16, 8]), ("eglS", [128, 16, 8])):
            setattr(self, nm_, sb(nm_, shp))
        self.stg_lc = sb("stg_lc", [128, 8, 51])
        self.stg_lh = sb("stg_lh", [128, 8, 17])
        self.stg_gc = sb("stg_gc", [128, 24, 51])

    def _load_states(self):
        S, sb = self.S, self.sb
        cb = self.cb
        tmp = sb("st_tmp", [48, 3072 + D])
        tmp2 = sb("st_tmp2", [16, D])
        tb = S.buf("sttmp")
        S.dma(tmp[:, 0:D], self.slc, writes=[tb])
        S.dma(tmp[:, D:D + 3072], self.sgc, writes=[tb])
        S.dma(tmp2[:], self.slh, writes=[tb])
        for g in range(32):
            pt, pb = self.ps()
            self.tr(pt[:, 0:48], tmp[:, g * 128:(g + 1) * 128], self.idf[0:48, 0:48], [tb, cb], [pb])
            dst = self.lcT[:, g, :] if g < 8 else self.gcT[:, g - 8, :]
            self.cp("dve", dst, pt[:, 0:48], [pb], [cb])
        for g in range(8):
            pt, pb = self.ps()
            self.tr(pt[:, 0:16], tmp2[:, g * 128:(g + 1) * 128], self.idf[0:16, 0:16], [tb, cb], [pb])
            self.cp("dve", self.h0T[:, g, :], pt[:, 0:16], [pb], [cb])

    def _phase_a(self):
        S, sb = self.S, self.sb
        cb = self.cb
        self._load_states()
        NB = 4
        xt = [sb("xt%d" % i, [128, D]) for i in range(NB)]
        xtb = [S.buf("xt") for _ in range(NB)]
        x16 = [sb("x16_%d" % i, [128, D], BF16) for i in range(NB)]
        x16b = [S.buf("x16") for _ in range(NB)]
        junk = sb("junk", [128, D], BF16)
        jb = S.buf("junk")
        ss = sb("ssA", [128, NT])
        ssb = [S.buf("ss") for _ in range(NT)]
        for i in range(NT):
            s = i % NB
            S.dma(xt[s][:], self.xsrc(i), writes=[xtb[s]])
            self.act(junk[:], xt[s][:], AF.Square, [xtb[s]], [jb, ssb[i]], accum_out=ss[:, i:i + 1])
            self.rstd(ss[:, i:i + 1], 1.0 / D, [ssb[i]])
            self.ts("dve", x16[s][:], xt[s][:], ss[:, i:i + 1], None, ALU.mult, None, [xtb[s], ssb[i]], [x16b[s]])
            pt, pb = self.ps()
            pv = pt[:].bitcast(BF16)
            for k in range(KC):
                self.tr(pv[:, k * 128:(k + 1) * 128], x16[s][:, k * 128:(k + 1) * 128], self.idb[:], [x16b[s], cb], [pb],
                        signal=(k == KC - 1))
            self.tt("dve", self.xn[:, :, i * 128:(i + 1) * 128], pv.rearrange("p (k t) -> p k t", t=128),
                    self.vecs[:, V_NPRE:V_NPRE + 8].unsqueeze(2).broadcast_to([128, KC, 128]), ALU.mult,
                    [pb, cb], [self.xnb[i]])

    def _gdn_scalars(self):
        S, sb = self.S, self.sb
        cb = self.cb
        gb = self.gsb = S.buf("gsc")
        wba = sb("wba16", [128, KC, 16], BF16)
        S.dma(wba[:], self.wba_d, writes=[gb], eng="pool")
        zba = sb("zba", [128, NT, 16])
        pt, pb = self.ps()
        for i in range(NT):
            for k in range(KC):
                self.mm(pt[:, i * 16:(i + 1) * 16], self.xn[:, k, i * 128:(i + 1) * 128], wba[:, k, :], k == 0, k == KC - 1,
                        [self.xnb[i], gb], [pb])
        self.cp("dve", zba[:].rearrange("p t c -> p (t c)"), pt[:, 0:NT * 16], [pb], [gb])
        self.act(self.beta[:], zba[:, :, 0:8], AF.Sigmoid, [gb], [gb])
        self.ts("dve", self.nbeta[:], self.beta[:], -1.0, None, ALU.mult, None, [gb], [gb])
        self.ts("dve", self.betah[:], self.beta[:], 0.5, None, ALU.mult, None, [gb], [gb])
        tmp = sb("gs_tmp", [128, NT, 8])
        self.tt("dve", tmp[:], zba[:, :, 8:16], self.dtb[:].unsqueeze(1).broadcast_to([128, NT, 8]), ALU.add, [gb, cb], [gb])
        self.act(tmp[:], tmp[:], AF.Exp, [gb], [gb])
        self.act(tmp[:], tmp[:], AF.Ln, [gb], [gb], bias=1.0)
        self.tt("dve", self.gtok[:], tmp[:], self.negA[:].unsqueeze(1).broadcast_to([128, NT, 8]), ALU.mult, [gb, cb], [gb])
        g2 = self.gtok[:].rearrange("p t h -> p (t h)")
        pt, pb = self.ps()
        self.mm(pt[:, 0:128], self.uincl[:], g2[:, 0:128], True, True, [gb, cb], [pb])
        self.mm(pt[:, 128:136], self.uincl_b[:], g2[:, 128:136], True, True, [gb, cb], [pb])
        self.act(self.egc[:].rearrange("p t h -> p (t h)"), pt[:, 0:136], AF.Exp, [pb], [gb])
        pt, pb = self.ps()
        self.mm(pt[:, 0:128], self.lstr[:], g2[:, 0:128], True, True, [gb, cb], [pb])
        self.mm(pt[:, 128:136], self.lstr_b[:], g2[:, 128:136], True, True, [gb, cb], [pb])
        self.act(self.kdc[:].rearrange("p t h -> p (t h)"), pt[:, 0:136], AF.Exp, [pb], [gb])
        pt, pb = self.ps()
        self.mm(pt[:, 0:128], self.onesf[:], g2[:, 0:128], True, True, [gb, cb], [pb])
        self.act(self.eglb[:].rearrange("p t h -> p (t h)"), pt[:, 0:128], AF.Exp, [pb], [gb])
        gsel = sb("gsel", [128, 16, 8])
        self.tt("dve", gsel[:], self.gtok[:, 16, :].unsqueeze(1).broadcast_to([128, 16, 8]),
                self.selc[:].unsqueeze(2).broadcast_to([128, 16, 8]), ALU.mult, [gb, cb], [gb])
        pt, pb = self.ps()
        self.mm(pt[:, 0:128], self.onesf[:], gsel[:].rearrange("p s h -> p (s h)"), True, True, [gb, cb], [pb])
        self.act(self.eglS[:].rearrange("p s h -> p (s h)"), pt[:, 0:128], AF.Exp, [pb], [gb])
        self.negegc = None

    def segs(self):
        return [(0, 512, 4), (512, 1024, 4), (1024, 1536, 4), (1536, 2048, 4), (2048, 2176, 1)]

    def proj(self, w, wb, c0, n):
        pt, pb = self.ps()
        tiles = range(c0 // 128, (c0 + n) // 128)
        rd = [self.xnb[t] for t in tiles] + [wb]
        for k in range(KC):
            self.mm(pt[:, 0:n], w[:, k, :], self.xn[:, k, c0:c0 + n], k == 0, k == KC - 1, rd, [pb])
        return pt, pb

    def conv(self, pt, pb, zp, zpb, zprev, zprevb, si, n, wcol, bcol, stateT, out, outb, stg):
        cb = self.cb
        if si < 4:
            self.cp("act", zp[:, 3:3 + n], pt[:, 0:n], [pb], [zpb])
            if si == 0:
                self.memset("pool", zp[:, 0:3], 0.0, [zpb])
            else:
                self.cp("pool", zp[:, 0:3], zprev[:, 512:515], [zprevb], [zpb])
            if si == 3:
                self.cp("pool", stg[:, 0:3], zp[:, 512:515], [zpb], [])
            src = [zp[:, i:i + n] for i in range(4)]
            dst = out[:, 0:n]
        else:
            z3 = zp[:, 0:176].rearrange("p (s t) -> p s t", t=11)
            self.cp("act", z3[:, :, 3:11], pt[:, 0:128].rearrange("p (s t) -> p s t", t=8), [pb], [zpb])
            self.cp("pool", z3[:, :, 0:3], stateT.rearrange("p (s i) -> p s i", i=3), [cb], [zpb])
            self.cp("pool", stg[:, 3:51].rearrange("p (s i) -> p s i", i=3), z3[:, :, 8:11], [zpb], [])
            src = [z3[:, :, i:i + 8] for i in range(4)]
            dst = out[:, 0:128].rearrange("p (s t) -> p s t", t=8)
        if bcol is None:
            self.ts("dve", dst, src[0], wcol(0), None, ALU.mult, None, [zpb, cb], [outb])
        else:
            self.ts("dve", dst, src[0], wcol(0), bcol, ALU.mult, ALU.add, [zpb, cb], [outb])
        for i in range(1, 4):
            self.stt(dst, src[i], wcol(i), dst, ALU.mult, ALU.add, [zpb, cb, outb], [outb])

    def _phase_lru(self):
        S, sb = self.S, self.sb
        cb = self.cb
        V = self.vecs
        wab = S.buf("wa")
        wa16, wx16 = sb("wa32", [128, 8, 128]), sb("wx32", [128, 8, 128])
        S.dma(wa16[:], self.wa_d, writes=[wab])
        S.dma(wx16[:], self.wx_d, writes=[wab])
        wxs = [sb("lwx%d" % i, [128, KC, 128], BF16) for i in range(2)]
        wgs = [sb("lwg%d" % i, [128, KC, 128], BF16) for i in range(2)]
        wxb = [S.buf("lwx") for _ in range(2)]
        wgb = [S.buf("lwg") for _ in range(2)]
        lzc = [sb("lzc%d" % i, [128, 4]) for i in range(2)]
        lzcb = [S.buf("lzc") for _ in range(2)]
        nm = ["xc", "r", "ig", "a", "bb", "h", "sg"]
        W = [{n: sb("l%s%d" % (n, i), [128, 520] if n == "zp" else [128, 512], BF16 if n == "xc16" else F32) for n in nm}
             for i in range(self.LSLOTS)]
        WB = [{n: S.buf("l" + n) for n in nm} for _ in range(self.LSLOTS)]
        cnt = 0
        for g in range(8):
            s = g % 2
            S.dma(wxs[s][:], self.winr[g], writes=[wxb[s]], eng="pool")
            S.dma(wgs[s][:], self.winr[8 + g], writes=[wgb[s]], eng="pool")
            prev = None
            for si, (c0, c1, nch) in enumerate(self.segs()):
                n = c1 - c0
                w, wb_ = W[cnt % self.LSLOTS], WB[cnt % self.LSLOTS]
                pw, pwb = W[(cnt - 1) % self.LSLOTS], WB[(cnt - 1) % self.LSLOTS]
                cnt += 1
                pt, pb = self.proj(wxs[s], wxb[s], c0, n)
                self.conv3(pt, pb, lzc[g % 2], lzcb[g % 2], si, n,
                           lambda i: V[:, V_LCW + i * 8 + g:V_LCW + i * 8 + g + 1],
                           self.lcT[:, g, :], w["xc"], wb_["xc"], self.stg_lc[:, g, :], bcol=V[:, V_LCB + g:V_LCB + g + 1])
                pa, pab = self.ps()
                self.mm(pa[:, 0:n], wa16[:, g, :], w["xc"][:, 0:n], True, True, [wab, wb_["xc"]], [pab])
                px, pxb = self.ps()
                self.mm(px[:, 0:n], wx16[:, g, :], w["xc"][:, 0:n], True, True, [wab, wb_["xc"]], [pxb])
                self.act(w["r"][:, 0:n], pa[:, 0:n], AF.Tanh, [pab, cb], [wb_["r"]], scale=0.5, bias=self.hb[:, g:g + 1])
                self.act(w["ig"][:, 0:n], px[:, 0:n], AF.Tanh, [pxb, cb], [wb_["ig"]], scale=0.5, bias=self.hb[:, 8 + g:9 + g])
                self.act(w["a"][:, 0:n], w["r"][:, 0:n], AF.Exp, [wb_["r"], cb], [wb_["a"]], scale=self.hc[:, g:g + 1],
                         bias=self.hc[:, g:g + 1])
                self.tt("pool", w["r"][:, 0:n], w["a"][:, 0:n], w["a"][:, 0:n], ALU.mult, [wb_["a"]], [wb_["r"]])
                self.act(w["r"][:, 0:n], w["r"][:, 0:n], AF.Sqrt, [wb_["r"]], [wb_["r"]], scale=-1.0, bias=1.0)
                self.ts("pool", w["ig"][:, 0:n], w["ig"][:, 0:n], 0.5, 0.5, ALU.mult, ALU.add, [wb_["ig"]], [wb_["ig"]])
                self.tt("pool", w["bb"][:, 0:n], w["ig"][:, 0:n], w["xc"][:, 0:n], ALU.mult, [wb_["ig"], wb_["xc"]], [wb_["bb"]])
                if si == 0:
                    self.memset("pool", w["r"][:, 0:1], 1.0, [wb_["r"]])
                    self.memset("pool", w["a"][:, 0:1], 0.0, [wb_["a"]])
                self.tt("pool", w["bb"][:, 0:n], w["bb"][:, 0:n], w["r"][:, 0:n], ALU.mult, [wb_["bb"], wb_["r"]], [wb_["bb"]])
                if si == 4:
                    a3 = w["a"][:, 0:128].rearrange("p (s t) -> p s t", t=8)
                    b3 = w["bb"][:, 0:128].rearrange("p (s t) -> p s t", t=8)
                    t3 = w["r"][:, 0:16].unsqueeze(2)
                    self.tt("pool", t3, a3[:, :, 0:1], self.h0T[:, g, :].unsqueeze(2), ALU.mult, [wb_["a"], cb], [wb_["r"]])
                    self.tt("pool", b3[:, :, 0:1], b3[:, :, 0:1], t3, ALU.add, [wb_["bb"], wb_["r"]], [wb_["bb"]])
                    self.memset("pool", a3[:, :, 0:1], 0.0, [wb_["a"]])
                init = 0.0 if si in (0, 4) else prev
                rds = [wb_["a"], wb_["bb"]] + ([prevb] if si in (1, 2, 3) else [])
                S.op("dve", lambda e, o=w["h"][:, 0:n], a=w["a"][:, 0:n], b=w["bb"][:, 0:n], i0=init:
                     e.tensor_tensor_scan(out=o, data0=a, data1=b, initial=i0, op0=ALU.mult, op1=ALU.add),
                     reads=rds, writes=[wb_["h"]], dur=0.12 + 2 * n / 960.0)
                prev, prevb = w["h"][:, n - 1:n], wb_["h"]
                if si == 3:
                    self.cp("pool", self.stg_lh[:, g, 0:1], w["h"][:, 511:512], [wb_["h"]], [])
                if si == 4:
                    self.cp("pool", self.stg_lh[:, g, 1:17].unsqueeze(2),
                            w["h"][:, 0:128].rearrange("p (s t) -> p s t", t=8)[:, :, 7:8], [wb_["h"]], [])
                pg, pgb = self.proj(wgs[s], wgb[s], c0, n)
                self.act(w["sg"][:, 0:n], pg[:, 0:n], AF.Tanh, [pgb], [wb_["sg"]], scale=0.5)
                self.stt(w["sg"][:, 0:n], w["sg"][:, 0:n], 1.0, pg[:, 0:n], ALU.add, ALU.mult, [wb_["sg"], pgb], [wb_["sg"]])
                self.stt(self.lru_out[:, g, c0:c1], w["sg"][:, 0:n], 0.5, w["h"][:, 0:n], ALU.mult, ALU.mult,
                         [wb_["h"], wb_["sg"]], [])

    def _phase_gdn(self):
        S, sb = self.S, self.sb
        NS = 2
        self.g_wq = [[sb("gw%d_%d" % (j, i), [128, KC, 128], BF16) for j in range(4)] for i in range(NS)]
        self.g_wqb = [[S.buf("gw") for j in range(4)] for i in range(NS)]

        def ws(i, nc_):
            nn = nc_ * 128
            d = {}
            d["zc"] = [sb("gzc%d_%d" % (j, i), [128, 4]) for j in range(3)]
            d["cv"] = sb("gcv%d" % i, [128, nn])
            d["cv2"] = sb("gcv2_%d" % i, [128, nn])
            d["sq"] = sb("gsq%d" % i, [128, nn], BF16)
            d["rt"] = sb("grt%d" % i, [128, nn])
            d["th"] = sb("gth%d" % i, [128, nn])
            d["kq"] = sb("gkq%d" % i, [128, nc_, 2, 128], BF16)
            d["vf"] = sb("gvf%d" % i, [128, nn], BF16)
            d["sgate"] = sb("gsg%d" % i, [128, nn], BF16)
            d["ktv"] = sb("gktv%d" % i, [128, 2, nc_, 128], BF16)
            d["gA"] = sb("ggA%d" % i, [128, nc_, 128])
            d["gB"] = sb("ggB%d" % i, [128, nc_, 128])
            if nc_ == 4:
                d["gO"] = sb("ggO%d" % i, [128, nc_, 128])
            if self.OT_ALIAS or nc_ == 1:
                d["ot"] = d["cv2"][:].rearrange("p (c t) -> p c t", t=128)
            else:
                d["ot"] = sb("got%d" % i, [128, nc_, 128])
            d["on"] = sb("gon%d" % i, [128, nc_, 128], BF16)
            d["ss4"] = sb("gss4_%d" % i, [128, 4])
            d["kg"] = sb("gkg%d" % i, [128, nc_, 128], BF16)
            d["kdec"] = sb("gkd%d" % i, [128, nc_, 128], BF16)
            d["aqk"] = sb("gaq%d" % i, [128, nc_, 128], BF16)
            d["py"] = [sb("gpy%d_%d" % (j, i), [128, nc_, 2, 128], BF16) for j in range(2)]
            d["pt"] = [sb("gpt%d_%d" % (j, i), [128, nc_, 128], BF16) for j in range(2)]
            d["ub"] = sb("gub%d" % i, [128, nc_, 128])
            d["wT"] = sb("gwT%d" % i, [128, nc_, 128], BF16)
            if nc_ == 4:
                d["s32"] = sb("s32_%d" % i, [128, 128])
                d["s16"] = sb("s16_%d" % i, [128, 128], BF16)
            d["qs"] = sb("gqs%d" % i, [128, 128])
            d["vn"] = sb("gvn%d" % i, [128, 128], BF16)
            return d
        self.g_W = [ws(i, 4) for i in range(NS)] + [ws(2, 1)]
        keys = ["zc0", "zc1", "zc2", "cv", "cv2", "sq", "rt", "th", "kq", "vf", "sgate", "ktv", "gA", "gB", "gO", "ot", "on", "ss4", "kg",
                "kdec", "aqk", "ub", "wT", "s32", "s16", "qs", "vn"] + \
               ["%s%d_%d" % (a_, j, p) for a_ in ("pyP", "pyY", "pt") for j in range(2) for p in range(2)]
        self.g_WB = [{k: S.buf("g" + k) for k in keys} for _ in range(NS + 1)]
        for di_, d_ in enumerate(self.g_WB):
            if self.OT_ALIAS or di_ == NS:
                d_["cv2"] = d_["ot"]
        self.g_s0 = sb("s0", [128, 16, 128])
        self.g_s0b = S.buf("s0")
        self.g_snew = [sb("snew%d" % i, [128, 4, 128]) for i in range(2)]
        self.g_snewb = [S.buf("snew") for _ in range(2)]
        self.g_kdm = sb("kdm", [128, 16, 128], BF16)
        self.g_wq32 = sb("wq32", [128, 16, 2, 8])
        self.g_wsq = sb("wsq", [128, 2, 128])
        self.g_mb = S.buf("masked")
        gl_h = self.GLIM[0]
        psr = self.PSR
        for h0 in range(0, gl_h, NS):
            hs = list(range(h0, min(h0 + NS, gl_h)))
            gens = [self.gdn_stream(h, h - h0, h - h0, [0, 1, 2, 3]) for h in hs] + [self.gdn_samples(hs, h0)]
            alive = [True] * len(gens)
            while any(alive):
                for gi, g in enumerate(gens):
                    if not alive[gi]:
                        continue
                    self.psrange = psr[gi] if gi < len(hs) else psr[2]
                    try:
                        next(g)
                    except StopIteration:
                        alive[gi] = False
                    self.psrange = tuple(range(8))

    def gdn_samples(self, hs, h0):
        for h in hs:
            yield from self.gdn_stream(h, h - h0, 2, [4])

    def gdn_stream(self, h, wslot, slot, seglist):
        S = self.S
        cb, gb = self.cb, self.gsb
        V = self.vecs
        w, wb_ = self.g_W[slot], self.g_WB[slot]
        wq, wqb = self.g_wq[wslot], self.g_wqb[wslot]
        prompt = (seglist[0] == 0)
        if prompt:
            s32, s16, s32b, s16b = w["s32"], w["s16"], wb_["s32"], wb_["s16"]
        s0, s0b_ = self.g_s0, self.g_s0b
        snew, snewb, kdm, mb = self.g_snew, self.g_snewb, self.g_kdm, self.g_mb
        wq32, wsq = self.g_wq32, self.g_wsq
        gl_h, gl_s, gl_st = self.GLIM
        scale_q = float(128 ** -0.5)
        bP = lambda j, c: wb_["pyP%d_%d" % (j, c // 2)]
        bY = lambda j, c: wb_["pyY%d_%d" % (j, c // 2)]
        bT = lambda j, c: wb_["pt%d_%d" % (j, c // 2)]
        if prompt:
            for j in range(4):
                S.dma(wq[j][:], self.winr[16 + 8 * j + h], writes=[wqb[j]], eng="pool")
            self.memset("pool", s32[:], 0.0, [s32b])
            self.memset("pool", s16[:], 0.0, [s16b])
            yield
        for si, (c0, c1, nch) in enumerate(self.segs()):
            if si not in seglist:
                continue
            n = c1 - c0
            sample = (si == 4)
            if (gl_s == 4 and sample) or (gl_s == 1 and si != 0) or (gl_s == -1 and not sample):
                continue
            um, usm, lm = (self.uincl_b, self.ustr_b, self.lstr_b) if sample else (self.uincl, self.ustr, self.lstr)
            S.tag = "2chunk"
            ti0 = c0 // 128
            bc = lambda t: t[:, ti0:ti0 + nch, h:h + 1].broadcast_to([128, nch, 128])
            cs = slice(0, nch)
            mask3 = lambda m: m[:].unsqueeze(1).broadcast_to([128, nch, 128])
            self.tt("pool", w["gA"][:, cs, :], mask3(um), bc(self.gtok), ALU.mult, [cb, gb], [wb_["gA"]])
            pd, pdb = self.ps()
            for c in range(nch):
                self.mm(pd[:, c * 128:(c + 1) * 128], lm[:], w["gA"][:, c, :], True, True, [cb, wb_["gA"]], [pdb],
                        signal=(c == nch - 1))
            self.act(w["gB"][:, cs, :], pd[:, 0:nch * 128].rearrange("p (c t) -> p c t", t=128), AF.Exp, [pdb], [wb_["gB"]])
            self.tt("pool", w["gA"][:, cs, :], w["gB"][:, cs, :], mask3(um), ALU.mult, [wb_["gB"], cb, wb_["gA"]], [wb_["gA"]])
            if sample:
                self.tt("pool", w["gB"][:, cs, :], w["gB"][:, cs, :], mask3(usm), ALU.mult, [wb_["gB"], cb], [wb_["gB"]])
                self.tt("pool", w["gB"][:, cs, :], w["gB"][:, cs, :], bc(self.nbeta), ALU.mult, [wb_["gB"], gb], [wb_["gB"]])
            else:
                goff = w["gO"][:, cs, :]
                self.tt("pool", w["gB"][:, cs, :], w["gB"][:, cs, :], bc(self.nbeta), ALU.mult, [wb_["gB"], gb], [wb_["gB"]])
                self.tt("pool", goff, w["gB"][:, cs, :], mask3(self.offf), ALU.mult, [wb_["gB"], cb], [wb_["gO"]])
                self.tt("pool", w["gB"][:, cs, :], w["gB"][:, cs, :], mask3(self.usbd), ALU.mult, [wb_["gB"], cb], [wb_["gB"]])
            for j, cvn in ((0, "cv"), (1, "cv2"), (2, "cv")):
                S.tag = "1proj"
                blk = 8 * j + h
                cvt, cvb = w[cvn], wb_[cvn]
                pt, pb = self.proj(wq[j], wqb[j], c0, n)
                self.conv3(pt, pb, w["zc"][j], wb_["zc%d" % j], si, n,
                           lambda i, blk=blk: V[:, V_GCW + i * 24 + blk:V_GCW + i * 24 + blk + 1],
                           self.gcT[:, blk, :], cvt, cvb, self.stg_gc[:, blk, :])
                self.act(w["th"][:, 0:n], cvt[:, 0:n], AF.Tanh, [cvb], [wb_["th"]], scale=0.5)
                if j == 2:
                    self.stt(w["vf"][:, 0:n], w["th"][:, 0:n], 1.0, cvt[:, 0:n], ALU.add, ALU.mult,
                             [wb_["th"], cvb], [wb_["vf"]])
                    yield
                    continue
                self.stt(cvt[:, 0:n], w["th"][:, 0:n], 1.0, cvt[:, 0:n], ALU.add, ALU.mult, [wb_["th"], cvb], [cvb])
                self.tt("pool", w["sq"][:, 0:n], cvt[:, 0:n], cvt[:, 0:n], ALU.mult, [cvb], [wb_["sq"]])
                p2, p2b = self.ps()
                self.mm(p2[:, 0:n], self.onesb[:], w["sq"][:, 0:n], True, True, [cb, wb_["sq"]], [p2b])
                self.act(w["rt"][:, 0:n], p2[:, 0:n], AF.Ln, [p2b], [wb_["rt"]], bias=4.0 * EPS)
                self.act(w["rt"][:, 0:n], w["rt"][:, 0:n], AF.Exp, [wb_["rt"]], [wb_["rt"]], scale=-0.5,
                         bias=(math.log(scale_q) if j == 0 else 0.0))
                dst = w["kq"][:, 0:nch, 1 - j, :]
                src = cvt[:, 0:n].rearrange("p (c t) -> p c t", t=128)
                rt3 = w["rt"][:, 0:n].rearrange("p (c t) -> p c t", t=128)
                self.tt(self.NORM_ENG, dst, src, rt3, ALU.mult, [cvb, wb_["rt"]], [wb_["kq"]])
                yield
            S.tag = "1proj"
            pg, pgb = self.proj(wq[3], wqb[3], c0, n)
            self.act(w["th"][:, 0:n], pg[:, 0:n], AF.Tanh, [pgb], [wb_["th"]], scale=0.5)
            self.stt(w["sgate"][:, 0:n], w["th"][:, 0:n], 1.0, pg[:, 0:n], ALU.add, ALU.mult, [wb_["th"], pgb], [wb_["sgate"]])
            yield
            if gl_st < 2:
                continue
            S.tag = "2chunk"
            ptr, ptrb = self.ps()
            pv = ptr[:].bitcast(BF16)
            for c in range(nch):
                self.tr(pv[:, c * 128:(c + 1) * 128], w["kq"][:, c, 0, :], self.idb[:], [wb_["kq"], cb], [ptrb], signal=False)
            for c in range(nch):
                self.tr(pv[:, 512 + c * 128:512 + (c + 1) * 128], w["vf"][:, c * 128:(c + 1) * 128], self.idb[:],
                        [wb_["vf"], cb], [ptrb], signal=(c == nch - 1))
            self.cp("act", w["ktv"][:, :, cs, :], pv.rearrange("p (a c t) -> p a c t", a=2, t=128)[:, :, cs, :], [ptrb], [wb_["ktv"]])
            KT = w["ktv"][:, 0, cs, :]
            self.tt("pool", w["kg"][:, cs, :], KT, bc(self.egc), ALU.mult, [wb_["ktv"], gb], [wb_["kg"]])
            self.tt("pool", w["kdec"][:, cs, :], KT, bc(self.kdc), ALU.mult, [wb_["ktv"], gb], [wb_["kdec"]])
            if nch == 1:
                pk2, pk2b1 = self.ps()
                pk2b = [pk2b1]
            else:
                pk2, pk2b = self.ps2()
            for c in range(nch):
                self.mm(pk2[:, c * 256:(c + 1) * 256], w["kq"][:, c, 0, :], w["kq"][:, c, :, :].rearrange("p a b -> p (a b)"),
                        True, True, [wb_["kq"]], pk2b, signal=(c == nch - 1))
            pk4 = pk2[:, 0:nch * 256].rearrange("p (c a t) -> p c a t", a=2, t=128)
            self.tt("dve", w["py"][0][:, cs, 0, :], pk4[:, :, 0, :], w["gB"][:, cs, :], ALU.mult, pk2b + [wb_["gB"]], [bP(0, 0)])
            if not sample:
                self.tt("dve", w["on"][:, cs, :], pk4[:, :, 0, :], goff, ALU.mult, pk2b + [wb_["gO"]], [wb_["on"]])
            self.tt("dve", w["aqk"][:, cs, :], pk4[:, :, 1, :], w["gA"][:, cs, :], ALU.mult, pk2b + [wb_["gA"]], [wb_["aqk"]])
            pp, ppb = self.ps()
            ppv = pp[:].bitcast(BF16)
            if not sample:
                for c in range(nch):
                    self.tr(ppv[:, 512 + c * 128:512 + (c + 1) * 128], w["on"][:, c, :], self.idb[:], [wb_["on"], cb], [ppb], signal=False)
            for c in range(nch):
                self.tr(ppv[:, c * 128:(c + 1) * 128], w["py"][0][:, c, 0, :], self.idb[:], [bP(0, 0), cb], [ppb],
                        signal=(c == nch - 1))
            self.cp("act", w["pt"][0][:, cs, :], ppv[:, 0:nch * 128].rearrange("p (c t) -> p c t", t=128), [ppb], [bT(0, 0)])
            if not sample:
                self.cp("act", w["ktv"][:, 0, cs, :], ppv[:, 512:512 + nch * 128].rearrange("p (c t) -> p c t", t=128), [ppb], [wb_["ktv"]])
            self.tt("pool", w["py"][1][:, cs, 1, :], w["py"][0][:, cs, 0, :], mask3(self.idb), ALU.add,
                    [bP(0, 0), cb], [bY(1, 0)])
            yield
            if gl_st < 3:
                continue
            S.tag = "3dbl"
            nit = 3 if sample else 6
            pX, pXb = self.ps()
            pB, pBb = self.ps()
            for c in range(nch):
                self.mm(pX[:, c * 128:(c + 1) * 128], w["pt"][0][:, c, :], w["py"][0][:, c, 0, :], True, True,
                        [bT(0, 0), bP(0, 0)], [pXb], signal=(c == nch - 1))
            for c in range(nch):
                self.mm(pB[:, c * 128:(c + 1) * 128], w["py"][0][:, c, 0, :], w["pt"][0][:, c, :], True, True,
                        [bT(0, 0), bP(0, 0)], [pBb], signal=(c == nch - 1))
            v3 = lambda p_: p_[:, 0:nch * 128].rearrange("p (c t) -> p c t", t=128)
            self.cp("act", w["py"][1][:, cs, 0, :], v3(pX), [pXb], [bP(1, 0)])
            self.cp("act", w["pt"][1][:, cs, :], v3(pB), [pBb], [bT(1, 0)])
            yield
            for k in range(2, nit + 1):
                rd, wr = (k - 1) % 2, k % 2
                last = (k == nit)
                rdb = [bT(rd, 0), bP(rd, 0), bY(rd, 0)]
                if last or nch == 1 or not self.DBL256:
                    pZ, pZb = self.ps()
                    for c in range(nch):
                        self.mm(pZ[:, c * 128:(c + 1) * 128], w["pt"][rd][:, c, :], w["py"][rd][:, c, 1, :], True, True,
                                rdb, [pZb], signal=(c == nch - 1))
                    self.tt("dve", w["py"][wr][:, cs, 1, :], v3(pZ), w["py"][rd][:, cs, 1, :], ALU.add, [pZb, bY(rd, 0)], [bY(wr, 0)])
                    if not last:
                        pX, pXb = self.ps()
                        for c in range(nch):
                            self.mm(pX[:, c * 128:(c + 1) * 128], w["pt"][rd][:, c, :], w["py"][rd][:, c, 0, :], True, True,
                                    rdb, [pXb], signal=(c == nch - 1))
                        self.cp("act", w["py"][wr][:, cs, 0, :], v3(pX), [pXb], [bP(wr, 0)])
                else:
                    pA2, pA2b = self.ps2()
                    for c in range(nch):
                        self.mm(pA2[:, c * 256:(c + 1) * 256], w["pt"][rd][:, c, :], w["py"][rd][:, c, :, :].rearrange("p a b -> p (a b)"),
                                True, True, rdb, pA2b, signal=(c == nch - 1))
                    p4 = pA2[:, 0:nch * 256].rearrange("p (c a t) -> p c a t", a=2, t=128)
                    self.tt("dve", w["py"][wr][:, cs, 1, :], p4[:, :, 1, :], w["py"][rd][:, cs, 1, :], ALU.add, pA2b + [bY(rd, 0)], [bY(wr, 0)])
                    self.cp("act", w["py"][wr][:, cs, 0, :], p4[:, :, 0, :], pA2b, [bP(wr, 0)])
                if not last:
                    pB, pBb = self.ps()
                    for c in range(nch):
                        self.mm(pB[:, c * 128:(c + 1) * 128], w["py"][rd][:, c, 0, :], w["pt"][rd][:, c, :], True, True,
                                rdb, [pBb], signal=(c == nch - 1))
                    self.cp("act", w["pt"][wr][:, cs, :], v3(pB), [pBb], [bT(wr, 0)])
                yield
            if gl_st < 4:
                continue
            yb = nit % 2
            S.tag = "4ubw"
            if sample:
                pu, pub = self.ps()
                pw2, pw2b = self.ps()
                for c in range(nch):
                    self.mm(pu[:, c * 128:(c + 1) * 128], w["py"][yb][:, c, 1, :], w["ktv"][:, 1, c, :], True, True,
                            [bY(yb, 0), wb_["ktv"]], [pub], signal=(c == nch - 1))
                for c in range(nch):
                    self.mm(pw2[:, c * 128:(c + 1) * 128], w["kg"][:, c, :], w["py"][yb][:, c, 1, :], True, True,
                            [bY(yb, 0), wb_["kg"]], [pw2b], signal=(c == nch - 1))
            else:
                Yd = lambda c: w["py"][yb][:, c, 1, :]
                sq3 = w["sq"][:, 0:n].rearrange("p (c t) -> p c t", t=128)
                vf3 = w["vf"][:, 0:n].rearrange("p (c t) -> p c t", t=128)
                pz_, pzb_ = self.ps()
                for c in range(nch):
                    self.mm(pz_[:, c * 128:(c + 1) * 128], w["ktv"][:, 0, c, :], Yd(c), True, True, [wb_["ktv"], bY(yb, 0)], [pzb_],
                            signal=(c == nch - 1))
                self.cp("act", w["on"][:, cs, :], v3(pz_), [pzb_], [wb_["on"]])
                pu, pub = self.ps()
                for c in range(nch):
                    self.mm(pu[:, c * 128:(c + 1) * 128], Yd(c), w["ktv"][:, 1, c, :], c == 0, False, [bY(yb, 0), wb_["ktv"]], [pub],
                            signal=(c == nch - 1), sgc=True)
                self.cp("act", sq3, v3(pu), [pub], [wb_["sq"]])
                pw0, pw0b = self.ps()
                for c in range(nch):
                    self.mm(pw0[:, c * 128:(c + 1) * 128], Yd(c), w["kg"][:, c, :], True, True, [bY(yb, 0), wb_["kg"]], [pw0b],
                            signal=(c == nch - 1))
                self.cp("act", vf3, v3(pw0), [pw0b], [wb_["vf"]])
                for c in range(nch):
                    self.mm(pu[:, c * 128:(c + 1) * 128], w["on"][:, c, :], sq3[:, c, :], False, True, [wb_["on"], wb_["sq"]], [pub],
                            signal=(c == nch - 1), sgc=True)
                pw2, pw2b = self.ps()
                for c in range(nch):
                    self.mm(pw2[:, c * 128:(c + 1) * 128], w["kg"][:, c, :], Yd(c), c == 0, False, [bY(yb, 0), wb_["kg"]], [pw2b],
                            signal=False, sgc=True)
                for c in range(nch):
                    self.mm(pw2[:, c * 128:(c + 1) * 128], vf3[:, c, :], w["on"][:, c, :], False, True, [wb_["vf"], wb_["on"]], [pw2b],
                            signal=(c == nch - 1), sgc=True)
            self.tt("dve", w["ub"][:, cs, :], v3(pu), bc(self.betah), ALU.mult, [pub, gb], [wb_["ub"]])
            if sample:
                self.cp("act", wq32[:, :, 0, :], pw2[:, 0:128].rearrange("p (s t) -> p s t", t=8), [pw2b], [mb])
            else:
                self.cp("act", w["wT"][:, cs, :], v3(pw2), [pw2b], [wb_["wT"]])
                yield
            if gl_st < 5:
                continue
            if sample:
                S.tag = "5samp"
                S.dma(s0[:], self.sgS[:, h, :, :].rearrange("s k v -> k s v"), writes=[s0b_])
                self.tt("pool", kdm[:], w["kdec"][:, 0, :].unsqueeze(1).broadcast_to([128, 16, 128]),
                        self.selc[:].unsqueeze(2).broadcast_to([128, 16, 128]), ALU.mult, [wb_["kdec"], cb], [mb])
                self.cp("pool", wq32[:, :, 1, :], w["kq"][:, 0, 1, :].rearrange("p (s t) -> p s t", t=8), [wb_["kq"]], [mb])
            if gl_st < 6:
                continue
            po, pob = self.ps()
            for c in range(nch):
                S.tag = "6chain" if not sample else "6chainS"
                ti = c0 // 128 + c
                vn, qss = w["vn"], w["qs"]
                col = lambda t, ti=ti: t[:, ti, h:h + 1]
                if not sample:
                    pw_, pwb_ = self.ps()
                    pq_, pqb_ = self.ps()
                    self.mm(pw_[:, 0:128], w["wT"][:, c, :], s16[:], True, True, [wb_["wT"], s16b], [pwb_])
                    self.mm(pq_[:, 0:128], w["kq"][:, c, 1, :], s16[:], True, True, [wb_["kq"], s16b], [pqb_])
                    pwv, pqv = pw_[:, 0:128], pq_[:, 0:128]
                    pqb2 = pqb_
                else:
                    pws, pwsb = self.ps()
                    for s_ in range(16):
                        self.mm(pws[:, s_ * 16:(s_ + 1) * 16], s0[:, s_, :], wq32[:, s_, :, :].rearrange("p a t -> p (a t)"),
                                True, True, [s0b_, mb], [pwsb], signal=(s_ == 15))
                    self.cp("act", wsq[:].rearrange("p a (s t) -> p a s t", t=8),
                            pws[:, 0:256].rearrange("p (s a t) -> p a s t", a=2, t=8), [pwsb], [mb])
                    pw_, pwb_ = self.ps()
                    self.tr(pw_[:, 0:128], wsq[:, 0, :], self.idf[:], [mb, cb], [pwb_], signal=False)
                    self.tr(pw_[:, 128:256], wsq[:, 1, :], self.idf[:], [mb, cb], [pwb_])
                    pwv, pqv = pw_[:, 0:128], pw_[:, 128:256]
                    pqb2 = pwb_
                self.stt(vn[:], pwv, col(self.nbeta), w["ub"][:, c, :], ALU.mult, ALU.add,
                         [pwb_, gb, wb_["ub"]], [wb_["vn"]])
                self.act(qss[:], pqv, AF.Copy, [pqb2, gb], [wb_["qs"]], scale=col(self.egc))
                self.mm(po[:, c * 128:(c + 1) * 128], w["aqk"][:, c, :], vn[:], True, True, [wb_["aqk"], wb_["vn"]], [pob])
                self.tt("dve", w["ot"][:, c, :], po[:, c * 128:(c + 1) * 128], qss[:], ALU.add, [pob, wb_["qs"]], [wb_["ot"]])
                if not sample:
                    pkv, pkvb = self.ps()
                    self.mm(pkv[:, 0:128], w["kdec"][:, c, :], vn[:], True, True, [wb_["kdec"], wb_["vn"]], [pkvb])
                    gl = self.eglb[:, ti, h:h + 1]
                    if self.S16_DVE:
                        self.stt(s16[:], s32[:], gl, pkv[:, 0:128], ALU.mult, ALU.add, [s32b, gb, pkvb], [s16b])
                        self.stt(s32[:], s32[:], gl, pkv[:, 0:128], ALU.mult, ALU.add, [s32b, gb, pkvb], [s32b])
                    else:
                        self.stt(s32[:], s32[:], gl, pkv[:, 0:128], ALU.mult, ALU.add, [s32b, gb, pkvb], [s32b])
                        self.cp("pool", s16[:], s32[:], [s32b], [s16b])
                else:
                    for q4 in range(4):
                        pkv, pkvb = self.ps()
                        for u in range(4):
                            s_ = q4 * 4 + u
                            self.mm(pkv[:, u * 128:(u + 1) * 128], kdm[:, s_, :], vn[:], True, True, [mb, wb_["vn"]], [pkvb],
                                    signal=(u == 3))
                        for u in range(4):
                            s_ = q4 * 4 + u
                            self.stt(snew[q4 % 2][:, u, :], s0[:, s_, :], self.eglS[:, s_, h:h + 1], pkv[:, u * 128:(u + 1) * 128],
                                     ALU.mult, ALU.add, [s0b_, gb, pkvb], [snewb[q4 % 2]])
                        S.dma(self.o_sgs[q4 * 4:(q4 + 1) * 4, h, :, :].rearrange("s k v -> k s v"), snew[q4 % 2][:],
                              reads=[snewb[q4 % 2]], is_output=True)
                yield
            S.tag = "7opath"
            for c in range(nch):
                self.act(w["on"][:, c, :], w["ot"][:, c, :], AF.Square, [wb_["ot"]], [wb_["on"], wb_["ss4"]],
                         accum_out=w["ss4"][:, c:c + 1])
            self.rstd(w["ss4"][:, cs], 1.0 / 128, [wb_["ss4"]])
            self.tt("dve", w["on"][:, cs, :], w["ot"][:, cs, :], w["ss4"][:, cs].unsqueeze(2).broadcast_to([128, nch, 128]),
                    ALU.mult, [wb_["ot"], wb_["ss4"]], [wb_["on"]])
            pz, pzb = self.ps()
            pzv = pz[:].bitcast(BF16)
            for c in range(nch):
                self.tr(pzv[:, c * 128:(c + 1) * 128], w["on"][:, c, :], self.idb[:], [wb_["on"], cb], [pzb], signal=(c == nch - 1))
            self.stt(self.gdn_out[:, h, c0:c0 + n], pzv[:, 0:n], self.gnwh[:, 0:1], w["sgate"][:, 0:n], ALU.mult, ALU.mult,
                     [pzb, cb, wb_["sgate"]], [])
            yield
        if prompt:
            S.dma(self.o_pgs[h, :, :], s32[:], reads=[s32b], is_output=True)
            yield

    def conv3(self, pt, pb, zc, zcb, si, n, wcol, stateT, out, outb, stg, bcol=0.0):
        cb = self.cb
        if si < 4:
            self.act(out[:, 0:n], pt[:, 0:n], AF.Identity, [pb, cb], [outb], scale=wcol(3), bias=bcol)
            for i in range(3):
                sh = 3 - i
                self.stt(out[:, sh:n], pt[:, 0:n - sh], wcol(i), out[:, sh:n], ALU.mult, ALU.add, [pb, cb, outb], [outb])
                if si > 0:
                    self.stt(out[:, 0:sh], zc[:, i:i + sh], wcol(i), out[:, 0:sh], ALU.mult, ALU.add, [zcb, cb, outb], [outb])
            if si == 3:
                self.cp("dve", stg[:, 0:3], pt[:, n - 3:n], [pb], [])
            else:
                self.cp("dve", zc[:, 0:3], pt[:, n - 3:n], [pb, outb], [zcb])
        else:
            p3 = pt[:, 0:128].rearrange("p (s t) -> p s t", t=8)
            o3 = out[:, 0:128].rearrange("p (s t) -> p s t", t=8)
            st3 = stateT.rearrange("p (s i) -> p s i", i=3)
            self.act(o3, p3, AF.Identity, [pb, cb], [outb], scale=wcol(3), bias=bcol)
            for i in range(3):
                sh = 3 - i
                self.stt(o3[:, :, sh:8], p3[:, :, 0:8 - sh], wcol(i), o3[:, :, sh:8], ALU.mult, ALU.add, [pb, cb, outb], [outb])
                self.stt(o3[:, :, 0:sh], st3[:, :, i:i + sh], wcol(i), o3[:, :, 0:sh], ALU.mult, ALU.add, [cb, outb], [outb])
            self.cp("dve", stg[:, 3:51].rearrange("p (s i) -> p s i", i=3), p3[:, :, 5:8], [pb], [])

    def conv2(self, pt, pb, zp, zpb, zc, zcb, si, n, wcol, bcol, stateT, out, outb, stg):
        cb = self.cb
        if si < 4:
            self.cp("act", zp[:, 3:3 + n], pt[:, 0:n], [pb], [zpb])
            if si == 0:
                self.memset("pool", zp[:, 0:3], 0.0, [zpb])
            else:
                self.cp("pool", zp[:, 0:3], zc[:, 0:3], [zcb], [zpb])
            if si == 3:
                self.cp("pool", stg[:, 0:3], zp[:, 512:515], [zpb], [])
            else:
                self.cp("pool", zc[:, 0:3], zp[:, 512:515], [zpb], [zcb])
            src = [zp[:, i:i + n] for i in range(4)]
            dst = out[:, 0:n]
        else:
            z3 = zp[:, 0:176].rearrange("p (s t) -> p s t", t=11)
            self.cp("act", z3[:, :, 3:11], pt[:, 0:128].rearrange("p (s t) -> p s t", t=8), [pb], [zpb])
            self.cp("pool", z3[:, :, 0:3], stateT.rearrange("p (s i) -> p s i", i=3), [cb], [zpb])
            self.cp("pool", stg[:, 3:51].rearrange("p (s i) -> p s i", i=3), z3[:, :, 8:11], [zpb], [])
            src = [z3[:, :, i:i + 8] for i in range(4)]
            dst = out[:, 0:128].rearrange("p (s t) -> p s t", t=8)
        if bcol is None:
            self.ts("dve", dst, src[0], wcol(0), None, ALU.mult, None, [zpb, cb], [outb])
        else:
            self.ts("dve", dst, src[0], wcol(0), bcol, ALU.mult, ALU.add, [zpb, cb], [outb])
        for i in range(1, 4):
            self.stt(dst, src[i], wcol(i), dst, ALU.mult, ALU.add, [zpb, cb, outb], [outb])

    def _state_outputs(self):
        S, sb = self.S, self.sb
        cb = self.cb
        scr2 = self.gdn_out[:].rearrange("p k t -> p (k t)").bitcast(F32)
        olc, olh, ogc = scr2[0:51, 0:D], scr2[0:17, D:2 * D], scr2[0:51, 2 * D:2 * D + 3072]
        ob = S.buf("ostate")
        for g in range(8):
            pt, pb = self.ps()
            self.tr(pt[0:51, 0:128], self.stg_lc[:, g, :], self.idf[:], [cb], [pb])
            self.cp("dve", olc[:, g * 128:(g + 1) * 128], pt[0:51, 0:128], [pb], [ob])
            pt, pb = self.ps()
            self.tr(pt[0:17, 0:128], self.stg_lh[:, g, :], self.idf[:], [cb], [pb])
            self.cp("dve", olh[:, g * 128:(g + 1) * 128], pt[0:17, 0:128], [pb], [ob])
        for g in range(24):
            pt, pb = self.ps()
            self.tr(pt[0:51, 0:128], self.stg_gc[:, g, :], self.idf[:], [cb], [pb])
            self.cp("dve", ogc[:, g * 128:(g + 1) * 128], pt[0:51, 0:128], [pb], [ob])
        S.dma(self.o_plc, olc[0:3, :], reads=[ob], is_output=True)
        S.dma(self.o_slc, olc[3:51, :], reads=[ob], is_output=True)
        S.dma(self.o_plh, olh[0:1, :], reads=[ob], is_output=True)
        S.dma(self.o_slh, olh[1:17, :], reads=[ob], is_output=True)
        S.dma(self.o_pgc, ogc[0:3, :], reads=[ob], is_output=True)
        S.dma(self.o_sgc, ogc[3:51, :], reads=[ob], is_output=True)

    def _phase_merge(self):
        S, sb = self.S, self.sb
        cb = self.cb
        merged, mgb = self.merged, self.mgb
        wm = [[sb("mw%d_%d" % (j, i), [128, KC, 128], BF16) for j in range(4)] for i in range(2)]
        wmb = [[S.buf("mw") for j in range(4)] for i in range(2)]
        s3 = [sb("ms3_%d" % i, [128, 512]) for i in range(2)]
        s4 = [sb("ms4_%d" % i, [128, 512]) for i in range(2)]
        t1 = [sb("mt1_%d" % i, [128, 512]) for i in range(2)]
        t2 = [sb("mt2_%d" % i, [128, 512]) for i in range(2)]
        mbs = [{k: S.buf("m" + k) for k in ("s3", "s4", "t1", "t2")} for _ in range(2)]
        cnt = 0
        for c in range(8):
            s = c % 2
            srcs = (self.wbl_d[c], self.wbg_d[c], self.winr[48 + c], self.winr[56 + c])
            for j in range(4):
                S.dma(wm[s][j][:], srcs[j], writes=[wmb[s][j]], eng="pool")
            for si, (c0, c1, nch) in enumerate(self.segs()):
                n = c1 - c0
                b = mbs[cnt % 2]
                i2 = cnt % 2
                cnt += 1
                acts = (self.lru_out, self.gdn_out, self.xn, self.xn)
                pp = []
                for j in range(4):
                    pt, pb = self.ps()
                    for k in range(KC):
                        self.mm(pt[:, 0:n], wm[s][j][:, k, :], acts[j][:, k, c0:c1], k == 0, k == KC - 1, [wmb[s][j]], [pb])
                    pp.append((pt, pb))
                self.act(s3[i2][:, 0:n], pp[2][0][:, 0:n], AF.Tanh, [pp[2][1]], [b["s3"]], scale=0.5)
                self.act(s4[i2][:, 0:n], pp[3][0][:, 0:n], AF.Tanh, [pp[3][1]], [b["s4"]], scale=0.5)
                self.stt(t1[i2][:, 0:n], s3[i2][:, 0:n], 1.0, pp[0][0][:, 0:n], ALU.add, ALU.mult, [pp[0][1], b["s3"]], [b["t1"]])
                self.stt(t2[i2][:, 0:n], s4[i2][:, 0:n], 1.0, pp[1][0][:, 0:n], ALU.add, ALU.mult, [pp[1][1], b["s4"]], [b["t2"]])
                self.tt("pool", merged[:, c, c0:c1], t1[i2][:, 0:n], t2[i2][:, 0:n], ALU.add, [b["t1"], b["t2"]], [mgb])

    def _phase_out(self):
        S, sb = self.S, self.sb
        cb = self.cb
        self._state_outputs()
        merged, mgb = self.merged, self.mgb
        wout16 = sb("wout16", [128, KC, D], BF16)
        wob = S.buf("wout")
        S.dma(wout16[:], self.wout_d, writes=[wob], eng="pool")
        npb = sb("npb", [128, D])
        npbb = S.buf("npb")
        S.dma(npb[:], self.npost_d.broadcast_to([128, D]), writes=[npbb])
        NB = 4
        scr = self.xn[:].rearrange("p k t -> p (k t)").bitcast(F32)
        yt = [scr[:, i * D:(i + 1) * D] for i in range(NB)]
        xr = [scr[:, (NB + i) * D:(NB + i + 1) * D] for i in range(NB)]
        ytb = [S.buf("yt") for _ in range(NB)]
        xrb = [S.buf("xr") for _ in range(NB)]
        ss = sb("ssD", [128, NT])
        ssb = [S.buf("ssd") for _ in range(NT)]
        jb = S.buf("junkd")
        junk = sb("junkd", [128, D], BF16)
        for i in range(NT):
            s = i % NB
            S.dma(xr[s][:], self.xsrc(i), writes=[xrb[s]])
            for hf in range(2):
                pt, pb = self.ps()
                for k in range(KC):
                    self.mm(pt[:, 0:512], merged[:, k, i * 128:(i + 1) * 128], wout16[:, k, hf * 512:(hf + 1) * 512],
                            k == 0, k == KC - 1, [mgb, wob], [pb])
                self.cp("act", yt[s][:, hf * 512:(hf + 1) * 512], pt[:, 0:512], [pb], [ytb[s]])
            self.act(junk[:], yt[s][:], AF.Square, [ytb[s]], [jb, ssb[i]], accum_out=ss[:, i:i + 1])
            self.rstd(ss[:, i:i + 1], 1.0 / D, [ssb[i]], eps=4.0 * EPS)
            self.stt(yt[s][:], yt[s][:], ss[:, i:i + 1], npb[:], ALU.mult, ALU.mult, [ytb[s], ssb[i], npbb], [ytb[s]])
            self.tt("pool", yt[s][:], yt[s][:], xr[s][:], ALU.add, [ytb[s], xrb[s]], [ytb[s]])
            S.dma(self.ydst(i), yt[s][:], reads=[ytb[s]], is_output=True)


_NC_CACHE = {}


def _program():
    if "nc" not in _NC_CACHE:
        _NC_CACHE["nc"] = K().build()
    return _NC_CACHE["nc"]


def _f(a):
    return np.ascontiguousarray(a, dtype=np.float32)


def kernel(x_prompt, x_sample, state_lru_conv, state_lru_h, state_gdn_conv, state_gdn_S,
           norm_pre, norm_post, w_in, lru_conv_w, lru_conv_b, lru_wa, lru_ba, lru_wx, lru_bx,
           lru_a_logit, gdn_conv_w, gdn_A_log, gdn_dt_bias, gdn_norm_w, w_br_lru, w_br_gdn, w_out):
    w_in0 = np.asarray(w_in)[0]
    wcat = np.concatenate([w_in0[:, 0:6144], w_in0[:, 6160:8208]], axis=1)
    winr = _f(wcat.reshape(KC, 128, 64, 128).transpose(2, 1, 0, 3))
    wba = _f(w_in0[:, 6144:6160].reshape(KC, 128, 16).transpose(1, 0, 2))
    wa = _f(np.asarray(lru_wa)[0].transpose(1, 0, 2))
    wx = _f(np.asarray(lru_wx)[0].transpose(1, 0, 2))
    wbl = _f(np.asarray(w_br_lru)[0].reshape(KC, 128, 8, 128).transpose(2, 1, 0, 3))
    wbg = _f(np.asarray(w_br_gdn)[0].reshape(KC, 128, 8, 128).transpose(2, 1, 0, 3))
    wout = _f(np.asarray(w_out)[0].reshape(KC, 128, D).transpose(1, 0, 2))
    cols = lambda v, n: np.asarray(v).reshape(n, 128).T
    vecs = _f(np.concatenate([
        cols(norm_pre[0], 8),
        np.asarray(lru_conv_w)[0].reshape(4, 8, 128).transpose(2, 0, 1).reshape(128, 32),
        cols(lru_conv_b[0], 8), cols(lru_ba[0], 8), cols(lru_bx[0], 8), cols(lru_a_logit[0], 8),
        np.asarray(gdn_conv_w)[0].reshape(4, 24, 128).transpose(2, 0, 1).reshape(128, 96),
        np.asarray(gdn_norm_w)[0].reshape(128, 1)], axis=1))
    npost = _f(np.asarray(norm_post).reshape(1, D))
    alog = _f(np.asarray(gdn_A_log).reshape(1, 8))
    dtb = _f(np.asarray(gdn_dt_bias).reshape(1, 8))
    in_maps = []
    for c in range(NCORES):
        sl = slice(16 * c, 16 * c + 16)
        in_maps.append(dict(
            xp=_f(x_prompt[c]), xs=_f(np.asarray(x_sample)[sl].reshape(128, D)),
            slc=_f(np.asarray(state_lru_conv)[0, sl].reshape(48, D)), slh=_f(np.asarray(state_lru_h)[0, sl]),
            sgc=_f(np.asarray(state_gdn_conv)[0, sl].reshape(48, 3072)), sgS=_f(np.asarray(state_gdn_S)[0, sl]),
            vecs=vecs, npost=npost, alog=alog, dtb=dtb, winr=winr, wba=wba, wa=wa, wx=wx, wbl=wbl, wbg=wbg, wout=wout))
    nc = _program()
    res = run_bass_kernel_spmd(nc, in_maps, core_ids=list(range(NCORES)))
    R = res.results
    cat = lambda name: np.stack([np.asarray(R[c][name]) for c in range(NCORES)], axis=0)
    yp = cat("yp")
    ys = cat("ys").reshape(128, 8, D)
    p_lc = cat("o_plc")[None]
    p_lh = cat("o_plh").reshape(1, 8, D)
    p_gc = cat("o_pgc")[None]
    p_gs = cat("o_pgs")[None]
    s_lc = cat("o_slc").reshape(1, 128, 3, D)
    s_lh = cat("o_slh").reshape(1, 128, D)
    s_gc = cat("o_sgc").reshape(1, 128, 3, 3072)
    s_gs = cat("o_sgs").reshape(1, 128, 8, 128, 128)
    return tuple(np.ascontiguousarray(a, dtype=np.float32) for a in (yp, ys, p_lc, p_lh, p_gc, p_gs, s_lc, s_lh, s_gc, s_gs))
```

```python
import math
import numpy as np
import concourse.bass as bass
import concourse.mybir as mybir
from concourse.bass_utils import run_bass_kernel_spmd
from contextlib import ExitStack

F32 = mybir.dt.float32
BF16 = mybir.dt.bfloat16
AF = mybir.ActivationFunctionType
ALU = mybir.AluOpType

NCORES = 8
D = 1024
KC = 8
NT = 17
TT = NT * 128
EPS = 1e-6
NV = 169
V_NPRE, V_LCW, V_LCB, V_LBA, V_LBX, V_LAL, V_GCW, V_GNW = 0, 8, 40, 48, 56, 64, 72, 168


class Buf:
    __slots__ = ("name", "w", "r", "dsem", "dcnt", "excl", "lastdma")

    def __init__(self, name):
        self.name = name
        self.w = None
        self.r = []
        self.dsem = None
        self.dcnt = 0
        self.excl = False
        self.lastdma = None


class Op:
    __slots__ = ("id", "eng", "fn", "deps", "signal", "dur", "tab", "dma", "lat", "epoch", "idx", "unit", "tag")


class Sched:
    ENG = ("pe", "act", "dve", "pool", "sp")
    BLEV = True
    BUCKET = 0.1

    def __init__(self, nc, stack):
        self.nc = nc
        self.stack = stack
        self.sem = {e: stack.enter_context(nc.semaphore("c_" + e)) for e in ("pe", "act", "dve", "pool")}
        self.allops = []
        self.epoch = 0
        self.final = {}
        self.dsems = []
        self.nb = 0

    def buf(self, name="b"):
        self.nb += 1
        return Buf("%s_%d" % (name, self.nb))

    def _record(self, eng, fn, reads, writes, signal, dur, tab, dma=None, lat=0.0):
        ex = [b for b in reads if b.excl]
        if ex:
            reads = [b for b in reads if not b.excl]
            writes = list(writes) + ex
        o = Op()
        o.id = len(self.allops)
        o.eng, o.fn, o.signal, o.dur, o.tab, o.dma, o.lat, o.epoch = eng, fn, signal, dur, tab, dma, lat, self.epoch
        o.tag = getattr(self, "tag", "")
        deps = set()
        for b in reads:
            if b.w is not None:
                deps.add(b.w)
        for b in writes:
            if b.w is not None:
                deps.add(b.w)
            deps.update(b.r)
        o.deps = deps
        for b in reads:
            b.r.append(o.id)
        for b in writes:
            b.w = o.id
            b.r = []
        self.allops.append(o)
        return o

    def op(self, eng, fn, reads=(), writes=(), signal=True, dur=0.3, tab=None):
        self._record(eng, fn, reads, writes, signal, dur, tab)

    def dsem_of(self, buf):
        if buf.dsem is None:
            buf.dsem = self.stack.enter_context(self.nc.semaphore("d_" + buf.name))
            self.dsems.append(buf)
        return buf.dsem

    def dma(self, out, in_, reads=(), writes=(), sembuf=None, eng="sp", is_output=False, nbytes=1 << 20, **kw):
        if sembuf is None:
            sembuf = writes[0] if writes else reads[0]
        self.dsem_of(sembuf)
        o = self._record(eng, lambda e, o=out, i=in_, k=kw: e.dma_start(out=o, in_=i, **k), reads, writes, True,
                         0.15 if eng != "pool" else 1.0, None, dma=(sembuf, is_output), lat=2.0 + nbytes / 2.0e5)
        if sembuf.lastdma is not None:
            o.deps.add(sembuf.lastdma)
        sembuf.lastdma = o.id

    def barrier(self):
        self.epoch += 1

    def _schedule_epoch(self, ops, order, t0):
        units = []
        cur = None
        for o in ops:
            if o.eng == "pe":
                if cur is None:
                    cur = [o]
                else:
                    cur.append(o)
                if o.signal:
                    units.append(cur)
                    cur = None
            else:
                units.append([o])
        assert cur is None, "PE op run without a final signalled op"
        uid = {}
        for ui, u in enumerate(units):
            for o in u:
                uid[o.id] = ui
        nu = len(units)
        first = ops[0].id if ops else 0
        preds = [set() for _ in range(nu)]
        succs = [[] for _ in range(nu)]
        for ui, u in enumerate(units):
            for o in u:
                for d in o.deps:
                    if d >= first and uid[d] != ui:
                        preds[ui].add(uid[d])
        for ui in range(nu):
            for p in preds[ui]:
                succs[p].append(ui)
        indeg = [len(p) for p in preds]
        ready_t = [t0] * nu
        udur = [sum(o.dur for o in u) for u in units]
        ueng = [u[0].eng for u in units]
        blev = [0.0] * nu
        for ui in range(nu - 1, -1, -1):
            m = 0.0
            for sidx in succs[ui]:
                if blev[sidx] > m:
                    m = blev[sidx]
            blev[ui] = udur[ui] + units[ui][-1].lat + 0.3 + m
        rdy = {e: [] for e in self.ENG}
        for ui in range(nu):
            if indeg[ui] == 0:
                rdy[ueng[ui]].append(ui)
        free = {e: t0 for e in self.ENG}
        curtab = {"act": None}
        done = 0
        tend = t0
        while done < nu:
            best = None
            for e in self.ENG:
                lst = rdy[e]
                if not lst:
                    continue
                f = free[e]
                cand = None
                for ui in lst:
                    st = max(f, ready_t[ui])
                    if e == "act":
                        tb = units[ui][0].tab
                        if tb is not None and curtab["act"] is not None and tb != curtab["act"]:
                            st += 1.3
                    key = (int(st / self.BUCKET), -blev[ui], ui) if self.BLEV else (st, ui)
                    if cand is None or key < cand[0]:
                        cand = (key, ui, st)
                if best is None or cand[0] < best[0]:
                    best = (cand[0], cand[1], cand[2], e)
            _, ui, st, e = best
            rdy[e].remove(ui)
            if e == "act" and units[ui][0].tab is not None:
                curtab["act"] = units[ui][0].tab
            fin = st + udur[ui]
            free[e] = fin
            lat = units[ui][-1].lat
            tend = max(tend, fin + lat)
            order[e].extend(units[ui])
            done += 1
            for sidx in succs[ui]:
                extra = 0.12 if ueng[sidx] == e else 0.3
                ready_t[sidx] = max(ready_t[sidx], fin + lat + extra)
                indeg[sidx] -= 1
                if indeg[sidx] == 0:
                    rdy[ueng[sidx]].append(sidx)
        return tend

    def emit(self):
        ops = self.allops
        order = {e: [] for e in self.ENG}
        bounds = []
        t = 0.0
        i = 0
        n = len(ops)
        while i < n:
            j = i
            while j < n and ops[j].epoch == ops[i].epoch:
                j += 1
            t = self._schedule_epoch(ops[i:j], order, t)
            bounds.append({e: len(order[e]) for e in self.ENG})
            i = j
        self.est_us = t
        cnt = {e: 0 for e in self.ENG}
        dmaval = {}
        for e in self.ENG:
            lst = order[e]
            pending = []
            for o in lst:
                if o.dma is not None:
                    sb_, is_out = o.dma
                    sb_.dcnt += 16
                    o.idx = (("d", sb_.name), sb_.dsem, sb_.dcnt)
                    if is_out:
                        self.final[sb_.name] = (sb_.dsem, sb_.dcnt)
                elif o.signal:
                    cnt[e] += 1
                    o.idx = (e, self.sem[e], cnt[e])
                    for p in pending:
                        p.idx = o.idx
                    pending = []
                else:
                    pending.append(o)
            assert not pending
        prog = {e: [] for e in self.ENG}
        waited = {e: {} for e in self.ENG}
        pos = {e: 0 for e in self.ENG}
        for bi, bd in enumerate(bounds):
            for e in self.ENG:
                for o in order[e][pos[e]:bd[e]]:
                    waits = []
                    for d in sorted(o.deps):
                        key, sem, val = ops[d].idx
                        if key == e and e == "pe":
                            continue
                        if waited[e].get(key, 0) >= val:
                            continue
                        waited[e][key] = val
                        waits.append((sem, val))
                    inc = None
                    if o.dma is not None:
                        inc = (o.idx[1], 16)
                    elif o.signal:
                        inc = (self.sem[e], 1)
                    prog[e].append((waits, o.fn, inc))
                pos[e] = bd[e]
            if bi < len(bounds) - 1:
                for e in self.ENG:
                    waits = []
                    for f in ("pe", "act", "dve", "pool"):
                        c = 0
                        for o in order[f][:bd[f]]:
                            if o.dma is None and o.signal:
                                c = max(c, o.idx[2])
                        if f != e and c > 0 and waited[e].get(f, 0) < c:
                            waited[e][f] = c
                            waits.append((self.sem[f], c))
                    for q in ("sp", "pool"):
                        last = {}
                        for o in order[q][:bd[q]]:
                            if o.dma is not None:
                                last[o.idx[0]] = o.idx
                        for key, (k_, sem, val) in last.items():
                            if waited[e].get(key, 0) < val:
                                waited[e][key] = val
                                waits.append((sem, val))
                    if waits:
                        prog[e].append((waits, None, None))
        fw = list(self.final.values())
        if fw:
            prog["sp"].append((fw, None, None))
        self.prog = prog
        with self.nc.Block() as block:
            def mk(name):
                lst = prog[name]

                def body(eng):
                    for waits, fn, inc in lst:
                        for s_, v in waits:
                            eng.wait_ge(s_, v)
                        if fn is not None:
                            ins = fn(eng)
                            if inc is not None:
                                ins.then_inc(*inc)
                return body
            block.tensor(mk("pe"))
            block.scalar(mk("act"))
            block.vector(mk("dve"))
            block.gpsimd(mk("pool"))
            block.sync(mk("sp"))


class PV:
    def __init__(self, t, off):
        self.t, self.off = t, off

    def __getitem__(self, key):
        if isinstance(key, slice):
            return self.t[:, self.off:self.off + 512]
        p, c = key
        return self.t[p, c.start + self.off:c.stop + self.off]


def _fsz(ap):
    n = 1
    for d in ap.shape[1:]:
        n *= int(d)
    return n


class K:
    STOP = 99
    NORM_ENG = "dve"
    S16_DVE = True
    LSLOTS = 4
    DBL256 = False
    PSR = [(0, 1, 2), (4, 5, 6), (3, 7)]
    OT_ALIAS = True
    PSPLIT = True
    GOFF = 0
    SKIPG = False
    GLIM = (8, 5, 9)

    def __init__(self):
        self.nc = bass.Bass("TRN2", target_bir_lowering=False)
        self.st = ExitStack()

    def sb(self, name, shape, dt=F32):
        self.nsb = getattr(self, "nsb", 0) + 1
        return self.cur.enter_context(self.nc.sbuf_tensor("s%d_%s" % (self.nsb, name), shape, dt))

    def phase(self, fn):
        with ExitStack() as ph:
            old, self.cur = self.cur, ph
            fn()
            self.S.barrier()
            self.cur = old

    def din(self, name, shape):
        return self.nc.dram_tensor(name, shape, F32, kind="ExternalInput").ap()

    def dout(self, name, shape):
        return self.nc.dram_tensor(name, shape, F32, kind="ExternalOutput").ap()

    def _banks(self):
        r = self.psrange
        if isinstance(r, dict):
            return r.get(self.S.tag, r["*"])
        return r

    def ps(self):
        banks = self._banks()
        k = self.psc.get(banks, 0)
        i = banks[k % len(banks)]
        self.psc[banks] = k + 1
        return PV(self.pst[i // 2], (i % 2) * 512), self.psb[i]

    def ps2(self):
        banks = self._banks()
        if len(banks) == 8:
            k = self.psc.get("pair", 0)
            self.psc["pair"] = k + 1
            i = 2 * (k % 4)
        else:
            i = banks[0]
            assert i % 2 == 0 and banks[1] == i + 1
        return self.pst[i // 2], [self.psb[i], self.psb[i + 1]]

    def mm(self, out, lhsT, rhs, start, stop, reads, writes, signal=None, sgc=False):
        if signal is None:
            signal = stop
        d = 0.064 + _fsz(rhs) / 2400.0
        if lhsT.dtype == F32:
            d *= 4
        if sgc:
            fn = lambda e, o=out, l=lhsT, r=rhs, a=start, b=stop: e.matmul(o, lhsT=l, rhs=r, start=a, stop=b, skip_group_check=True)
        else:
            fn = lambda e, o=out, l=lhsT, r=rhs, a=start, b=stop: e.matmul(o, lhsT=l, rhs=r, start=a, stop=b)
        self.S.op("pe", fn, reads=reads, writes=writes, signal=signal, dur=d)

    def tr(self, out, in_, ident, reads, writes, signal=True):
        self.S.op("pe", lambda e, o=out, i=in_, d=ident: e.transpose(out=o, in_=i, identity=d),
                  reads=reads, writes=writes, signal=signal, dur=0.12)

    def act(self, out, in_, func, reads, writes, scale=1.0, bias=0.0, accum_out=None):
        def fn(e, o=out, i=in_, f=func, s=scale, b=bias, a=accum_out):
            kw = {}
            if a is not None:
                kw["accum_out"] = a
            return e.activation(out=o, in_=i, func=f, bias=b, scale=s, **kw)
        tab = {AF.Tanh: 0, AF.Ln: 6, AF.Sigmoid: 2, AF.Sqrt: 3}.get(func)
        self.S.op("act", fn, reads=reads, writes=writes, dur=0.25 + _fsz(out) / 1200.0, tab=tab)

    def tt(self, eng, out, in0, in1, op, reads, writes):
        self.S.op(eng, lambda e, o=out, a=in0, b=in1, p=op: e.tensor_tensor(out=o, in0=a, in1=b, op=p),
                  reads=reads, writes=writes, dur=self.edur(eng, out))

    def ts(self, eng, out, in0, s1, s2, op0, op1, reads, writes):
        if s2 is None:
            self.S.op(eng, lambda e, o=out, a=in0, x=s1, p=op0: e.tensor_scalar(out=o, in0=a, scalar1=x, scalar2=None, op0=p),
                      reads=reads, writes=writes, dur=self.edur(eng, out))
        else:
            self.S.op(eng, lambda e, o=out, a=in0, x=s1, y=s2, p=op0, q=op1:
                      e.tensor_scalar(out=o, in0=a, scalar1=x, scalar2=y, op0=p, op1=q), reads=reads, writes=writes,
                      dur=self.edur(eng, out))

    def stt(self, out, in0, scalar, in1, op0, op1, reads, writes):
        self.S.op("dve", lambda e, o=out, a=in0, s=scalar, b=in1, p=op0, q=op1:
                  e.scalar_tensor_tensor(out=o, in0=a, scalar=s, in1=b, op0=p, op1=q), reads=reads, writes=writes,
                  dur=self.edur("dve", out))

    def cp(self, eng, out, in_, reads, writes):
        if eng == "act":
            self.S.op("act", lambda e, o=out, i=in_: e.copy(out=o, in_=i), reads=reads, writes=writes,
                      dur=0.25 + _fsz(out) / 1200.0)
        else:
            self.S.op(eng, lambda e, o=out, i=in_: e.tensor_copy(out=o, in_=i), reads=reads, writes=writes,
                      dur=self.edur(eng, out))

    def edur(self, eng, out):
        n = _fsz(out)
        return (0.12 + n / 960.0) if eng == "dve" else (0.2 + n / 500.0)

    def memset(self, eng, ap, val, writes):
        self.S.op(eng, lambda e, a=ap, v=val: e.memset(a, v), writes=writes, dur=self.edur(eng, ap))

    def asel(self, out, in_, pattern, op, base, cm, reads, writes):
        self.S.op("pool", lambda e, o=out, i=in_, p=pattern, c=op, b=base, m=cm:
                  e.affine_select(out=o, in_=i, pattern=p, compare_op=c, fill=0.0, base=b, channel_multiplier=m),
                  reads=reads, writes=writes)

    def rstd(self, col, scale, reads_writes, eps=EPS):
        self.act(col, col, AF.Ln, reads_writes, reads_writes, scale=scale, bias=eps)
        self.act(col, col, AF.Exp, reads_writes, reads_writes, scale=-0.5)

    def build(self):
        nc, st = self.nc, self.st
        with st:
            self.S = S = Sched(nc, st)
            self._io()
            self.pst = [st.enter_context(nc.psum_tensor("ps%d" % i, [128, 1024], F32)) for i in range(4)]
            self.psb = [S.buf("ps") for _ in range(8)]
            for b in self.psb:
                b.excl = True
            self.psrange = tuple(range(8))
            self.psc = {}
            self.cur = st
            stop = self.STOP
            self._consts()
            if stop >= 1:
                self.phase(lambda: (self._phase_a(), self._gdn_scalars()))
            if stop >= 3 and not self.SKIPG:
                self.phase(self._phase_gdn)
            self.lru_out = self.sb("lru_out", [128, KC, TT], BF16)
            if stop >= 4:
                self.phase(self._phase_lru)
            self.merged = self.sb("merged", [128, KC, TT], BF16)
            self.mgb = S.buf("merged")
            if stop >= 6:
                self.phase(self._phase_merge)
            if stop >= 7:
                self.phase(self._phase_out)
            S.emit()
        return nc

    def _io(self):
        i, o = self.din, self.dout
        self.xp, self.xs = i("xp", [2048, D]), i("xs", [128, D])
        self.slc, self.slh = i("slc", [48, D]), i("slh", [16, D])
        self.sgc, self.sgS = i("sgc", [48, 3072]), i("sgS", [16, 8, 128, 128])
        self.vecs_d, self.npost_d = i("vecs", [128, NV]), i("npost", [1, D])
        self.alog_d, self.dtb_d = i("alog", [1, 8]), i("dtb", [1, 8])
        self.winr, self.wba_d = i("winr", [64, 128, KC, 128]), i("wba", [128, KC, 16])
        self.wa_d, self.wx_d = i("wa", [128, 8, 128]), i("wx", [128, 8, 128])
        self.wbl_d, self.wbg_d = i("wbl", [8, 128, KC, 128]), i("wbg", [8, 128, KC, 128])
        self.wout_d = i("wout", [128, KC, D])
        self.yp, self.ys = o("yp", [2048, D]), o("ys", [128, D])
        self.o_plc, self.o_plh = o("o_plc", [3, D]), o("o_plh", [1, D])
        self.o_pgc, self.o_pgs = o("o_pgc", [3, 3072]), o("o_pgs", [8, 128, 128])
        self.o_slc, self.o_slh = o("o_slc", [48, D]), o("o_slh", [16, D])
        self.o_sgc, self.o_sgs = o("o_sgc", [48, 3072]), o("o_sgs", [16, 8, 128, 128])

    def xsrc(self, i):
        return self.xp[i * 128:(i + 1) * 128, :] if i < 16 else self.xs

    def ydst(self, i):
        return self.yp[i * 128:(i + 1) * 128, :] if i < 16 else self.ys

    def _consts(self):
        S, sb = self.S, self.sb
        cb = self.cb = S.buf("const")
        self.vecs = sb("vecs", [128, NV])
        S.dma(self.vecs[:], self.vecs_d, writes=[cb])
        self.alog = sb("alog", [128, 8])
        self.dtb = sb("dtbs", [128, 8])
        S.dma(self.alog[:], self.alog_d.broadcast_to([128, 8]), writes=[cb])
        S.dma(self.dtb[:], self.dtb_d.broadcast_to([128, 8]), writes=[cb])
        self.idf = sb("idf", [128, 128])
        self.idb = sb("idb", [128, 128], BF16)
        self.memset("pool", self.idf[:], 1.0, [cb])
        self.asel(self.idf[:], self.idf[:], [[-1, 128]], ALU.is_equal, 0, 1, [cb], [cb])
        self.cp("pool", self.idb[:], self.idf[:], [cb], [cb])
        self.onesb = sb("onesb", [128, 128], BF16)
        self.memset("pool", self.onesb[:], 1.0, [cb])
        self.onesf = sb("onesf", [128, 128])
        self.memset("pool", self.onesf[:], 1.0, [cb])
        self.uincl = sb("uincl", [128, 128])
        self.ustr = sb("ustr", [128, 128])
        self.lstr = sb("lstr", [128, 128])
        for t, pat, op, cm in ((self.uincl, [[1, 128]], ALU.is_ge, -1), (self.ustr, [[1, 128]], ALU.is_gt, -1),
                               (self.lstr, [[-1, 128]], ALU.is_gt, 1)):
            self.memset("pool", t[:], 1.0, [cb])
            self.asel(t[:], t[:], pat, op, 0, cm, [cb], [cb])
        self.bd64 = sb("bd64", [128, 128], BF16)
        self.offur = sb("offur", [128, 128], BF16)
        self.memset("pool", self.bd64[:], 0.0, [cb])
        self.memset("pool", self.bd64[0:64, 0:64], 1.0, [cb])
        self.memset("pool", self.bd64[64:128, 64:128], 1.0, [cb])
        self.memset("pool", self.offur[:], 0.0, [cb])
        self.memset("pool", self.offur[0:64, 64:128], 1.0, [cb])
        self.usbd = sb("usbd", [128, 128])
        self.offf = sb("offf", [128, 128])
        self.cp("pool", self.offf[:], self.offur[:], [cb], [cb])
        self.cp("pool", self.usbd[:], self.bd64[:], [cb], [cb])
        self.tt("pool", self.usbd[:], self.usbd[:], self.ustr[:], ALU.mult, [cb], [cb])
        self.selT = sb("selT", [16, 128])
        self.memset("pool", self.selT[:], 1.0, [cb])
        self.asel(self.selT[:], self.selT[:], [[1, 128]], ALU.is_ge, 0, -8, [cb], [cb])
        self.asel(self.selT[:], self.selT[:], [[-1, 128]], ALU.is_ge, 7, 8, [cb], [cb])
        self.selc = sb("selc", [128, 16])
        self.memset("pool", self.selc[:], 1.0, [cb])
        self.asel(self.selc[:], self.selc[:], [[-8, 16]], ALU.is_ge, 0, 1, [cb], [cb])
        self.asel(self.selc[:], self.selc[:], [[8, 16]], ALU.is_ge, 7, -1, [cb], [cb])
        pt, pb = self.ps()
        self.mm(pt[:, 0:128], self.selT[:], self.selT[:], True, True, [cb], [pb])
        self.blk = sb("blk", [128, 128])
        self.cp("dve", self.blk[:], pt[:, 0:128], [pb], [cb])
        self.uincl_b, self.ustr_b, self.lstr_b = sb("uincl_b", [128, 128]), sb("ustr_b", [128, 128]), sb("lstr_b", [128, 128])
        for a, b in ((self.uincl_b, self.uincl), (self.ustr_b, self.ustr), (self.lstr_b, self.lstr)):
            self.tt("pool", a[:], b[:], self.blk[:], ALU.mult, [cb], [cb])
        self.clru = sb("clru", [128, 8])
        self.act(self.clru[:], self.vecs[:, V_LAL:V_LAL + 8], AF.Exp, [cb], [cb], scale=-1.0)
        self.act(self.clru[:], self.clru[:], AF.Ln, [cb], [cb], bias=1.0)
        self.ts("dve", self.clru[:], self.clru[:], -8.0, None, ALU.mult, None, [cb], [cb])
        self.gnwh = sb("gnwh", [128, 1])
        self.ts("dve", self.gnwh[:], self.vecs[:, V_GNW:V_GNW + 1], 0.5, None, ALU.mult, None, [cb], [cb])
        self.hb = sb("hb", [128, 16])
        self.ts("dve", self.hb[:], self.vecs[:, V_LBA:V_LBA + 16], 0.5, None, ALU.mult, None, [cb], [cb])
        self.hc = sb("hc", [128, 8])
        self.ts("dve", self.hc[:], self.clru[:], 0.5, None, ALU.mult, None, [cb], [cb])
        self.negA = sb("negA", [128, 8])
        self.act(self.negA[:], self.alog[:], AF.Exp, [cb], [cb])
        self.ts("dve", self.negA[:], self.negA[:], -1.0, None, ALU.mult, None, [cb], [cb])
        self.lcT = sb("lcT", [128, 8, 48])
        self.h0T = sb("h0T", [128, 8, 16])
        self.gcT = sb("gcT", [128, 24, 48])
        self.xn = sb("xn", [128, KC, TT], BF16)
        self.xnb = [S.buf("xn") for _ in range(NT)]
        self.gdn_out = sb("gdn_out", [128, KC, TT], BF16)
        for nm_, shp in (("betah", [128, NT, 8]), ("beta", [128, NT, 8]), ("nbeta", [128, NT, 8]), ("gtok", [128, NT, 8]), ("egc", [128, NT, 8]),
                         ("kdc", [128, NT, 8]), ("eglb", [128, 16, 8]), ("eglS", [128, 16, 8])):
            setattr(self, nm_, sb(nm_, shp))
        self.stg_lc = sb("stg_lc", [128, 8, 51])
        self.stg_lh = sb("stg_lh", [128, 8, 17])
        self.stg_gc = sb("stg_gc", [128, 24, 51])

    def _load_states(self):
        S, sb = self.S, self.sb
        cb = self.cb
        tmp = sb("st_tmp", [48, 3072 + D])
        tmp2 = sb("st_tmp2", [16, D])
        tb = S.buf("sttmp")
        S.dma(tmp[:, 0:D], self.slc, writes=[tb])
        S.dma(tmp[:, D:D + 3072], self.sgc, writes=[tb])
        S.dma(tmp2[:], self.slh, writes=[tb])
        for g in range(32):
            pt, pb = self.ps()
            self.tr(pt[:, 0:48], tmp[:, g * 128:(g + 1) * 128], self.idf[0:48, 0:48], [tb, cb], [pb])
            dst = self.lcT[:, g, :] if g < 8 else self.gcT[:, g - 8, :]
            self.cp("dve", dst, pt[:, 0:48], [pb], [cb])
        for g in range(8):
            pt, pb = self.ps()
            self.tr(pt[:, 0:16], tmp2[:, g * 128:(g + 1) * 128], self.idf[0:16, 0:16], [tb, cb], [pb])
            self.cp("dve", self.h0T[:, g, :], pt[:, 0:16], [pb], [cb])

    def _phase_a(self):
        S, sb = self.S, self.sb
        cb = self.cb
        self._load_states()
        NB = 4
        xt = [sb("xt%d" % i, [128, D]) for i in range(NB)]
        xtb = [S.buf("xt") for _ in range(NB)]
        x16 = [sb("x16_%d" % i, [128, D], BF16) for i in range(NB)]
        x16b = [S.buf("x16") for _ in range(NB)]
        junk = sb("junk", [128, D], BF16)
        jb = S.buf("junk")
        ss = sb("ssA", [128, NT])
        ssb = [S.buf("ss") for _ in range(NT)]
        for i in range(NT):
            s = i % NB
            S.dma(xt[s][:], self.xsrc(i), writes=[xtb[s]])
            self.act(junk[:], xt[s][:], AF.Square, [xtb[s]], [jb, ssb[i]], accum_out=ss[:, i:i + 1])
            self.rstd(ss[:, i:i + 1], 1.0 / D, [ssb[i]])
            self.ts("dve", x16[s][:], xt[s][:], ss[:, i:i + 1], None, ALU.mult, None, [xtb[s], ssb[i]], [x16b[s]])
            pt, pb = self.ps()
            pv = pt[:].bitcast(BF16)
            for k in range(KC):
                self.tr(pv[:, k * 128:(k + 1) * 128], x16[s][:, k * 128:(k + 1) * 128], self.idb[:], [x16b[s], cb], [pb],
                        signal=(k == KC - 1))
            self.tt("dve", self.xn[:, :, i * 128:(i + 1) * 128], pv.rearrange("p (k t) -> p k t", t=128),
                    self.vecs[:, V_NPRE:V_NPRE + 8].unsqueeze(2).broadcast_to([128, KC, 128]), ALU.mult,
                    [pb, cb], [self.xnb[i]])

    def _gdn_scalars(self):
        S, sb = self.S, self.sb
        cb = self.cb
        gb = self.gsb = S.buf("gsc")
        wba = sb("wba16", [128, KC, 16], BF16)
        S.dma(wba[:], self.wba_d, writes=[gb], eng="pool")
        zba = sb("zba", [128, NT, 16])
        pt, pb = self.ps()
        for i in range(NT):
            for k in range(KC):
                self.mm(pt[:, i * 16:(i + 1) * 16], self.xn[:, k, i * 128:(i + 1) * 128], wba[:, k, :], k == 0, k == KC - 1,
                        [self.xnb[i], gb], [pb])
        self.cp("dve", zba[:].rearrange("p t c -> p (t c)"), pt[:, 0:NT * 16], [pb], [gb])
        self.act(self.beta[:], zba[:, :, 0:8], AF.Sigmoid, [gb], [gb])
        self.ts("dve", self.nbeta[:], self.beta[:], -1.0, None, ALU.mult, None, [gb], [gb])
        self.ts("dve", self.betah[:], self.beta[:], 0.5, None, ALU.mult, None, [gb], [gb])
        tmp = sb("gs_tmp", [128, NT, 8])
        self.tt("dve", tmp[:], zba[:, :, 8:16], self.dtb[:].unsqueeze(1).broadcast_to([128, NT, 8]), ALU.add, [gb, cb], [gb])
        self.act(tmp[:], tmp[:], AF.Exp, [gb], [gb])
        self.act(tmp[:], tmp[:], AF.Ln, [gb], [gb], bias=1.0)
        self.tt("dve", self.gtok[:], tmp[:], self.negA[:].unsqueeze(1).broadcast_to([128, NT, 8]), ALU.mult, [gb, cb], [gb])
        g2 = self.gtok[:].rearrange("p t h -> p (t h)")
        pt, pb = self.ps()
        self.mm(pt[:, 0:128], self.uincl[:], g2[:, 0:128], True, True, [gb, cb], [pb])
        self.mm(pt[:, 128:136], self.uincl_b[:], g2[:, 128:136], True, True, [gb, cb], [pb])
        self.act(self.egc[:].rearrange("p t h -> p (t h)"), pt[:, 0:136], AF.Exp, [pb], [gb])
        pt, pb = self.ps()
        self.mm(pt[:, 0:128], self.lstr[:], g2[:, 0:128], True, True, [gb, cb], [pb])
        self.mm(pt[:, 128:136], self.lstr_b[:], g2[:, 128:136], True, True, [gb, cb], [pb])
        self.act(self.kdc[:].rearrange("p t h -> p (t h)"), pt[:, 0:136], AF.Exp, [pb], [gb])
        pt, pb = self.ps()
        self.mm(pt[:, 0:128], self.onesf[:], g2[:, 0:128], True, True, [gb, cb], [pb])
        self.act(self.eglb[:].rearrange("p t h -> p (t h)"), pt[:, 0:128], AF.Exp, [pb], [gb])
        gsel = sb("gsel", [128, 16, 8])
        self.tt("dve", gsel[:], self.gtok[:, 16, :].unsqueeze(1).broadcast_to([128, 16, 8]),
                self.selc[:].unsqueeze(2).broadcast_to([128, 16, 8]), ALU.mult, [gb, cb], [gb])
        pt, pb = self.ps()
        self.mm(pt[:, 0:128], self.onesf[:], gsel[:].rearrange("p s h -> p (s h)"), True, True, [gb, cb], [pb])
        self.act(self.eglS[:].rearrange("p s h -> p (s h)"), pt[:, 0:128], AF.Exp, [pb], [gb])
        self.negegc = None

    def segs(self):
        return [(0, 512, 4), (512, 1024, 4), (1024, 1536, 4), (1536, 2048, 4), (2048, 2176, 1)]

    def proj(self, w, wb, c0, n):
        pt, pb = self.ps()
        tiles = range(c0 // 128, (c0 + n) // 128)
        rd = [self.xnb[t] for t in tiles] + [wb]
        for k in range(KC):
            self.mm(pt[:, 0:n], w[:, k, :], self.xn[:, k, c0:c0 + n], k == 0, k == KC - 1, rd, [pb])
        return pt, pb

    def conv(self, pt, pb, zp, zpb, zprev, zprevb, si, n, wcol, bcol, stateT, out, outb, stg):
        cb = self.cb
        if si < 4:
            self.cp("act", zp[:, 3:3 + n], pt[:, 0:n], [pb], [zpb])
            if si == 0:
                self.memset("pool", zp[:, 0:3], 0.0, [zpb])
            else:
                self.cp("pool", zp[:, 0:3], zprev[:, 512:515], [zprevb], [zpb])
            if si == 3:
                self.cp("pool", stg[:, 0:3], zp[:, 512:515], [zpb], [])
            src = [zp[:, i:i + n] for i in range(4)]
            dst = out[:, 0:n]
        else:
            z3 = zp[:, 0:176].rearrange("p (s t) -> p s t", t=11)
            self.cp("act", z3[:, :, 3:11], pt[:, 0:128].rearrange("p (s t) -> p s t", t=8), [pb], [zpb])
            self.cp("pool", z3[:, :, 0:3], stateT.rearrange("p (s i) -> p s i", i=3), [cb], [zpb])
            self.cp("pool", stg[:, 3:51].rearrange("p (s i) -> p s i", i=3), z3[:, :, 8:11], [zpb], [])
            src = [z3[:, :, i:i + 8] for i in range(4)]
            dst = out[:, 0:128].rearrange("p (s t) -> p s t", t=8)
        if bcol is None:
            self.ts("dve", dst, src[0], wcol(0), None, ALU.mult, None, [zpb, cb], [outb])
        else:
            self.ts("dve", dst, src[0], wcol(0), bcol, ALU.mult, ALU.add, [zpb, cb], [outb])
        for i in range(1, 4):
            self.stt(dst, src[i], wcol(i), dst, ALU.mult, ALU.add, [zpb, cb, outb], [outb])

    def _phase_lru(self):
        S, sb = self.S, self.sb
        cb = self.cb
        V = self.vecs
        wab = S.buf("wa")
        wa16, wx16 = sb("wa32", [128, 8, 128]), sb("wx32", [128, 8, 128])
        S.dma(wa16[:], self.wa_d, writes=[wab])
        S.dma(wx16[:], self.wx_d, writes=[wab])
        wxs = [sb("lwx%d" % i, [128, KC, 128], BF16) for i in range(2)]
        wgs = [sb("lwg%d" % i, [128, KC, 128], BF16) for i in range(2)]
        wxb = [S.buf("lwx") for _ in range(2)]
        wgb = [S.buf("lwg") for _ in range(2)]
        lzc = [sb("lzc%d" % i, [128, 4]) for i in range(2)]
        lzcb = [S.buf("lzc") for _ in range(2)]
        nm = ["xc", "r", "ig", "a", "bb", "h", "sg"]
        W = [{n: sb("l%s%d" % (n, i), [128, 520] if n == "zp" else [128, 512], BF16 if n == "xc16" else F32) for n in nm}
             for i in range(self.LSLOTS)]
        WB = [{n: S.buf("l" + n) for n in nm} for _ in range(self.LSLOTS)]
        cnt = 0
        for g in range(8):
            s = g % 2
            S.dma(wxs[s][:], self.winr[g], writes=[wxb[s]], eng="pool")
            S.dma(wgs[s][:], self.winr[8 + g], writes=[wgb[s]], eng="pool")
            prev = None
            for si, (c0, c1, nch) in enumerate(self.segs()):
                n = c1 - c0
                w, wb_ = W[cnt % self.LSLOTS], WB[cnt % self.LSLOTS]
                pw, pwb = W[(cnt - 1) % self.LSLOTS], WB[(cnt - 1) % self.LSLOTS]
                cnt += 1
                pt, pb = self.proj(wxs[s], wxb[s], c0, n)
                self.conv3(pt, pb, lzc[g % 2], lzcb[g % 2], si, n,
                           lambda i: V[:, V_LCW + i * 8 + g:V_LCW + i * 8 + g + 1],
                           self.lcT[:, g, :], w["xc"], wb_["xc"], self.stg_lc[:, g, :], bcol=V[:, V_LCB + g:V_LCB + g + 1])
                pa, pab = self.ps()
                self.mm(pa[:, 0:n], wa16[:, g, :], w["xc"][:, 0:n], True, True, [wab, wb_["xc"]], [pab])
                px, pxb = self.ps()
                self.mm(px[:, 0:n], wx16[:, g, :], w["xc"][:, 0:n], True, True, [wab, wb_["xc"]], [pxb])
                self.act(w["r"][:, 0:n], pa[:, 0:n], AF.Tanh, [pab, cb], [wb_["r"]], scale=0.5, bias=self.hb[:, g:g + 1])
                self.act(w["ig"][:, 0:n], px[:, 0:n], AF.Tanh, [pxb, cb], [wb_["ig"]], scale=0.5, bias=self.hb[:, 8 + g:9 + g])
                self.act(w["a"][:, 0:n], w["r"][:, 0:n], AF.Exp, [wb_["r"], cb], [wb_["a"]], scale=self.hc[:, g:g + 1],
                         bias=self.hc[:, g:g + 1])
                self.tt("pool", w["r"][:, 0:n], w["a"][:, 0:n], w["a"][:, 0:n], ALU.mult, [wb_["a"]], [wb_["r"]])
                self.act(w["r"][:, 0:n], w["r"][:, 0:n], AF.Sqrt, [wb_["r"]], [wb_["r"]], scale=-1.0, bias=1.0)
                self.ts("pool", w["ig"][:, 0:n], w["ig"][:, 0:n], 0.5, 0.5, ALU.mult, ALU.add, [wb_["ig"]], [wb_["ig"]])
                self.tt("pool", w["bb"][:, 0:n], w["ig"][:, 0:n], w["xc"][:, 0:n], ALU.mult, [wb_["ig"], wb_["xc"]], [wb_["bb"]])
                if si == 0:
                    self.memset("pool", w["r"][:, 0:1], 1.0, [wb_["r"]])
                    self.memset("pool", w["a"][:, 0:1], 0.0, [wb_["a"]])
                self.tt("pool", w["bb"][:, 0:n], w["bb"][:, 0:n], w["r"][:, 0:n], ALU.mult, [wb_["bb"], wb_["r"]], [wb_["bb"]])
                if si == 4:
                    a3 = w["a"][:, 0:128].rearrange("p (s t) -> p s t", t=8)
                    b3 = w["bb"][:, 0:128].rearrange("p (s t) -> p s t", t=8)
                    t3 = w["r"][:, 0:16].unsqueeze(2)
                    self.tt("pool", t3, a3[:, :, 0:1], self.h0T[:, g, :].unsqueeze(2), ALU.mult, [wb_["a"], cb], [wb_["r"]])
                    self.tt("pool", b3[:, :, 0:1], b3[:, :, 0:1], t3, ALU.add, [wb_["bb"], wb_["r"]], [wb_["bb"]])
                    self.memset("pool", a3[:, :, 0:1], 0.0, [wb_["a"]])
                init = 0.0 if si in (0, 4) else prev
                rds = [wb_["a"], wb_["bb"]] + ([prevb] if si in (1, 2, 3) else [])
                S.op("dve", lambda e, o=w["h"][:, 0:n], a=w["a"][:, 0:n], b=w["bb"][:, 0:n], i0=init:
                     e.tensor_tensor_scan(out=o, data0=a, data1=b, initial=i0, op0=ALU.mult, op1=ALU.add),
                     reads=rds, writes=[wb_["h"]], dur=0.12 + 2 * n / 960.0)
                prev, prevb = w["h"][:, n - 1:n], wb_["h"]
                if si == 3:
                    self.cp("pool", self.stg_lh[:, g, 0:1], w["h"][:, 511:512], [wb_["h"]], [])
                if si == 4:
                    self.cp("pool", self.stg_lh[:, g, 1:17].unsqueeze(2),
                            w["h"][:, 0:128].rearrange("p (s t) -> p s t", t=8)[:, :, 7:8], [wb_["h"]], [])
                pg, pgb = self.proj(wgs[s], wgb[s], c0, n)
                self.act(w["sg"][:, 0:n], pg[:, 0:n], AF.Tanh, [pgb], [wb_["sg"]], scale=0.5)
                self.stt(w["sg"][:, 0:n], w["sg"][:, 0:n], 1.0, pg[:, 0:n], ALU.add, ALU.mult, [wb_["sg"], pgb], [wb_["sg"]])
                self.stt(self.lru_out[:, g, c0:c1], w["sg"][:, 0:n], 0.5, w["h"][:, 0:n], ALU.mult, ALU.mult,
                         [wb_["h"], wb_["sg"]], [])

    def _phase_gdn(self):
        S, sb = self.S, self.sb
        NS = 2
        self.g_wq = [[sb("gw%d_%d" % (j, i), [128, KC, 128], BF16) for j in range(4)] for i in range(NS)]
        self.g_wqb = [[S.buf("gw") for j in range(4)] for i in range(NS)]

        def ws(i, nc_):
            nn = nc_ * 128
            d = {}
            d["zc"] = [sb("gzc%d_%d" % (j, i), [128, 4]) for j in range(3)]
            d["cv"] = sb("gcv%d" % i, [128, nn])
            d["cv2"] = sb("gcv2_%d" % i, [128, nn])
            d["sq"] = sb("gsq%d" % i, [128, nn], BF16)
            d["rt"] = sb("grt%d" % i, [128, nn])
            d["th"] = sb("gth%d" % i, [128, nn])
            d["kq"] = sb("gkq%d" % i, [128, nc_, 2, 128], BF16)
            d["vf"] = sb("gvf%d" % i, [128, nn], BF16)
            d["sgate"] = sb("gsg%d" % i, [128, nn], BF16)
            d["ktv"] = sb("gktv%d" % i, [128, 2, nc_, 128], BF16)
            d["gA"] = sb("ggA%d" % i, [128, nc_, 128])
            d["gB"] = sb("ggB%d" % i, [128, nc_, 128])
            if nc_ == 4:
                d["gO"] = sb("ggO%d" % i, [128, nc_, 128])
            if self.OT_ALIAS or nc_ == 1:
                d["ot"] = d["cv2"][:].rearrange("p (c t) -> p c t", t=128)
            else:
                d["ot"] = sb("got%d" % i, [128, nc_, 128])
            d["on"] = sb("gon%d" % i, [128, nc_, 128], BF16)
            d["ss4"] = sb("gss4_%d" % i, [128, 4])
            d["kg"] = sb("gkg%d" % i, [128, nc_, 128], BF16)
            d["kdec"] = sb("gkd%d" % i, [128, nc_, 128], BF16)
            d["aqk"] = sb("gaq%d" % i, [128, nc_, 128], BF16)
            d["py"] = [sb("gpy%d_%d" % (j, i), [128, nc_, 2, 128], BF16) for j in range(2)]
            d["pt"] = [sb("gpt%d_%d" % (j, i), [128, nc_, 128], BF16) for j in range(2)]
            d["ub"] = sb("gub%d" % i, [128, nc_, 128])
            d["wT"] = sb("gwT%d" % i, [128, nc_, 128], BF16)
            if nc_ == 4:
                d["s32"] = sb("s32_%d" % i, [128, 128])
                d["s16"] = sb("s16_%d" % i, [128, 128], BF16)
            d["qs"] = sb("gqs%d" % i, [128, 128])
            d["vn"] = sb("gvn%d" % i, [128, 128], BF16)
            return d
        self.g_W = [ws(i, 4) for i in range(NS)] + [ws(2, 1)]
        keys = ["zc0", "zc1", "zc2", "cv", "cv2", "sq", "rt", "th", "kq", "vf", "sgate", "ktv", "gA", "gB", "gO", "ot", "on", "ss4", "kg",
                "kdec", "aqk", "ub", "wT", "s32", "s16", "qs", "vn"] + \
               ["%s%d_%d" % (a_, j, p) for a_ in ("pyP", "pyY", "pt") for j in range(2) for p in range(2)]
        self.g_WB = [{k: S.buf("g" + k) for k in keys} for _ in range(NS + 1)]
        for di_, d_ in enumerate(self.g_WB):
            if self.OT_ALIAS or di_ == NS:
                d_["cv2"] = d_["ot"]
        self.g_s0 = sb("s0", [128, 16, 128])
        self.g_s0b = S.buf("s0")
        self.g_snew = [sb("snew%d" % i, [128, 4, 128]) for i in range(2)]
        self.g_snewb = [S.buf("snew") for _ in range(2)]
        self.g_kdm = sb("kdm", [128, 16, 128], BF16)
        self.g_wq32 = sb("wq32", [128, 16, 2, 8])
        self.g_wsq = sb("wsq", [128, 2, 128])
        self.g_mb = S.buf("masked")
        gl_h = self.GLIM[0]
        psr = self.PSR
        for h0 in range(0, gl_h, NS):
            hs = list(range(h0, min(h0 + NS, gl_h)))
            gens = [self.gdn_stream(h, h - h0, h - h0, [0, 1, 2, 3]) for h in hs] + [self.gdn_samples(hs, h0)]
            alive = [True] * len(gens)
            while any(alive):
                for gi, g in enumerate(gens):
                    if not alive[gi]:
                        continue
                    self.psrange = psr[gi] if gi < len(hs) else psr[2]
                    try:
                        next(g)
                    except StopIteration:
                        alive[gi] = False
                    self.psrange = tuple(range(8))

    def gdn_samples(self, hs, h0):
        for h in hs:
            yield from self.gdn_stream(h, h - h0, 2, [4])

    def gdn_stream(self, h, wslot, slot, seglist):
        S = self.S
        cb, gb = self.cb, self.gsb
        V = self.vecs
        w, wb_ = self.g_W[slot], self.g_WB[slot]
        wq, wqb = self.g_wq[wslot], self.g_wqb[wslot]
        prompt = (seglist[0] == 0)
        if prompt:
            s32, s16, s32b, s16b = w["s32"], w["s16"], wb_["s32"], wb_["s16"]
        s0, s0b_ = self.g_s0, self.g_s0b
        snew, snewb, kdm, mb = self.g_snew, self.g_snewb, self.g_kdm, self.g_mb
        wq32, wsq = self.g_wq32, self.g_wsq
        gl_h, gl_s, gl_st = self.GLIM
        scale_q = float(128 ** -0.5)
        bP = lambda j, c: wb_["pyP%d_%d" % (j, c // 2)]
        bY = lambda j, c: wb_["pyY%d_%d" % (j, c // 2)]
        bT = lambda j, c: wb_["pt%d_%d" % (j, c // 2)]
        if prompt:
            for j in range(4):
                S.dma(wq[j][:], self.winr[16 + 8 * j + h], writes=[wqb[j]], eng="pool")
            self.memset("pool", s32[:], 0.0, [s32b])
            self.memset("pool", s16[:], 0.0, [s16b])
            yield
        for si, (c0, c1, nch) in enumerate(self.segs()):
            if si not in seglist:
                continue
            n = c1 - c0
            sample = (si == 4)
            if (gl_s == 4 and sample) or (gl_s == 1 and si != 0) or (gl_s == -1 and not sample):
                continue
            um, usm, lm = (self.uincl_b, self.ustr_b, self.lstr_b) if sample else (self.uincl, self.ustr, self.lstr)
            S.tag = "2chunk"
            ti0 = c0 // 128
            bc = lambda t: t[:, ti0:ti0 + nch, h:h + 1].broadcast_to([128, nch, 128])
            cs = slice(0, nch)
            mask3 = lambda m: m[:].unsqueeze(1).broadcast_to([128, nch, 128])
            self.tt("pool", w["gA"][:, cs, :], mask3(um), bc(self.gtok), ALU.mult, [cb, gb], [wb_["gA"]])
            pd, pdb = self.ps()
            for c in range(nch):
                self.mm(pd[:, c * 128:(c + 1) * 128], lm[:], w["gA"][:, c, :], True, True, [cb, wb_["gA"]], [pdb],
                        signal=(c == nch - 1))
            self.act(w["gB"][:, cs, :], pd[:, 0:nch * 128].rearrange("p (c t) -> p c t", t=128), AF.Exp, [pdb], [wb_["gB"]])
            self.tt("pool", w["gA"][:, cs, :], w["gB"][:, cs, :], mask3(um), ALU.mult, [wb_["gB"], cb, wb_["gA"]], [wb_["gA"]])
            if sample:
                self.tt("pool", w["gB"][:, cs, :], w["gB"][:, cs, :], mask3(usm), ALU.mult, [wb_["gB"], cb], [wb_["gB"]])
                self.tt("pool", w["gB"][:, cs, :], w["gB"][:, cs, :], bc(self.nbeta), ALU.mult, [wb_["gB"], gb], [wb_["gB"]])
            else:
                goff = w["gO"][:, cs, :]
                self.tt("pool", w["gB"][:, cs, :], w["gB"][:, cs, :], bc(self.nbeta), ALU.mult, [wb_["gB"], gb], [wb_["gB"]])
                self.tt("pool", goff, w["gB"][:, cs, :], mask3(self.offf), ALU.mult, [wb_["gB"], cb], [wb_["gO"]])
                self.tt("pool", w["gB"][:, cs, :], w["gB"][:, cs, :], mask3(self.usbd), ALU.mult, [wb_["gB"], cb], [wb_["gB"]])
            for j, cvn in ((0, "cv"), (1, "cv2"), (2, "cv")):
                S.tag = "1proj"
                blk = 8 * j + h
                cvt, cvb = w[cvn], wb_[cvn]
                pt, pb = self.proj(wq[j], wqb[j], c0, n)
                self.conv3(pt, pb, w["zc"][j], wb_["zc%d" % j], si, n,
                           lambda i, blk=blk: V[:, V_GCW + i * 24 + blk:V_GCW + i * 24 + blk + 1],
                           self.gcT[:, blk, :], cvt, cvb, self.stg_gc[:, blk, :])
                self.act(w["th"][:, 0:n], cvt[:, 0:n], AF.Tanh, [cvb], [wb_["th"]], scale=0.5)
                if j == 2:
                    self.stt(w["vf"][:, 0:n], w["th"][:, 0:n], 1.0, cvt[:, 0:n], ALU.add, ALU.mult,
                             [wb_["th"], cvb], [wb_["vf"]])
                    yield
                    continue
                self.stt(cvt[:, 0:n], w["th"][:, 0:n], 1.0, cvt[:, 0:n], ALU.add, ALU.mult, [wb_["th"], cvb], [cvb])
                self.tt("pool", w["sq"][:, 0:n], cvt[:, 0:n], cvt[:, 0:n], ALU.mult, [cvb], [wb_["sq"]])
                p2, p2b = self.ps()
                self.mm(p2[:, 0:n], self.onesb[:], w["sq"][:, 0:n], True, True, [cb, wb_["sq"]], [p2b])
                self.act(w["rt"][:, 0:n], p2[:, 0:n], AF.Ln, [p2b], [wb_["rt"]], bias=4.0 * EPS)
                self.act(w["rt"][:, 0:n], w["rt"][:, 0:n], AF.Exp, [wb_["rt"]], [wb_["rt"]], scale=-0.5,
                         bias=(math.log(scale_q) if j == 0 else 0.0))
                dst = w["kq"][:, 0:nch, 1 - j, :]
                src = cvt[:, 0:n].rearrange("p (c t) -> p c t", t=128)
                rt3 = w["rt"][:, 0:n].rearrange("p (c t) -> p c t", t=128)
                self.tt(self.NORM_ENG, dst, src, rt3, ALU.mult, [cvb, wb_["rt"]], [wb_["kq"]])
                yield
            S.tag = "1proj"
            pg, pgb = self.proj(wq[3], wqb[3], c0, n)
            self.act(w["th"][:, 0:n], pg[:, 0:n], AF.Tanh, [pgb], [wb_["th"]], scale=0.5)
            self.stt(w["sgate"][:, 0:n], w["th"][:, 0:n], 1.0, pg[:, 0:n], ALU.add, ALU.mult, [wb_["th"], pgb], [wb_["sgate"]])
            yield
            if gl_st < 2:
                continue
            S.tag = "2chunk"
            ptr, ptrb = self.ps()
            pv = ptr[:].bitcast(BF16)
            for c in range(nch):
                self.tr(pv[:, c * 128:(c + 1) * 128], w["kq"][:, c, 0, :], self.idb[:], [wb_["kq"], cb], [ptrb], signal=False)
            for c in range(nch):
                self.tr(pv[:, 512 + c * 128:512 + (c + 1) * 128], w["vf"][:, c * 128:(c + 1) * 128], self.idb[:],
                        [wb_["vf"], cb], [ptrb], signal=(c == nch - 1))
            self.cp("act", w["ktv"][:, :, cs, :], pv.rearrange("p (a c t) -> p a c t", a=2, t=128)[:, :, cs, :], [ptrb], [wb_["ktv"]])
            KT = w["ktv"][:, 0, cs, :]
            self.tt("pool", w["kg"][:, cs, :], KT, bc(self.egc), ALU.mult, [wb_["ktv"], gb], [wb_["kg"]])
            self.tt("pool", w["kdec"][:, cs, :], KT, bc(self.kdc), ALU.mult, [wb_["ktv"], gb], [wb_["kdec"]])
            if nch == 1:
                pk2, pk2b1 = self.ps()
                pk2b = [pk2b1]
            else:
                pk2, pk2b = self.ps2()
            for c in range(nch):
                self.mm(pk2[:, c * 256:(c + 1) * 256], w["kq"][:, c, 0, :], w["kq"][:, c, :, :].rearrange("p a b -> p (a b)"),
                        True, True, [wb_["kq"]], pk2b, signal=(c == nch - 1))
            pk4 = pk2[:, 0:nch * 256].rearrange("p (c a t) -> p c a t", a=2, t=128)
            self.tt("dve", w["py"][0][:, cs, 0, :], pk4[:, :, 0, :], w["gB"][:, cs, :], ALU.mult, pk2b + [wb_["gB"]], [bP(0, 0)])
            if not sample:
                self.tt("dve", w["on"][:, cs, :], pk4[:, :, 0, :], goff, ALU.mult, pk2b + [wb_["gO"]], [wb_["on"]])
            self.tt("dve", w["aqk"][:, cs, :], pk4[:, :, 1, :], w["gA"][:, cs, :], ALU.mult, pk2b + [wb_["gA"]], [wb_["aqk"]])
            pp, ppb = self.ps()
            ppv = pp[:].bitcast(BF16)
            if not sample:
                for c in range(nch):
                    self.tr(ppv[:, 512 + c * 128:512 + (c + 1) * 128], w["on"][:, c, :], self.idb[:], [wb_["on"], cb], [ppb], signal=False)
            for c in range(nch):
                self.tr(ppv[:, c * 128:(c + 1) * 128], w["py"][0][:, c, 0, :], self.idb[:], [bP(0, 0), cb], [ppb],
                        signal=(c == nch - 1))
            self.cp("act", w["pt"][0][:, cs, :], ppv[:, 0:nch * 128].rearrange("p (c t) -> p c t", t=128), [ppb], [bT(0, 0)])
            if not sample:
                self.cp("act", w["ktv"][:, 0, cs, :], ppv[:, 512:512 + nch * 128].rearrange("p (c t) -> p c t", t=128), [ppb], [wb_["ktv"]])
            self.tt("pool", w["py"][1][:, cs, 1, :], w["py"][0][:, cs, 0, :], mask3(self.idb), ALU.add,
                    [bP(0, 0), cb], [bY(1, 0)])
            yield
            if gl_st < 3:
                continue
            S.tag = "3dbl"
            nit = 3 if sample else 6
            pX, pXb = self.ps()
            pB, pBb = self.ps()
            for c in range(nch):
                self.mm(pX[:, c * 128:(c + 1) * 128], w["pt"][0][:, c, :], w["py"][0][:, c, 0, :], True, True,
                        [bT(0, 0), bP(0, 0)], [pXb], signal=(c == nch - 1))
            for c in range(nch):
                self.mm(pB[:, c * 128:(c + 1) * 128], w["py"][0][:, c, 0, :], w["pt"][0][:, c, :], True, True,
                        [bT(0, 0), bP(0, 0)], [pBb], signal=(c == nch - 1))
            v3 = lambda p_: p_[:, 0:nch * 128].rearrange("p (c t) -> p c t", t=128)
            self.cp("act", w["py"][1][:, cs, 0, :], v3(pX), [pXb], [bP(1, 0)])
            self.cp("act", w["pt"][1][:, cs, :], v3(pB), [pBb], [bT(1, 0)])
            yield
            for k in range(2, nit + 1):
                rd, wr = (k - 1) % 2, k % 2
                last = (k == nit)
                rdb = [bT(rd, 0), bP(rd, 0), bY(rd, 0)]
                if last or nch == 1 or not self.DBL256:
                    pZ, pZb = self.ps()
                    for c in range(nch):
                        self.mm(pZ[:, c * 128:(c + 1) * 128], w["pt"][rd][:, c, :], w["py"][rd][:, c, 1, :], True, True,
                                rdb, [pZb], signal=(c == nch - 1))
                    self.tt("dve", w["py"][wr][:, cs, 1, :], v3(pZ), w["py"][rd][:, cs, 1, :], ALU.add, [pZb, bY(rd, 0)], [bY(wr, 0)])
                    if not last:
                        pX, pXb = self.ps()
                        for c in range(nch):
                            self.mm(pX[:, c * 128:(c + 1) * 128], w["pt"][rd][:, c, :], w["py"][rd][:, c, 0, :], True, True,
                                    rdb, [pXb], signal=(c == nch - 1))
                        self.cp("act", w["py"][wr][:, cs, 0, :], v3(pX), [pXb], [bP(wr, 0)])
                else:
                    pA2, pA2b = self.ps2()
                    for c in range(nch):
                        self.mm(pA2[:, c * 256:(c + 1) * 256], w["pt"][rd][:, c, :], w["py"][rd][:, c, :, :].rearrange("p a b -> p (a b)"),
                                True, True, rdb, pA2b, signal=(c == nch - 1))
                    p4 = pA2[:, 0:nch * 256].rearrange("p (c a t) -> p c a t", a=2, t=128)
                    self.tt("dve", w["py"][wr][:, cs, 1, :], p4[:, :, 1, :], w["py"][rd][:, cs, 1, :], ALU.add, pA2b + [bY(rd, 0)], [bY(wr, 0)])
                    self.cp("act", w["py"][wr][:, cs, 0, :], p4[:, :, 0, :], pA2b, [bP(wr, 0)])
                if not last:
                    pB, pBb = self.ps()
                    for c in range(nch):
                        self.mm(pB[:, c * 128:(c + 1) * 128], w["py"][rd][:, c, 0, :], w["pt"][rd][:, c, :], True, True,
                                rdb, [pBb], signal=(c == nch - 1))
                    self.cp("act", w["pt"][wr][:, cs, :], v3(pB), [pBb], [bT(wr, 0)])
                yield
            if gl_st < 4:
                continue
            yb = nit % 2
            S.tag = "4ubw"
            if sample:
                pu, pub = self.ps()
                pw2, pw2b = self.ps()
                for c in range(nch):
                    self.mm(pu[:, c * 128:(c + 1) * 128], w["py"][yb][:, c, 1, :], w["ktv"][:, 1, c, :], True, True,
                            [bY(yb, 0), wb_["ktv"]], [pub], signal=(c == nch - 1))
                for c in range(nch):
                    self.mm(pw2[:, c * 128:(c + 1) * 128], w["kg"][:, c, :], w["py"][yb][:, c, 1, :], True, True,
                            [bY(yb, 0), wb_["kg"]], [pw2b], signal=(c == nch - 1))
            else:
                Yd = lambda c: w["py"][yb][:, c, 1, :]
                sq3 = w["sq"][:, 0:n].rearrange("p (c t) -> p c t", t=128)
                vf3 = w["vf"][:, 0:n].rearrange("p (c t) -> p c t", t=128)
                pz_, pzb_ = self.ps()
                for c in range(nch):
                    self.mm(pz_[:, c * 128:(c + 1) * 128], w["ktv"][:, 0, c, :], Yd(c), True, True, [wb_["ktv"], bY(yb, 0)], [pzb_],
                            signal=(c == nch - 1))
                self.cp("act", w["on"][:, cs, :], v3(pz_), [pzb_], [wb_["on"]])
                pu, pub = self.ps()
                for c in range(nch):
                    self.mm(pu[:, c * 128:(c + 1) * 128], Yd(c), w["ktv"][:, 1, c, :], c == 0, False, [bY(yb, 0), wb_["ktv"]], [pub],
                            signal=(c == nch - 1), sgc=True)
                self.cp("act", sq3, v3(pu), [pub], [wb_["sq"]])
                pw0, pw0b = self.ps()
                for c in range(nch):
                    self.mm(pw0[:, c * 128:(c + 1) * 128], Yd(c), w["kg"][:, c, :], True, True, [bY(yb, 0), wb_["kg"]], [pw0b],
                            signal=(c == nch - 1))
                self.cp("act", vf3, v3(pw0), [pw0b], [wb_["vf"]])
                for c in range(nch):
                    self.mm(pu[:, c * 128:(c + 1) * 128], w["on"][:, c, :], sq3[:, c, :], False, True, [wb_["on"], wb_["sq"]], [pub],
                            signal=(c == nch - 1), sgc=True)
                pw2, pw2b = self.ps()
                for c in range(nch):
                    self.mm(pw2[:, c * 128:(c + 1) * 128], w["kg"][:, c, :], Yd(c), c == 0, False, [bY(yb, 0), wb_["kg"]], [pw2b],
                            signal=False, sgc=True)
                for c in range(nch):
                    self.mm(pw2[:, c * 128:(c + 1) * 128], vf3[:, c, :], w["on"][:, c, :], False, True, [wb_["vf"], wb_["on"]], [pw2b],
                            signal=(c == nch - 1), sgc=True)
            self.tt("dve", w["ub"][:, cs, :], v3(pu), bc(self.betah), ALU.mult, [pub, gb], [wb_["ub"]])
            if sample:
                self.cp("act", wq32[:, :, 0, :], pw2[:, 0:128].rearrange("p (s t) -> p s t", t=8), [pw2b], [mb])
            else:
                self.cp("dve", w["wT"][:, cs, :], v3(pw2), [pw2b], [wb_["wT"]])
                yield
            if gl_st < 5:
                continue
            if sample:
                S.tag = "5samp"
                S.dma(s0[:], self.sgS[:, h, :, :].rearrange("s k v -> k s v"), writes=[s0b_])
                self.tt("pool", kdm[:], w["kdec"][:, 0, :].unsqueeze(1).broadcast_to([128, 16, 128]),
                        self.selc[:].unsqueeze(2).broadcast_to([128, 16, 128]), ALU.mult, [wb_["kdec"], cb], [mb])
                self.cp("pool", wq32[:, :, 1, :], w["kq"][:, 0, 1, :].rearrange("p (s t) -> p s t", t=8), [wb_["kq"]], [mb])
            if gl_st < 6:
                continue
            po, pob = self.ps()
            for c in range(nch):
                S.tag = "6chain" if not sample else "6chainS"
                ti = c0 // 128 + c
                vn, qss = w["vn"], w["qs"]
                col = lambda t, ti=ti: t[:, ti, h:h + 1]
                if not sample:
                    pw_, pwb_ = self.ps()
                    pq_, pqb_ = self.ps()
                    self.mm(pw_[:, 0:128], w["wT"][:, c, :], s16[:], True, True, [wb_["wT"], s16b], [pwb_])
                    self.mm(pq_[:, 0:128], w["kq"][:, c, 1, :], s16[:], True, True, [wb_["kq"], s16b], [pqb_])
                    pwv, pqv = pw_[:, 0:128], pq_[:, 0:128]
                    pqb2 = pqb_
                else:
                    pws, pwsb = self.ps()
                    for s_ in range(16):
                        self.mm(pws[:, s_ * 16:(s_ + 1) * 16], s0[:, s_, :], wq32[:, s_, :, :].rearrange("p a t -> p (a t)"),
                                True, True, [s0b_, mb], [pwsb], signal=(s_ == 15))
                    self.cp("act", wsq[:].rearrange("p a (s t) -> p a s t", t=8),
                            pws[:, 0:256].rearrange("p (s a t) -> p a s t", a=2, t=8), [pwsb], [mb])
                    pw_, pwb_ = self.ps()
                    self.tr(pw_[:, 0:128], wsq[:, 0, :], self.idf[:], [mb, cb], [pwb_], signal=False)
                    self.tr(pw_[:, 128:256], wsq[:, 1, :], self.idf[:], [mb, cb], [pwb_])
                    pwv, pqv = pw_[:, 0:128], pw_[:, 128:256]
                    pqb2 = pwb_
                self.stt(vn[:], pwv, col(self.nbeta), w["ub"][:, c, :], ALU.mult, ALU.add,
                         [pwb_, gb, wb_["ub"]], [wb_["vn"]])
                self.act(qss[:], pqv, AF.Copy, [pqb2, gb], [wb_["qs"]], scale=col(self.egc))
                self.mm(po[:, c * 128:(c + 1) * 128], w["aqk"][:, c, :], vn[:], True, True, [wb_["aqk"], wb_["vn"]], [pob])
                self.tt("dve", w["ot"][:, c, :], po[:, c * 128:(c + 1) * 128], qss[:], ALU.add, [pob, wb_["qs"]], [wb_["ot"]])
                if not sample:
                    pkv, pkvb = self.ps()
                    self.mm(pkv[:, 0:128], w["kdec"][:, c, :], vn[:], True, True, [wb_["kdec"], wb_["vn"]], [pkvb])
                    gl = self.eglb[:, ti, h:h + 1]
                    if self.S16_DVE:
                        self.stt(s16[:], s32[:], gl, pkv[:, 0:128], ALU.mult, ALU.add, [s32b, gb, pkvb], [s16b])
                        self.stt(s32[:], s32[:], gl, pkv[:, 0:128], ALU.mult, ALU.add, [s32b, gb, pkvb], [s32b])
                    else:
                        self.stt(s32[:], s32[:], gl, pkv[:, 0:128], ALU.mult, ALU.add, [s32b, gb, pkvb], [s32b])
                        self.cp("pool", s16[:], s32[:], [s32b], [s16b])
                else:
                    for q4 in range(4):
                        pkv, pkvb = self.ps()
                        for u in range(4):
                            s_ = q4 * 4 + u
                            self.mm(pkv[:, u * 128:(u + 1) * 128], kdm[:, s_, :], vn[:], True, True, [mb, wb_["vn"]], [pkvb],
                                    signal=(u == 3))
                        for u in range(4):
                            s_ = q4 * 4 + u
                            self.stt(snew[q4 % 2][:, u, :], s0[:, s_, :], self.eglS[:, s_, h:h + 1], pkv[:, u * 128:(u + 1) * 128],
                                     ALU.mult, ALU.add, [s0b_, gb, pkvb], [snewb[q4 % 2]])
                        S.dma(self.o_sgs[q4 * 4:(q4 + 1) * 4, h, :, :].rearrange("s k v -> k s v"), snew[q4 % 2][:],
                              reads=[snewb[q4 % 2]], is_output=True)
                yield
            S.tag = "7opath"
            for c in range(nch):
                self.act(w["on"][:, c, :], w["ot"][:, c, :], AF.Square, [wb_["ot"]], [wb_["on"], wb_["ss4"]],
                         accum_out=w["ss4"][:, c:c + 1])
            self.rstd(w["ss4"][:, cs], 1.0 / 128, [wb_["ss4"]])
            self.tt("dve", w["on"][:, cs, :], w["ot"][:, cs, :], w["ss4"][:, cs].unsqueeze(2).broadcast_to([128, nch, 128]),
                    ALU.mult, [wb_["ot"], wb_["ss4"]], [wb_["on"]])
            pz, pzb = self.ps()
            pzv = pz[:].bitcast(BF16)
            for c in range(nch):
                self.tr(pzv[:, c * 128:(c + 1) * 128], w["on"][:, c, :], self.idb[:], [wb_["on"], cb], [pzb], signal=(c == nch - 1))
            self.stt(self.gdn_out[:, h, c0:c0 + n], pzv[:, 0:n], self.gnwh[:, 0:1], w["sgate"][:, 0:n], ALU.mult, ALU.mult,
                     [pzb, cb, wb_["sgate"]], [])
            yield
        if prompt:
            S.dma(self.o_pgs[h, :, :], s32[:], reads=[s32b], is_output=True)
            yield

    def conv3(self, pt, pb, zc, zcb, si, n, wcol, stateT, out, outb, stg, bcol=0.0):
        cb = self.cb
        if si < 4:
            self.act(out[:, 0:n], pt[:, 0:n], AF.Identity, [pb, cb], [outb], scale=wcol(3), bias=bcol)
            for i in range(3):
                sh = 3 - i
                self.stt(out[:, sh:n], pt[:, 0:n - sh], wcol(i), out[:, sh:n], ALU.mult, ALU.add, [pb, cb, outb], [outb])
                if si > 0:
                    self.stt(out[:, 0:sh], zc[:, i:i + sh], wcol(i), out[:, 0:sh], ALU.mult, ALU.add, [zcb, cb, outb], [outb])
            if si == 3:
                self.cp("dve", stg[:, 0:3], pt[:, n - 3:n], [pb], [])
            else:
                self.cp("dve", zc[:, 0:3], pt[:, n - 3:n], [pb, outb], [zcb])
        else:
            p3 = pt[:, 0:128].rearrange("p (s t) -> p s t", t=8)
            o3 = out[:, 0:128].rearrange("p (s t) -> p s t", t=8)
            st3 = stateT.rearrange("p (s i) -> p s i", i=3)
            self.act(o3, p3, AF.Identity, [pb, cb], [outb], scale=wcol(3), bias=bcol)
            for i in range(3):
                sh = 3 - i
                self.stt(o3[:, :, sh:8], p3[:, :, 0:8 - sh], wcol(i), o3[:, :, sh:8], ALU.mult, ALU.add, [pb, cb, outb], [outb])
                self.stt(o3[:, :, 0:sh], st3[:, :, i:i + sh], wcol(i), o3[:, :, 0:sh], ALU.mult, ALU.add, [cb, outb], [outb])
            self.cp("dve", stg[:, 3:51].rearrange("p (s i) -> p s i", i=3), p3[:, :, 5:8], [pb], [])

    def conv2(self, pt, pb, zp, zpb, zc, zcb, si, n, wcol, bcol, stateT, out, outb, stg):
        cb = self.cb
        if si < 4:
            self.cp("act", zp[:, 3:3 + n], pt[:, 0:n], [pb], [zpb])
            if si == 0:
                self.memset("pool", zp[:, 0:3], 0.0, [zpb])
            else:
                self.cp("pool", zp[:, 0:3], zc[:, 0:3], [zcb], [zpb])
            if si == 3:
                self.cp("pool", stg[:, 0:3], zp[:, 512:515], [zpb], [])
            else:
                self.cp("pool", zc[:, 0:3], zp[:, 512:515], [zpb], [zcb])
            src = [zp[:, i:i + n] for i in range(4)]
            dst = out[:, 0:n]
        else:
            z3 = zp[:, 0:176].rearrange("p (s t) -> p s t", t=11)
            self.cp("act", z3[:, :, 3:11], pt[:, 0:128].rearrange("p (s t) -> p s t", t=8), [pb], [zpb])
            self.cp("pool", z3[:, :, 0:3], stateT.rearrange("p (s i) -> p s i", i=3), [cb], [zpb])
            self.cp("pool", stg[:, 3:51].rearrange("p (s i) -> p s i", i=3), z3[:, :, 8:11], [zpb], [])
            src = [z3[:, :, i:i + 8] for i in range(4)]
            dst = out[:, 0:128].rearrange("p (s t) -> p s t", t=8)
        if bcol is None:
            self.ts("dve", dst, src[0], wcol(0), None, ALU.mult, None, [zpb, cb], [outb])
        else:
            self.ts("dve", dst, src[0], wcol(0), bcol, ALU.mult, ALU.add, [zpb, cb], [outb])
        for i in range(1, 4):
            self.stt(dst, src[i], wcol(i), dst, ALU.mult, ALU.add, [zpb, cb, outb], [outb])

    def _state_outputs(self):
        S, sb = self.S, self.sb
        cb = self.cb
        scr2 = self.gdn_out[:].rearrange("p k t -> p (k t)").bitcast(F32)
        olc, olh, ogc = scr2[0:51, 0:D], scr2[0:17, D:2 * D], scr2[0:51, 2 * D:2 * D + 3072]
        ob = S.buf("ostate")
        for g in range(8):
            pt, pb = self.ps()
            self.tr(pt[0:51, 0:128], self.stg_lc[:, g, :], self.idf[:], [cb], [pb])
            self.cp("dve", olc[:, g * 128:(g + 1) * 128], pt[0:51, 0:128], [pb], [ob])
            pt, pb = self.ps()
            self.tr(pt[0:17, 0:128], self.stg_lh[:, g, :], self.idf[:], [cb], [pb])
            self.cp("dve", olh[:, g * 128:(g + 1) * 128], pt[0:17, 0:128], [pb], [ob])
        for g in range(24):
            pt, pb = self.ps()
            self.tr(pt[0:51, 0:128], self.stg_gc[:, g, :], self.idf[:], [cb], [pb])
            self.cp("dve", ogc[:, g * 128:(g + 1) * 128], pt[0:51, 0:128], [pb], [ob])
        S.dma(self.o_plc, olc[0:3, :], reads=[ob], is_output=True)
        S.dma(self.o_slc, olc[3:51, :], reads=[ob], is_output=True)
        S.dma(self.o_plh, olh[0:1, :], reads=[ob], is_output=True)
        S.dma(self.o_slh, olh[1:17, :], reads=[ob], is_output=True)
        S.dma(self.o_pgc, ogc[0:3, :], reads=[ob], is_output=True)
        S.dma(self.o_sgc, ogc[3:51, :], reads=[ob], is_output=True)

    def _phase_merge(self):
        S, sb = self.S, self.sb
        cb = self.cb
        merged, mgb = self.merged, self.mgb
        wm = [[sb("mw%d_%d" % (j, i), [128, KC, 128], BF16) for j in range(4)] for i in range(2)]
        wmb = [[S.buf("mw") for j in range(4)] for i in range(2)]
        s3 = [sb("ms3_%d" % i, [128, 512]) for i in range(2)]
        s4 = [sb("ms4_%d" % i, [128, 512]) for i in range(2)]
        t1 = [sb("mt1_%d" % i, [128, 512]) for i in range(2)]
        t2 = [sb("mt2_%d" % i, [128, 512]) for i in range(2)]
        mbs = [{k: S.buf("m" + k) for k in ("s3", "s4", "t1", "t2")} for _ in range(2)]
        cnt = 0
        for c in range(8):
            s = c % 2
            srcs = (self.wbl_d[c], self.wbg_d[c], self.winr[48 + c], self.winr[56 + c])
            for j in range(4):
                S.dma(wm[s][j][:], srcs[j], writes=[wmb[s][j]], eng="pool")
            for si, (c0, c1, nch) in enumerate(self.segs()):
                n = c1 - c0
                b = mbs[cnt % 2]
                i2 = cnt % 2
                cnt += 1
                acts = (self.lru_out, self.gdn_out, self.xn, self.xn)
                pp = []
                for j in range(4):
                    pt, pb = self.ps()
                    for k in range(KC):
                        self.mm(pt[:, 0:n], wm[s][j][:, k, :], acts[j][:, k, c0:c1], k == 0, k == KC - 1, [wmb[s][j]], [pb])
                    pp.append((pt, pb))
                self.act(s3[i2][:, 0:n], pp[2][0][:, 0:n], AF.Tanh, [pp[2][1]], [b["s3"]], scale=0.5)
                self.act(s4[i2][:, 0:n], pp[3][0][:, 0:n], AF.Tanh, [pp[3][1]], [b["s4"]], scale=0.5)
                self.stt(t1[i2][:, 0:n], s3[i2][:, 0:n], 1.0, pp[0][0][:, 0:n], ALU.add, ALU.mult, [pp[0][1], b["s3"]], [b["t1"]])
                self.stt(t2[i2][:, 0:n], s4[i2][:, 0:n], 1.0, pp[1][0][:, 0:n], ALU.add, ALU.mult, [pp[1][1], b["s4"]], [b["t2"]])
                self.tt("pool", merged[:, c, c0:c1], t1[i2][:, 0:n], t2[i2][:, 0:n], ALU.add, [b["t1"], b["t2"]], [mgb])

    def _phase_out(self):
        S, sb = self.S, self.sb
        cb = self.cb
        self._state_outputs()
        merged, mgb = self.merged, self.mgb
        wout16 = sb("wout16", [128, KC, D], BF16)
        wob = S.buf("wout")
        S.dma(wout16[:], self.wout_d, writes=[wob], eng="pool")
        npb = sb("npb", [128, D])
        npbb = S.buf("npb")
        S.dma(npb[:], self.npost_d.broadcast_to([128, D]), writes=[npbb])
        NB = 4
        scr = self.xn[:].rearrange("p k t -> p (k t)").bitcast(F32)
        yt = [scr[:, i * D:(i + 1) * D] for i in range(NB)]
        xr = [scr[:, (NB + i) * D:(NB + i + 1) * D] for i in range(NB)]
        ytb = [S.buf("yt") for _ in range(NB)]
        xrb = [S.buf("xr") for _ in range(NB)]
        ss = sb("ssD", [128, NT])
        ssb = [S.buf("ssd") for _ in range(NT)]
        jb = S.buf("junkd")
        junk = sb("junkd", [128, D], BF16)
        for i in range(NT):
            s = i % NB
            S.dma(xr[s][:], self.xsrc(i), writes=[xrb[s]])
            for hf in range(2):
                pt, pb = self.ps()
                for k in range(KC):
                    self.mm(pt[:, 0:512], merged[:, k, i * 128:(i + 1) * 128], wout16[:, k, hf * 512:(hf + 1) * 512],
                            k == 0, k == KC - 1, [mgb, wob], [pb])
                self.cp("act", yt[s][:, hf * 512:(hf + 1) * 512], pt[:, 0:512], [pb], [ytb[s]])
            self.act(junk[:], yt[s][:], AF.Square, [ytb[s]], [jb, ssb[i]], accum_out=ss[:, i:i + 1])
            self.rstd(ss[:, i:i + 1], 1.0 / D, [ssb[i]], eps=4.0 * EPS)
            self.stt(yt[s][:], yt[s][:], ss[:, i:i + 1], npb[:], ALU.mult, ALU.mult, [ytb[s], ssb[i], npbb], [ytb[s]])
            self.tt("pool", yt[s][:], yt[s][:], xr[s][:], ALU.add, [ytb[s], xrb[s]], [ytb[s]])
            S.dma(self.ydst(i), yt[s][:], reads=[ytb[s]], is_output=True)


_NC_CACHE = {}


def _program():
    if "nc" not in _NC_CACHE:
        _NC_CACHE["nc"] = K().build()
    return _NC_CACHE["nc"]


def _f(a):
    return np.ascontiguousarray(a, dtype=np.float32)


def kernel(x_prompt, x_sample, state_lru_conv, state_lru_h, state_gdn_conv, state_gdn_S,
           norm_pre, norm_post, w_in, lru_conv_w, lru_conv_b, lru_wa, lru_ba, lru_wx, lru_bx,
           lru_a_logit, gdn_conv_w, gdn_A_log, gdn_dt_bias, gdn_norm_w, w_br_lru, w_br_gdn, w_out):
    w_in0 = np.asarray(w_in)[0]
    wcat = np.concatenate([w_in0[:, 0:6144], w_in0[:, 6160:8208]], axis=1)
    winr = _f(wcat.reshape(KC, 128, 64, 128).transpose(2, 1, 0, 3))
    wba = _f(w_in0[:, 6144:6160].reshape(KC, 128, 16).transpose(1, 0, 2))
    wa = _f(np.asarray(lru_wa)[0].transpose(1, 0, 2))
    wx = _f(np.asarray(lru_wx)[0].transpose(1, 0, 2))
    wbl = _f(np.asarray(w_br_lru)[0].reshape(KC, 128, 8, 128).transpose(2, 1, 0, 3))
    wbg = _f(np.asarray(w_br_gdn)[0].reshape(KC, 128, 8, 128).transpose(2, 1, 0, 3))
    wout = _f(np.asarray(w_out)[0].reshape(KC, 128, D).transpose(1, 0, 2))
    cols = lambda v, n: np.asarray(v).reshape(n, 128).T
    vecs = _f(np.concatenate([
        cols(norm_pre[0], 8),
        np.asarray(lru_conv_w)[0].reshape(4, 8, 128).transpose(2, 0, 1).reshape(128, 32),
        cols(lru_conv_b[0], 8), cols(lru_ba[0], 8), cols(lru_bx[0], 8), cols(lru_a_logit[0], 8),
        np.asarray(gdn_conv_w)[0].reshape(4, 24, 128).transpose(2, 0, 1).reshape(128, 96),
        np.asarray(gdn_norm_w)[0].reshape(128, 1)], axis=1))
    npost = _f(np.asarray(norm_post).reshape(1, D))
    alog = _f(np.asarray(gdn_A_log).reshape(1, 8))
    dtb = _f(np.asarray(gdn_dt_bias).reshape(1, 8))
    in_maps = []
    for c in range(NCORES):
        sl = slice(16 * c, 16 * c + 16)
        in_maps.append(dict(
            xp=_f(x_prompt[c]), xs=_f(np.asarray(x_sample)[sl].reshape(128, D)),
            slc=_f(np.asarray(state_lru_conv)[0, sl].reshape(48, D)), slh=_f(np.asarray(state_lru_h)[0, sl]),
            sgc=_f(np.asarray(state_gdn_conv)[0, sl].reshape(48, 3072)), sgS=_f(np.asarray(state_gdn_S)[0, sl]),
            vecs=vecs, npost=npost, alog=alog, dtb=dtb, winr=winr, wba=wba, wa=wa, wx=wx, wbl=wbl, wbg=wbg, wout=wout))
    nc = _program()
    res = run_bass_kernel_spmd(nc, in_maps, core_ids=list(range(NCORES)))
    R = res.results
    cat = lambda name: np.stack([np.asarray(R[c][name]) for c in range(NCORES)], axis=0)
    yp = cat("yp")
    ys = cat("ys").reshape(128, 8, D)
    p_lc = cat("o_plc")[None]
    p_lh = cat("o_plh").reshape(1, 8, D)
    p_gc = cat("o_pgc")[None]
    p_gs = cat("o_pgs")[None]
    s_lc = cat("o_slc").reshape(1, 128, 3, D)
    s_lh = cat("o_slh").reshape(1, 128, D)
    s_gc = cat("o_sgc").reshape(1, 128, 3, 3072)
    s_gs = cat("o_sgs").reshape(1, 128, 8, 128, 128)
    return tuple(np.ascontiguousarray(a, dtype=np.float32) for a in (yp, ys, p_lc, p_lh, p_gc, p_gs, s_lc, s_lh, s_gc, s_gs))
```
